# Optimizing a Trainium2 kernel written in Bass

```python
import math
import jax, jax.numpy as jnp
from jax import lax
import numpy as np

D_MODEL = 1024
BATCH = 16
SEQ = 2048
DEPTH = 2

N_A_LAYERS = DEPTH // 2
N_B_LAYERS = DEPTH - N_A_LAYERS
CONV_WIDTH = D_MODEL
CONV_KERNEL = 31
N_HEADS = D_MODEL // 128
HEAD_DIM = 64
V_DIM = 2 * HEAD_DIM
ATTN_WIDTH = N_HEADS * V_DIM
Q_BLOCK = 128
EPS = 1e-6

kernel_name = "yoco_conformer_diffattn_hybrid"


def _rms_norm(x, g):
    xf = x.astype(jnp.float32)
    y = xf * lax.rsqrt(jnp.mean(xf * xf, axis=-1, keepdims=True) + EPS)
    return (y * g.astype(jnp.float32)).astype(x.dtype)


def _layer_norm(x, g, b):
    xf = x.astype(jnp.float32)
    mu = jnp.mean(xf, axis=-1, keepdims=True)
    var = jnp.mean(jnp.square(xf - mu), axis=-1, keepdims=True)
    y = (xf - mu) * lax.rsqrt(var + EPS)
    return (y * g.astype(jnp.float32) + b.astype(jnp.float32)).astype(x.dtype)


def _alibi_slopes():
    i = jnp.arange(1, N_HEADS + 1, dtype=jnp.float32)
    return jnp.exp2(-8.0 * i / N_HEADS)


def _causal_depthwise_conv(u, w, b):
    k = w[:, None, :].astype(u.dtype)
    y = lax.conv_general_dilated(u, k, window_strides=(1,), padding=[(CONV_KERNEL - 1, 0)],
                                 dimension_numbers=("NWC", "WIO", "NWC"),
                                 feature_group_count=u.shape[-1])
    return y + b.astype(u.dtype)


def _conformer_conv_layer(x, g_pre, w_in, w_dw, b_dw, ln_g, ln_b, w_out, g_post):
    h = _rms_norm(x, g_pre)
    u = h @ w_in
    a, b, z = jnp.split(u, 3, axis=-1)
    c = a * jax.nn.sigmoid(b)
    c = _causal_depthwise_conv(c, w_dw, b_dw)
    c = jax.nn.silu(_layer_norm(c, ln_g, ln_b))
    y = (c * jax.nn.silu(z)) @ w_out
    return x + _rms_norm(y, g_post)


def _diff_attention(q, k, v, lam):
    bsz, seq = q.shape[0], q.shape[1]
    nblk = seq // Q_BLOCK
    qb = q.reshape(bsz, nblk, Q_BLOCK, N_HEADS, 2, HEAD_DIM).transpose(1, 0, 2, 3, 4, 5)
    slopes = _alibi_slopes()
    kpos = jnp.arange(seq)
    scale = HEAD_DIM ** -0.5

    def one_block(args):
        qi, blk = args
        qpos = blk * Q_BLOCK + jnp.arange(Q_BLOCK)
        dist = (qpos[:, None] - kpos[None, :]).astype(jnp.float32)
        bias = jnp.where(dist[None] >= 0, -slopes[:, None, None] * dist[None], -jnp.inf)
        s = jnp.einsum("bqhcd,bkhcd->bhcqk", qi, k, preferred_element_type=jnp.float32)
        p = jax.nn.softmax(s * scale + bias[None, :, None], axis=-1)
        attn = p[:, :, 0] - lam * p[:, :, 1]
        return jnp.einsum("bhqk,bkhe->bqhe", attn.astype(v.dtype), v)

    out = lax.map(one_block, (qb, jnp.arange(nblk)))
    return out.transpose(1, 0, 2, 3, 4).reshape(bsz, seq, N_HEADS, V_DIM)


def _diff_attn_layer(x, k, v, layer_idx, g_pre, w_in, lam_p, g_sub, w_out, g_post):
    bsz, seq = x.shape[0], x.shape[1]
    h = _rms_norm(x, g_pre)
    u = h @ w_in
    q = u[..., :2 * N_HEADS * HEAD_DIM].reshape(bsz, seq, N_HEADS, 2, HEAD_DIM)
    z = u[..., 2 * N_HEADS * HEAD_DIM:]
    lam_init = 0.8 - 0.6 * math.exp(-0.3 * layer_idx)
    lp = lam_p.astype(jnp.float32)
    lam = jnp.exp(jnp.sum(lp[0] * lp[1])) - jnp.exp(jnp.sum(lp[2] * lp[3])) + lam_init
    o = _diff_attention(q, k, v, lam)
    o = _rms_norm(o, g_sub) * (1.0 - lam_init)
    o = o.reshape(bsz, seq, ATTN_WIDTH).astype(x.dtype)
    y = (o * jax.nn.silu(z)) @ w_out
    return x + _rms_norm(y, g_post)


def setup_inputs(seed: int = 0) -> dict:
    key = jax.random.key(seed)
    ks = jax.random.split(key, 20)
    D, E, K = D_MODEL, CONV_WIDTH, CONV_KERNEL
    nA, nB = N_A_LAYERS, N_B_LAYERS
    nrm = lambda k, shape, fan: jax.random.normal(k, shape, jnp.float32) * fan ** -0.5
    gain = lambda k, shape: 1.0 + 0.02 * jax.random.normal(k, shape, jnp.float32)
    kv_cols = 2 * N_HEADS * HEAD_DIM + ATTN_WIDTH
    q_cols = 2 * N_HEADS * HEAD_DIM + ATTN_WIDTH
    return {
        "x": jax.random.normal(ks[0], (BATCH, SEQ, D), jnp.float32),
        "a_g_pre": gain(ks[1], (nA, D)),
        "a_w_in": nrm(ks[2], (nA, D, 3 * E), D),
        "a_w_dw": nrm(ks[3], (nA, K, E), K),
        "a_b_dw": 0.02 * jax.random.normal(ks[4], (nA, E), jnp.float32),
        "a_ln_g": gain(ks[5], (nA, E)),
        "a_ln_b": 0.02 * jax.random.normal(ks[6], (nA, E), jnp.float32),
        "a_w_out": nrm(ks[7], (nA, E, D), E),
        "a_g_post": gain(ks[8], (nA, D)),
        "kv_g": gain(ks[9], (D,)),
        "w_kv": nrm(ks[10], (D, kv_cols), D),
        "b_g_pre": gain(ks[11], (nB, D)),
        "b_w_in": nrm(ks[12], (nB, D, q_cols), D),
        "b_lambda": 0.1 * jax.random.normal(ks[13], (nB, 4, HEAD_DIM), jnp.float32),
        "b_g_sub": gain(ks[14], (nB, V_DIM)),
        "b_w_out": nrm(ks[15], (nB, ATTN_WIDTH, D), ATTN_WIDTH),
        "b_g_post": gain(ks[16], (nB, D)),
    }


def reference(x, a_g_pre, a_w_in, a_w_dw, a_b_dw, a_ln_g, a_ln_b, a_w_out, a_g_post,
              kv_g, w_kv, b_g_pre, b_w_in, b_lambda, b_g_sub, b_w_out, b_g_post):
    bsz, seq = x.shape[0], x.shape[1]
    k = v = None
    for l in range(DEPTH):
        if l < N_A_LAYERS:
            x = _conformer_conv_layer(x, a_g_pre[l], a_w_in[l], a_w_dw[l], a_b_dw[l],
                                      a_ln_g[l], a_ln_b[l], a_w_out[l], a_g_post[l])
        else:
            if l == N_A_LAYERS:
                kvh = _rms_norm(x, kv_g) @ w_kv
                k = kvh[..., :2 * N_HEADS * HEAD_DIM].reshape(bsz, seq, N_HEADS, 2, HEAD_DIM)
                v = kvh[..., 2 * N_HEADS * HEAD_DIM:].reshape(bsz, seq, N_HEADS, V_DIM)
            j = l - N_A_LAYERS
            x = _diff_attn_layer(x, k, v, l + 1, b_g_pre[j], b_w_in[j], b_lambda[j],
                                 b_g_sub[j], b_w_out[j], b_g_post[j])
    return x
```

```python
import math
import numpy as np
from contextlib import ExitStack
import concourse.bass as bass
import concourse.mybir as mybir
from concourse.bass_utils import run_bass_kernel_spmd

F32 = mybir.dt.float32
BF16 = mybir.dt.bfloat16
AF = mybir.ActivationFunctionType
ALU = mybir.AluOpType

D = 1024
KC = 8
S = 2048
NSEQ = 2
NTOK = NSEQ * S
EPS = 1e-6
NH = 8
KW = 31
LAM_INIT = 0.8 - 0.6 * math.exp(-0.3 * 2)
GA = 256
NCOLV = 48 + 8 * KW + 1
NCST = 128 + NH * 17

ENGS = ("pe", "act", "dve", "pool", "sp")


class Dep:
    __slots__ = ("w", "r", "psum")

    def __init__(self, psum=False):
        self.w = None
        self.r = []
        self.psum = psum


class Tracker:
    def __init__(self, nc, stack):
        self.nc = nc
        self.stack = stack
        self.ops = {e: [] for e in ENGS}
        self.sems = {}
        self.cnt = {}
        self.seen = {e: {} for e in ENGS}
        for e in ENGS:
            self._sem("S_" + e)
        self.n_dma = 0

    def _sem(self, name):
        self.sems[name] = self.stack.enter_context(self.nc.semaphore(name))
        self.cnt[name] = 0
        return name

    def dma_sem(self):
        self.n_dma += 1
        return self._sem("D%d" % self.n_dma)

    def _collect(self, eng, reads, writes):
        own = "S_" + eng
        deps = {}

        def add(tok):
            if tok is not None and deps.get(tok[0], 0) < tok[1]:
                deps[tok[0]] = tok[1]

        for b in reads:
            add(b.w)
            if b.psum:
                for t in b.r:
                    if t[0] != own:
                        add(t)
        for b in writes:
            if b.w is not None and b.w[0] != own:
                add(b.w)
            for t in b.r:
                if t[0] != own:
                    add(t)
        waits = []
        for s, v in deps.items():
            if self.seen[eng].get(s, 0) < v:
                self.seen[eng][s] = v
                waits.append((s, v))
        return waits

    @staticmethod
    def _update(tok, reads, writes):
        for b in reads:
            b.r = [t for t in b.r if t[0] != tok[0]]
            b.r.append(tok)
        for b in writes:
            b.w = tok
            b.r = []

    def op(self, eng, fn, reads=(), writes=(), inc=True):
        waits = self._collect(eng, reads, writes)
        own = "S_" + eng
        if inc:
            self.cnt[own] += 1
            tok = (own, self.cnt[own])
            self.ops[eng].append((waits, fn, own, 1))
        else:
            tok = (own, self.cnt[own] + 1)
            self.ops[eng].append((waits, fn, None, 0))
        self._update(tok, reads, writes)
        return tok

    def dma(self, queue, fn, sem, reads=(), writes=()):
        waits = self._collect(queue, reads, writes)
        self.cnt[sem] += 16
        tok = (sem, self.cnt[sem])
        self.ops[queue].append((waits, fn, sem, 16))
        self._update(tok, reads, writes)
        return tok

    def barrier(self):
        for e in ENGS:
            waits = []
            for s, v in self.cnt.items():
                if v > 0 and s != "S_" + e and self.seen[e].get(s, 0) < v:
                    self.seen[e][s] = v
                    waits.append((s, v))
            if waits:
                self.ops[e].append((waits, None, None, 0))

    def finish(self, toks):
        best = {}
        for s, v in toks:
            best[s] = max(best.get(s, 0), v)
        self.ops["sp"].append((list(best.items()), None, None, 0))

    def replay(self):
        nc, sems, ops = self.nc, self.sems, self.ops
        with nc.Block() as block:
            def run(name):
                def body(eng):
                    for waits, fn, isem, ival in ops[name]:
                        for s, v in waits:
                            eng.wait_ge(sems[s], v)
                        if fn is not None:
                            ins = fn(eng)
                            if isem is not None:
                                ins.then_inc(sems[isem], ival)
                return body

            block.tensor(run("pe"))
            block.scalar(run("act"))
            block.vector(run("dve"))
            block.gpsimd(run("pool"))
            block.sync(run("sp"))


class Ring:
    def __init__(self, T, n, apfn, dma=False, psum=False):
        self.n = n
        self.apfn = apfn
        self.deps = [Dep(psum) for _ in range(n)]
        self.sems = [T.dma_sem() for _ in range(n)] if dma else None
        self.i = -1

    def next(self):
        self.i = (self.i + 1) % self.n
        return self.i

    def ap(self, i):
        return self.apfn(i)


def emit_rstd(T, ss_ap, v_ap, r_ap, nh_ap, d_ss, d_v, d_r, d_nh, inv_n):
    T.op("dve", lambda e: e.tensor_scalar(out=v_ap, in0=ss_ap, scalar1=inv_n, scalar2=EPS,
                                          op0=ALU.mult, op1=ALU.add), reads=[d_ss], writes=[d_v])
    T.op("pool", lambda e: e.tensor_tensor(out=r_ap, in0=v_ap, in1=nh_ap, op=ALU.pow),
         reads=[d_v, d_nh], writes=[d_r])


def weight_chunk_emitters(T, w_dram, ncols, wbf, d_w, slots, scale_fn, colblk=1024, ctr=[0], dve_only=False):
    out = []
    for k in range(KC):
        for cb in range(ncols // colblk):
            def f(k=k, cb=cb):
                n = ctr[0]
                ctr[0] += 1
                st_ap, st_dep, st_sem = slots[n % len(slots)]
                src = w_dram[k * 128:(k + 1) * 128, cb * colblk:(cb + 1) * colblk]
                T.dma("sp", lambda e, o=st_ap, i=src: e.dma_start(out=o, in_=i), st_sem, writes=[st_dep])
                dst = wbf[:, k, cb * colblk:(cb + 1) * colblk]
                sc = scale_fn(k, cb)
                rd = [st_dep]
                if sc is not None:
                    sc_ap, sc_dep = sc
                    rd = rd + [sc_dep]
                if n % 2 == 0 and not dve_only:
                    if sc is None:
                        T.op("act", lambda e, o=dst, i=st_ap: e.activation(out=o, in_=i, func=AF.Copy),
                             reads=rd, writes=[d_w[0]])
                    else:
                        T.op("act", lambda e, o=dst, i=st_ap, s=sc_ap: e.activation(out=o, in_=i, func=AF.Copy, scale=s),
                             reads=rd, writes=[d_w[0]])
                else:
                    if sc is None:
                        T.op("dve", lambda e, o=dst, i=st_ap: e.tensor_copy(out=o, in_=i), reads=rd, writes=[d_w[1]])
                    else:
                        T.op("dve", lambda e, o=dst, i=st_ap, s=sc_ap: e.tensor_scalar(
                            out=o, in0=i, scalar1=s, scalar2=None, op0=ALU.mult), reads=rd, writes=[d_w[1]])
            out.append(f)
    return out


def load_cast_weight(T, w_dram, ncols, wbf, d_w, slots, scale_fn, colblk=1024):
    for f in weight_chunk_emitters(T, w_dram, ncols, wbf, d_w, slots, scale_fn, colblk):
        f()


def ring_slots(*rings):
    out = []
    for r in rings:
        for i in range(r.n):
            out.append((r.ap(i), r.deps[i], r.sems[i]))
    return out


def norm_stage_a1(T, x_ap, d_x, junk, d_junk, small, d_small, nhalf, d_nh):
    ss, v, r = small[:, 0:1], small[:, 1:2], small[:, 2:3]
    T.op("act", lambda e: e.activation(out=junk, in_=x_ap, func=AF.Square, accum_out=ss),
         reads=[d_x], writes=[d_junk, d_small])
    emit_rstd(T, ss, v, r, nhalf[:, 0:1], d_small, d_small, d_small, d_nh, 1.0 / D)


def norm_stage_a2(T, x_ap, d_x, small, d_small, hbf_ap, d_hbf):
    r = small[:, 2:3]
    T.op("act", lambda e: e.activation(out=hbf_ap, in_=x_ap, func=AF.Copy, scale=r),
         reads=[d_x, d_small], writes=[d_hbf])


def norm_stage_a(T, x_ap, d_x, junk, d_junk, small, d_small, nhalf, d_nh, hbf_ap, d_hbf):
    norm_stage_a1(T, x_ap, d_x, junk, d_junk, small, d_small, nhalf, d_nh)
    norm_stage_a2(T, x_ap, d_x, small, d_small, hbf_ap, d_hbf)


def norm_stage_b(T, hbf_ap, d_hbf, identb, d_id, pT, d_pT, hT_out_ap, d_hT):
    for k in range(KC):
        T.op("pe", lambda e, k=k: e.transpose(out=pT[:, k * 128:(k + 1) * 128],
                                              in_=hbf_ap[:, k * 128:(k + 1) * 128], identity=identb),
             reads=[d_hbf, d_id], writes=[d_pT], inc=(k == KC - 1))
    T.op("dve", lambda e: e.tensor_copy(out=hT_out_ap, in_=pT.rearrange("p (k n) -> p k n", k=KC)),
         reads=[d_pT], writes=[d_hT])


def norm_transpose(T, x_ap, d_x, junk, d_junk, small, d_small, nhalf, d_nh, hbf_ap, d_hbf,
                   identb, d_id, pT, d_pT, hT_out_ap, d_hT):
    norm_stage_a(T, x_ap, d_x, junk, d_junk, small, d_small, nhalf, d_nh, hbf_ap, d_hbf)
    norm_stage_b(T, hbf_ap, d_hbf, identb, d_id, pT, d_pT, hT_out_ap, d_hT)


def emit_y_half(T, half, pY, d_pY, w_bf, d_w, lhs_fn, lhs_deps, yo_ap, d_yo, junk, d_junk, small, d_small):
    ssy = small[:, 4:6]
    for k in range(KC):
        T.op("pe", lambda e, k=k: e.matmul(
            pY[:, 0:512], lhsT=lhs_fn(k), rhs=w_bf[:, k, half * 512:(half + 1) * 512],
            start=(k == 0), stop=(k == KC - 1)),
            reads=list(lhs_deps) + list(d_w), writes=[d_pY], inc=(k == KC - 1))
    T.op("act", lambda e: e.activation(out=yo_ap[:, half * 512:(half + 1) * 512], in_=pY[:, 0:512],
                                       func=AF.Copy), reads=[d_pY], writes=[d_yo])
    T.op("act", lambda e: e.activation(out=junk[:, 0:512], in_=pY[:, 0:512], func=AF.Square,
                                       accum_out=ssy[:, half:half + 1]),
         reads=[d_pY], writes=[d_junk, d_small])


def emit_post_a(T, small, d_small, nhalf, d_nh):
    ssy = small[:, 4:6]
    ss, v, r = small[:, 6:7], small[:, 7:8], small[:, 8:9]
    T.op("dve", lambda e: e.tensor_tensor(out=ss, in0=ssy[:, 0:1], in1=ssy[:, 1:2], op=ALU.add),
         reads=[d_small], writes=[d_small])
    emit_rstd(T, ss, v, r, nhalf[:, 0:1], d_small, d_small, d_small, d_nh, 1.0 / D)


def emit_post_b(T, yo_ap, d_yo, small, d_small, gpost, d_gp, xres_ap, d_xres, out_ap, d_out):
    r = small[:, 8:9]
    T.op("dve", lambda e: e.scalar_tensor_tensor(out=yo_ap, in0=yo_ap, scalar=r, in1=gpost,
                                                 op0=ALU.mult, op1=ALU.mult),
         reads=[d_yo, d_small, d_gp], writes=[d_yo])
    T.op("dve", lambda e: e.tensor_tensor(out=out_ap, in0=yo_ap, in1=xres_ap, op=ALU.add),
         reads=[d_yo, d_xres], writes=[d_out])


def emit_post(T, yo_ap, d_yo, small, d_small, nhalf, d_nh, gpost, d_gp, xres_ap, d_xres, out_ap, d_out):
    emit_post_a(T, small, d_small, nhalf, d_nh)
    emit_post_b(T, yo_ap, d_yo, small, d_small, gpost, d_gp, xres_ap, d_xres, out_ap, d_out)


def post_norm_residual(T, pY, d_pY, w_bf, d_w, lhs_fn, lhs_deps, yo_ap, d_yo, junk, d_junk,
                       small, d_small, nhalf, d_nh, gpost, d_gp, xres_ap, d_xres, out_ap, d_out):
    for half in range(2):
        emit_y_half(T, half, pY, d_pY, w_bf, d_w, lhs_fn, lhs_deps, yo_ap, d_yo, junk, d_junk, small, d_small)
    emit_post(T, yo_ap, d_yo, small, d_small, nhalf, d_nh, gpost, d_gp, xres_ap, d_xres, out_ap, d_out)


def phase_a(nc, T, st, dr, x1_deps):
    sb = lambda name, shape, dt: st.enter_context(nc.sbuf_tensor("sb_" + name, shape, dt))
    x_d, x1_d = dr["x"], dr["x1"]

    colv = sb("colv", [128, NCOLV], F32); d_colv = Dep()
    cst = sb("cst", [128, NCST], F32); d_cst = Dep()
    identb = sb("identb", [128, 128], BF16); d_id = Dep()
    onesb = sb("onesb", [128, 128], BF16); d_ones = Dep()
    nhalf = sb("nhalf", [128, GA], F32); d_nh = Dep()
    hv = sb("hv", [128, 24], F32); d_hv = Dep()
    gpost = sb("gpost", [128, D], F32); d_gp = Dep()
    wA = sb("wA", [128, KC, 3 * D], BF16); d_wA = (Dep(), Dep())
    wO = sb("wO", [128, KC, D], BF16); d_wO = (Dep(), Dep())
    diag = sb("diag", [128, KC * KW, 128], BF16); d_diag = [(Dep(), Dep()) for _ in range(KC)]
    xt = sb("xt", [128, 3, D], F32)
    xr = sb("xr", [128, 2, D], F32)
    junk = sb("junk", [128, D], BF16); d_junk = Dep()
    hbf = sb("hbf", [128, 2, D], BF16)
    hT2 = sb("hT", [128, 2, KC, GA], BF16); d_hT2 = [Dep(), Dep()]
    cbuf = sb("cbuf", [128, KC, 30 + GA], BF16); d_c = [Dep() for _ in range(KC)]
    th = sb("th", [128, 2, GA], F32)
    szt = sb("szt", [128, KC, GA], BF16); d_sz = [Dep() for _ in range(KC)]
    cvo = sb("cvo", [128, KC, GA], F32); d_cvo = [Dep() for _ in range(KC)]
    cbq = sb("cbq", [128, 2, 2, GA], BF16)
    stt = sb("stt", [128, 3, GA], F32); d_stt = Dep()
    lnt = sb("lnt", [128, 1, 3, GA], F32)
    gat = sb("gat", [128, KC, GA], BF16); d_gat = Dep()
    yo = sb("yo", [128, 2, D], F32)
    smalls = sb("smalls", [128, 4, 16], F32)

    xt_r = Ring(T, 3, lambda i: xt[:, i, :], dma=True)
    xr_r = Ring(T, 2, lambda i: xr[:, i, :], dma=True)
    hbf_r = Ring(T, 2, lambda i: hbf[:, i, :])
    th_r = Ring(T, 2, lambda i: th[:, i, :])
    cbq_r = Ring(T, 2, lambda i: cbq[:, i, :, :])
    ln_r = Ring(T, 1, lambda i: lnt[:, i, :, :])
    yo_r = Ring(T, 2, lambda i: yo[:, i, :], dma=True)
    stage_slots = ring_slots(yo_r, xr_r, xt_r)
    sm_r = Ring(T, 4, lambda i: smalls[:, i, :])

    PS = [(st.enter_context(nc.psum_tensor("psA%d" % i, [128, 512], F32)), Dep(psum=True)) for i in range(8)]
    (pTt, d_pT), (pAB0, d_AB0), (pAB1, d_AB1), (pZ, d_Z), (pCV0, d_CV0), (pCV1, d_CV1), (pST, d_ST), (pY, d_Y) = PS
    pT = pTt[:].bitcast(BF16)
    AB = [(pAB0, d_AB0), (pAB1, d_AB1)]
    CV = [(pCV0, d_CV0), (pCV1, d_CV1)]

    s_c1, s_c2, s_c3 = T.dma_sem(), T.dma_sem(), T.dma_sem()
    T.dma("sp", lambda e: e.dma_start(out=colv[:], in_=dr["colv"]), s_c1, writes=[d_colv])
    T.dma("sp", lambda e: e.dma_start(out=cst[:], in_=dr["cst"]), s_c2, writes=[d_cst])
    T.dma("sp", lambda e: e.dma_start(out=gpost[:], in_=dr["rowv"][0].partition_broadcast(128)), s_c3, writes=[d_gp])
    T.op("dve", lambda e: e.tensor_copy(out=identb[:], in_=cst[:, 0:128]), reads=[d_cst], writes=[d_id])
    T.op("dve", lambda e: e.memset(onesb[:], 1.0), writes=[d_ones])
    T.op("pool", lambda e: e.memset(nhalf[:], -0.5), writes=[d_nh])
    T.op("dve", lambda e: e.tensor_scalar(out=hv[:, 0:8], in0=colv[:, 0:8], scalar1=0.5, scalar2=None, op0=ALU.mult),
         reads=[d_colv], writes=[d_hv])
    T.op("dve", lambda e: e.tensor_scalar(out=hv[:, 8:24], in0=colv[:, 32:48], scalar1=0.5, scalar2=None, op0=ALU.mult),
         reads=[d_colv], writes=[d_hv])

    def scaleA(k, cb):
        if cb == 1:
            return (colv[:, k:k + 1], d_colv)
        return (hv[:, k:k + 1], d_hv)
    load_cast_weight(T, dr["a_w_in"], 3 * D, wA, d_wA, stage_slots, scaleA)
    dctr = [0]

    def build_diag(ec):
        for k in range(KW):
            col = 48 + ec * KW + k
            dst = diag[:, ec * KW + k, :]
            if dctr[0] % 2 == 0:
                T.op("dve", lambda e, o=dst, c=col: e.tensor_scalar(out=o, in0=cst[:, 0:128], scalar1=colv[:, c:c + 1],
                                                                   scalar2=None, op0=ALU.mult),
                     reads=[d_cst, d_colv], writes=[d_diag[ec][0]])
            else:
                T.op("act", lambda e, o=dst, c=col: e.activation(out=o, in_=cst[:, 0:128], func=AF.Copy,
                                                                 scale=colv[:, c:c + 1]),
                     reads=[d_cst, d_colv], writes=[d_diag[ec][1]])
            dctr[0] += 1
    late_wO = weight_chunk_emitters(T, dr["a_w_out"], D, wO, d_wO, ring_slots(yo_r, xr_r), lambda k, cb: None)

    NG = NTOK // GA
    TPG = GA // 128
    xslots = {}

    def fetch_x(tile):
        si = xt_r.next()
        xslots[tile] = si
        T.dma("sp", lambda e, o=xt_r.ap(si), i=x_d[tile * 128:(tile + 1) * 128, :]: e.dma_start(out=o, in_=i),
              xt_r.sems[si], writes=[xt_r.deps[si]])

    fh_state = {}

    def front_head_stage(g, stage, tl):
        tile = g * TPG + tl
        if stage == 0:
            si = xslots.pop(tile)
            smi = sm_r.next()
            fh_state[(g, tl)] = (si, smi)
            norm_stage_a1(T, xt_r.ap(si), xt_r.deps[si], junk[:], d_junk, sm_r.ap(smi), sm_r.deps[smi], nhalf, d_nh)
        elif stage == 1:
            si, smi = fh_state[(g, tl)]
            hi = hbf_r.next()
            fh_state[(g, tl)] = (si, smi, hi)
            norm_stage_a2(T, xt_r.ap(si), xt_r.deps[si], sm_r.ap(smi), sm_r.deps[smi], hbf_r.ap(hi), hbf_r.deps[hi])
            if tl == TPG - 1 and g + 1 < NG:
                for t2 in range(TPG):
                    fetch_x((g + 1) * TPG + t2)
        else:
            si, smi, hi = fh_state.pop((g, tl))
            norm_stage_b(T, hbf_r.ap(hi), hbf_r.deps[hi], identb[:], d_id, pT, d_pT,
                         hT2[:, g % 2, :, tl * 128:(tl + 1) * 128], d_hT2[g % 2])

    def front_head(g):
        for stage in range(3):
            for tl in range(TPG):
                front_head_stage(g, stage, tl)

    def front_halo(g):
        first_in_seq = (g % (S // GA) == 0)
        for ec in range(KC):
            if first_in_seq:
                T.op("pool", lambda e, ec=ec: e.memset(cbuf[:, ec, 0:30], 0.0), writes=[d_c[ec]])
            else:
                T.op("pool", lambda e, ec=ec: e.tensor_copy(out=cbuf[:, ec, 0:30], in_=cbuf[:, ec, GA:GA + 30]),
                     reads=[d_c[ec]], writes=[d_c[ec]])

    def emit_ab(ec, g):
        hT, d_hT = hT2[:, g % 2, :, :], d_hT2[g % 2]
        pab, d_ab = AB[ec % 2]
        for part, coff in ((0, 0), (1, D)):
            for k in range(KC):
                T.op("pe", lambda e, k=k, part=part, coff=coff, pab=pab: e.matmul(
                    pab[:, part * GA:(part + 1) * GA], lhsT=wA[:, k, coff + ec * 128: coff + (ec + 1) * 128],
                    rhs=hT[:, k, :], start=(k == 0), stop=(k == KC - 1)),
                    reads=[*d_wA, d_hT], writes=[d_ab], inc=(part == 1 and k == KC - 1))
        ti = th_r.next()
        T.op("act", lambda e, pab=pab, ti=ti: e.activation(out=th_r.ap(ti), in_=pab[:, GA:2 * GA], func=AF.Tanh, scale=0.5),
             reads=[d_ab], writes=[th_r.deps[ti]])
        T.op("dve", lambda e, pab=pab, ti=ti: e.scalar_tensor_tensor(
            out=cbuf[:, ec, 30:30 + GA], in0=th_r.ap(ti), scalar=1.0, in1=pab[:, 0:GA], op0=ALU.add, op1=ALU.mult),
            reads=[th_r.deps[ti], d_ab], writes=[d_c[ec]])

    def emit_conv(ec):
        pcv, d_cv = CV[ec % 2]
        for k in range(KW):
            T.op("pe", lambda e, k=k, pcv=pcv: e.matmul(
                pcv[:, 0:GA], lhsT=diag[:, ec * KW + k, :], rhs=cbuf[:, ec, k:k + GA],
                start=(k == 0), stop=(k == KW - 1)),
                reads=[*d_diag[ec], d_c[ec]], writes=[d_cv], inc=(k == KW - 1))
        qi = cbq_r.next()
        q_ap = cbq_r.ap(qi)
        bcol = colv[:, 24 + ec:25 + ec]
        T.op("act", lambda e, pcv=pcv: e.activation(out=cvo[:, ec, :], in_=pcv[:, 0:GA], func=AF.Identity, bias=bcol),
             reads=[d_cv, d_colv], writes=[d_cvo[ec]])
        T.op("act", lambda e, pcv=pcv, q_ap=q_ap: e.activation(out=q_ap[:, 1, :], in_=pcv[:, 0:GA], func=AF.Square, bias=bcol),
             reads=[d_cv, d_colv], writes=[cbq_r.deps[qi]])
        T.op("dve", lambda e, q_ap=q_ap: e.tensor_copy(out=q_ap[:, 0, :], in_=cvo[:, ec, :]),
             reads=[d_cvo[ec]], writes=[cbq_r.deps[qi]])
        return qi

    def emit_stats(ec, qi):
        q_ap = cbq_r.ap(qi)
        T.op("pe", lambda e: e.matmul(pST[:, 0:2 * GA], lhsT=onesb[:], rhs=q_ap.rearrange("p a n -> p (a n)"),
                                      start=(ec == 0), stop=(ec == KC - 1)),
             reads=[d_ones, cbq_r.deps[qi]], writes=[d_ST], inc=True)

    def emit_z(ec, g):
        hT, d_hT = hT2[:, g % 2, :, :], d_hT2[g % 2]
        for k in range(KC):
            T.op("pe", lambda e, k=k, ec=ec: e.matmul(
                pZ[:, 0:GA], lhsT=wA[:, k, 2 * D + ec * 128: 2 * D + (ec + 1) * 128], rhs=hT[:, k, :],
                start=(k == 0), stop=(k == KC - 1)), reads=[*d_wA, d_hT], writes=[d_Z], inc=(k == KC - 1))
        ti = th_r.next()
        T.op("act", lambda e, ti=ti: e.activation(out=th_r.ap(ti), in_=pZ[:, 0:GA], func=AF.Tanh),
             reads=[d_Z], writes=[th_r.deps[ti]])
        T.op("dve", lambda e, ti=ti, ec=ec: e.scalar_tensor_tensor(
            out=szt[:, ec, :], in0=th_r.ap(ti), scalar=1.0, in1=pZ[:, 0:GA], op0=ALU.add, op1=ALU.mult),
            reads=[th_r.deps[ti], d_Z], writes=[d_sz[ec]])

    mu, msq, rs = stt[:, 0, :], stt[:, 1, :], stt[:, 2, :]

    def back_stats():
        T.op("dve", lambda e: e.tensor_scalar(out=mu, in0=pST[:, 0:GA], scalar1=1.0 / D, scalar2=None, op0=ALU.mult),
             reads=[d_ST], writes=[d_stt])
        T.op("dve", lambda e: e.tensor_tensor(out=msq, in0=mu, in1=mu, op=ALU.mult), reads=[d_stt], writes=[d_stt])
        T.op("dve", lambda e: e.scalar_tensor_tensor(out=msq, in0=pST[:, GA:2 * GA], scalar=1.0 / D, in1=msq,
                                                     op0=ALU.mult, op1=ALU.subtract),
             reads=[d_ST, d_stt], writes=[d_stt])
        T.op("dve", lambda e: e.tensor_scalar(out=msq, in0=msq, scalar1=EPS, scalar2=None, op0=ALU.add),
             reads=[d_stt], writes=[d_stt])
        T.op("act", lambda e: e.activation(out=msq, in_=msq, func=AF.Sqrt), reads=[d_stt], writes=[d_stt])
        T.op("dve", lambda e: e.reciprocal(out=rs, in_=msq), reads=[d_stt], writes=[d_stt])

    def back_ln(ec):
        li = ln_r.next()
        l_ap = ln_r.ap(li)
        d_l = ln_r.deps[li]
        d2, tt, up = l_ap[:, 0, :], l_ap[:, 1, :], l_ap[:, 2, :]
        T.op("pool", lambda e, ec=ec, d2=d2: e.tensor_tensor(out=d2, in0=cvo[:, ec, :], in1=mu, op=ALU.subtract),
             reads=[d_cvo[ec], d_stt], writes=[d_l])
        T.op("dve", lambda e, d2=d2: e.tensor_tensor(out=d2, in0=d2, in1=rs, op=ALU.mult),
             reads=[d_l, d_stt], writes=[d_l])
        T.op("act", lambda e, ec=ec, d2=d2, tt=tt: e.activation(out=tt, in_=d2, func=AF.Tanh, bias=hv[:, 16 + ec:17 + ec],
                                                                scale=hv[:, 8 + ec:9 + ec]),
             reads=[d_l, d_hv], writes=[d_l])
        T.op("dve", lambda e, ec=ec, d2=d2, up=up: e.tensor_scalar(out=up, in0=d2, scalar1=hv[:, 8 + ec:9 + ec],
                                                                   scalar2=hv[:, 16 + ec:17 + ec], op0=ALU.mult, op1=ALU.add),
             reads=[d_l, d_hv], writes=[d_l])
        T.op("dve", lambda e, tt=tt, up=up: e.scalar_tensor_tensor(out=up, in0=tt, scalar=1.0, in1=up, op0=ALU.add, op1=ALU.mult),
             reads=[d_l], writes=[d_l])
        T.op("dve", lambda e, ec=ec, up=up: e.tensor_tensor(out=gat[:, ec, :], in0=up, in1=szt[:, ec, :], op=ALU.mult),
             reads=[d_l, d_sz[ec]], writes=[d_gat])

    def back_out(g, xr_slots):
        for tl in range(TPG):
            tile = g * TPG + tl
            ri = xr_slots[tl]
            yi = yo_r.next()
            smi = sm_r.next()
            for half in range(2):
                pb, d_pb = (pY, d_Y) if half == 0 else (pZ, d_Z)
                emit_y_half(T, half, pb, d_pb, wO, d_wO, lambda k, tl=tl: gat[:, k, tl * 128:(tl + 1) * 128], [d_gat],
                            yo_r.ap(yi), yo_r.deps[yi], junk[:], d_junk, sm_r.ap(smi), sm_r.deps[smi])
            emit_post(T, yo_r.ap(yi), yo_r.deps[yi], sm_r.ap(smi), sm_r.deps[smi], nhalf, d_nh, gpost[:], d_gp,
                      xr_r.ap(ri), xr_r.deps[ri], yo_r.ap(yi), yo_r.deps[yi])
            T.dma("sp", lambda e, o=x1_d[tile * 128:(tile + 1) * 128, :], i=yo_r.ap(yi): e.dma_start(out=o, in_=i),
                  yo_r.sems[yi], reads=[yo_r.deps[yi]], writes=[x1_deps[tile]])

    def fetch_xr(g):
        slots = []
        for tl in range(TPG):
            tile = g * TPG + tl
            ri = xr_r.next()
            slots.append(ri)
            T.dma("sp", lambda e, o=xr_r.ap(ri), i=x_d[tile * 128:(tile + 1) * 128, :]: e.dma_start(out=o, in_=i),
                  xr_r.sems[ri], writes=[xr_r.deps[ri]])
        return slots

    for tl in range(TPG):
        fetch_x(tl)
    front_head(0)
    for gi in range(NG + 1):
        fg = gi if gi < NG else None
        bg = gi - 1 if gi >= 1 else None
        if fg == 0:
            front_halo(fg)
            emit_ab(0, fg)
        if bg is not None:
            xr_slots = fetch_xr(bg)
        pend = []
        for ec in range(KC):
            if fg is not None and ec + 1 < KC:
                emit_ab(ec + 1, fg)
            if bg is not None:
                back_ln(ec)
            if fg is not None:
                if gi == 0:
                    build_diag(ec)
                qi = emit_conv(ec)
                emit_z(ec, fg)
                if gi == 0 and late_wO:
                    late_wO.pop(0)()
                if pend:
                    emit_stats(*pend.pop(0))
                pend.append((ec, qi))
                if fg + 1 < NG:
                    sched = {1: [(0, 0), (0, 1)], 3: [(1, 0)], 4: [(1, 1)], 5: [(2, 0)], 6: [(2, 1)]}
                    for stage, tl in sched.get(ec, []):
                        front_head_stage(fg + 1, stage, tl)
        if fg is not None:
            emit_stats(*pend.pop(0))
            back_stats()
            if fg + 1 < NG:
                front_halo(fg + 1)
                emit_ab(0, fg + 1)
        if bg is not None:
            back_out(bg, xr_slots)


def phase_b(nc, T, st, dr, x1_deps, out_deps):
    sb = lambda name, shape, dt: st.enter_context(nc.sbuf_tensor("sc_" + name, shape, dt))
    x1_d, out_d = dr["x1"], dr["out"]
    QW = 256
    NJ = S // QW
    NT = S // 128

    colv = sb("colv", [128, NCOLV], F32); d_colv = Dep()
    cst = sb("cst", [128, NCST], F32); d_cst = Dep()
    identb = sb("identb", [128, 128], BF16); d_id = Dep()
    nhalf = sb("nhalf", [128, 8], F32); d_nh = Dep()
    mneg = sb("mneg", [128, 2, 128], BF16); d_mneg = Dep()
    hv = sb("hv", [128, 8], F32); d_hv = Dep()
    gsc = sb("gsc", [128, 2], F32); d_gsc = Dep()
    gpost = sb("gpost", [128, D], F32); d_gp = Dep()
    lamt = sb("lamt", [128, 256], F32); d_lam = Dep()
    lams = sb("lams", [128, 8], F32); d_lams = Dep()
    KT = sb("KT", [128, NH, S], BF16); d_KT = [Dep() for _ in range(NH)]
    Va = sb("Va", [128, NT * NH, 130], BF16); d_V = Dep()
    wIN = sb("wIN", [128, KC, 2 * D], BF16); d_wIN = (Dep(), Dep())
    wOB = sb("wOB", [128, KC, D], BF16); d_wOB = (Dep(), Dep())
    R = sb("R", [128, 18432], BF16)
    xt = sb("xt", [128, 2, D], F32)
    xr = sb("xr", [128, 2, D], F32)
    junk = sb("junk", [128, D], BF16); d_junk = Dep()
    hbf = sb("hbf", [128, 1, D], BF16)
    hT2 = sb("hT", [128, 2, KC, QW], BF16); d_hT2 = [Dep(), Dep()]
    QT2 = sb("QT", [128, 2, NH, 2 * QW], BF16); d_QT2 = [Dep(), Dep()]
    hT, d_hT = hT2[:, 0, :, :], d_hT2[0]
    tz = sb("tz", [128, 1, 512], F32)
    smalls = sb("smalls", [128, 4, 16], F32)
    osm = sb("osm", [128, 4, 8], F32)

    wKV = R[:, 0:16384].rearrange("p (k n) -> p k n", k=KC); d_wKV = (Dep(), Dep())
    gz2 = R[:, 0:4096].rearrange("p (b t n) -> p b t n", b=2, t=2); d_gz2 = [Dep(), Dep()]
    yo = R[:, 4096:8192].bitcast(F32).rearrange("p (t n) -> p t n", t=2)
    gated2 = R[:, 8192:12288].rearrange("p (b t n) -> p b t n", b=2, t=2); d_gated2 = [[Dep(), Dep()], [Dep(), Dep()]]
    gT = R[:, 12288:14336].rearrange("p (s k n) -> p s k n", s=2, k=KC)
    et = R[:, 14336:16384].rearrange("p (s c n) -> p s c n", s=4, c=2)
    otmp = R[:, 16384:18432].bitcast(F32).rearrange("p (s a n) -> p s a n", s=2, a=4)

    xt_r = Ring(T, 2, lambda i: xt[:, i, :], dma=True)
    xr_r = Ring(T, 2, lambda i: xr[:, i, :], dma=True)
    hbf_r = Ring(T, 1, lambda i: hbf[:, i, :])
    tz_r = Ring(T, 1, lambda i: tz[:, i, :])
    sm_r = Ring(T, 4, lambda i: smalls[:, i, :])
    osm_r = Ring(T, 4, lambda i: osm[:, i, :])
    yo_r = Ring(T, 2, lambda i: yo[:, i, :], dma=True)
    gT_r = Ring(T, 2, lambda i: gT[:, i, :, :])
    et_r = Ring(T, 4, lambda i: et[:, i, :, :])
    ot_r = Ring(T, 2, lambda i: otmp[:, i, :, :])

    pS = [st.enter_context(nc.psum_tensor("psS%d" % i, [128, 2, QW], F32)) for i in range(3)]
    d_S = [Dep(psum=True), Dep(psum=True), Dep(psum=True)]
    pOs = [st.enter_context(nc.psum_tensor("psO%d" % i, [128, 512], F32)) for i in range(2)]
    d_Os = [Dep(psum=True) for _ in range(2)]
    pYs = [st.enter_context(nc.psum_tensor("psY%d" % i, [128, 512], F32)) for i in range(3)]
    d_Ys = [Dep(psum=True) for _ in range(3)]
    pT, d_pT = pYs[0][:].bitcast(BF16), d_Ys[0]
    yb_i = [0]
    o_ctr = [0]

    y_flush = [None]

    def next_y():
        yb_i[0] = (yb_i[0] + 1) % 3
        if y_flush[0] is not None and yb_i[0] in y_pending:
            y_flush[0]()
        return pYs[yb_i[0]], d_Ys[yb_i[0]]
    y_pending = set()
    pY, d_Y = pYs[0], d_Ys[0]
    GB = [(pS[0][:].rearrange("p c n -> p (c n)"), d_S[0]), (pS[1][:].rearrange("p c n -> p (c n)"), d_S[1]),
          (pS[2][:].rearrange("p c n -> p (c n)"), d_S[2]),
          (pOs[0][:], d_Os[0]), (pOs[1][:], d_Os[1]), (pYs[1][:], d_Ys[1]), (pYs[2][:], d_Ys[2])]
    gb_i = [0]

    def next_bank():
        gb_i[0] = (gb_i[0] + 1) % len(GB)
        return GB[gb_i[0]]

    sems = [T.dma_sem() for _ in range(5)]
    T.dma("sp", lambda e: e.dma_start(out=colv[:], in_=dr["colv"]), sems[0], writes=[d_colv])
    T.dma("sp", lambda e: e.dma_start(out=cst[:], in_=dr["cst"]), sems[1], writes=[d_cst])
    T.dma("sp", lambda e: e.dma_start(out=gpost[:], in_=dr["rowv"][1].partition_broadcast(128)), sems[2], writes=[d_gp])
    T.dma("sp", lambda e: e.dma_start(out=lamt[:], in_=dr["lamv"].partition_broadcast(128)), sems[4], writes=[d_lam])
    T.op("dve", lambda e: e.tensor_copy(out=identb[:], in_=cst[:, 0:128]), reads=[d_cst], writes=[d_id])
    T.op("pool", lambda e: e.memset(nhalf[:], -0.5), writes=[d_nh])
    T.op("pool", lambda e: e.memset(Va[:, :, 128:130], 1.0), writes=[d_V])
    T.op("pool", lambda e: e.memset(mneg[:], 0.0), writes=[d_mneg])
    T.op("pool", lambda e: e.affine_select(out=mneg[:], in_=mneg[:], pattern=[[0, 2], [1, 128]], compare_op=ALU.is_ge,
                                           fill=-30000.0, base=0, channel_multiplier=-1), reads=[d_mneg], writes=[d_mneg])
    T.op("dve", lambda e: e.tensor_scalar(out=hv[:, 0:8], in0=colv[:, 16:24], scalar1=0.5, scalar2=None, op0=ALU.mult),
         reads=[d_colv], writes=[d_hv])
    T.op("dve", lambda e: e.tensor_scalar(out=gsc[:, 0:1], in0=colv[:, NCOLV - 1:NCOLV], scalar1=1.0 - LAM_INIT, scalar2=None,
                                          op0=ALU.mult), reads=[d_colv], writes=[d_gsc])
    T.op("dve", lambda e: e.scalar_tensor_tensor(out=junk[:, 0:64], in0=lamt[:, 0:64], scalar=1.0, in1=lamt[:, 64:128],
                                                 op0=ALU.mult, op1=ALU.mult, accum_out=lams[:, 0:1]),
         reads=[d_lam], writes=[d_junk, d_lams])
    T.op("dve", lambda e: e.scalar_tensor_tensor(out=junk[:, 0:64], in0=lamt[:, 128:192], scalar=1.0, in1=lamt[:, 192:256],
                                                 op0=ALU.mult, op1=ALU.mult, accum_out=lams[:, 1:2]),
         reads=[d_lam], writes=[d_junk, d_lams])
    T.op("act", lambda e: e.activation(out=lams[:, 2:4], in_=lams[:, 0:2], func=AF.Exp), reads=[d_lams], writes=[d_lams])
    T.op("dve", lambda e: e.scalar_tensor_tensor(out=lams[:, 4:5], in0=lams[:, 3:4], scalar=-LAM_INIT, in1=lams[:, 2:3],
                                                 op0=ALU.add, op1=ALU.subtract), reads=[d_lams], writes=[d_lams])
    nlam = lams[:, 4:5]

    def scaleIN(k, cb):
        if cb == 0:
            return (colv[:, 16 + k:17 + k], d_colv)
        return (hv[:, k:k + 1], d_hv)
    stage_slots = ring_slots(xt_r, xr_r)
    qst = QT2[:].rearrange("p b h n -> p (b h n)").bitcast(F32).rearrange("p (s n) -> p s n", s=4)
    qst_slots = [(qst[:, i_, :], Dep(), T.dma_sem()) for i_ in range(4)]
    late_w = (weight_chunk_emitters(T, dr["b_w_in"], 2 * D, wIN, d_wIN, qst_slots, scaleIN, dve_only=True)
              + weight_chunk_emitters(T, dr["b_w_out"], D, wOB, d_wOB, qst_slots, lambda k, cb: (gsc[:, 0:1], d_gsc),
                                      dve_only=True))

    wkv_sems = [T.dma_sem() for _ in range(KC)]
    d_wkvbf = [Dep() for _ in range(KC)]
    xslots = {}

    def fetch_x1(gtile):
        si = xt_r.next()
        xslots[gtile] = si
        T.dma("sp", lambda e, o=xt_r.ap(si), i=x1_d[gtile * 128:(gtile + 1) * 128, :]: e.dma_start(out=o, in_=i),
              xt_r.sems[si], reads=[x1_deps[gtile]], writes=[xt_r.deps[si]])

    n_evac = [0]

    def evac_copy(out_ap, in_ap, rd, wr, scale=None):
        n_evac[0] += 1
        if False:
            if scale is None:
                T.op("act", lambda e: e.activation(out=out_ap, in_=in_ap, func=AF.Copy), reads=rd, writes=wr)
            else:
                T.op("act", lambda e: e.activation(out=out_ap, in_=in_ap, func=AF.Copy, scale=scale), reads=rd, writes=wr)
        else:
            if scale is None:
                T.op("dve", lambda e: e.tensor_copy(out=out_ap, in_=in_ap), reads=rd, writes=wr)
            else:
                T.op("dve", lambda e: e.tensor_scalar(out=out_ap, in0=in_ap, scalar1=scale, scalar2=None, op0=ALU.mult),
                     reads=rd, writes=wr)

    for s in range(NSEQ):
        tbase = s * NT
        T.barrier()
        if s == 0:
            load_cast_weight(T, dr["w_kv"], 2 * D, wKV, d_wKV, stage_slots, lambda k, cb: (colv[:, 8 + k:9 + k], d_colv))
            for k in range(KC):
                T.dma("sp", lambda e, k=k: e.dma_start(out=dr["wkvbf"][:, k, :], in_=wKV[:, k, :]), wkv_sems[k],
                      reads=[*d_wKV], writes=[d_wkvbf[k]])
        else:
            for k in range(KC):
                T.dma("sp", lambda e, k=k: e.dma_start(out=wKV[:, k, :], in_=dr["wkvbf"][:, k, :]), wkv_sems[k],
                      reads=[d_wkvbf[k]], writes=[d_wKV[k % 2]])
        kv_slots = ring_slots(xt_r, xr_r)
        kv_map = {}
        kv_ctr = [0]

        def kv_fetch(gtile):
            ap_, dep_, sem_ = kv_slots[kv_ctr[0] % len(kv_slots)]
            kv_ctr[0] += 1
            kv_map[gtile] = (ap_, dep_)
            T.dma("sp", lambda e, o=ap_, i=x1_d[gtile * 128:(gtile + 1) * 128, :]: e.dma_start(out=o, in_=i),
                  sem_, reads=[x1_deps[gtile]], writes=[dep_])

        kvn = {}

        def kv_norm_stage(g, stage, tl):
            gtile = tbase + 2 * g + tl
            if stage == 0:
                ap_, dep_ = kv_map.pop(gtile)
                smi = sm_r.next()
                kvn[(g, tl)] = (ap_, dep_, smi)
                norm_stage_a1(T, ap_, dep_, junk[:], d_junk, sm_r.ap(smi), sm_r.deps[smi], nhalf, d_nh)
            elif stage == 1:
                ap_, dep_, smi = kvn.pop((g, tl))
                hi = hbf_r.next()
                kvn[(g, tl)] = hi
                norm_stage_a2(T, ap_, dep_, sm_r.ap(smi), sm_r.deps[smi], hbf_r.ap(hi), hbf_r.deps[hi])
                if g + 2 < NJ:
                    kv_fetch(tbase + 2 * (g + 2) + tl)
            else:
                hi = kvn.pop((g, tl))
                norm_stage_b(T, hbf_r.ap(hi), hbf_r.deps[hi], identb[:], d_id, pT, d_pT,
                             hT2[:, g % 2, :, tl * 128:(tl + 1) * 128], d_hT2[g % 2])

        def kv_norm(g):
            for tl in range(2):
                kv_norm_stage(g, 0, tl)
            for tl in range(2):
                kv_norm_stage(g, 1, tl)
                kv_norm_stage(g, 2, tl)

        for t_ in range(4):
            kv_fetch(tbase + t_)
        kv_norm(0)
        for g in range(NJ):
            if g + 1 < NJ:
                kv_norm_stage(g + 1, 0, 0)
                kv_norm_stage(g + 1, 0, 1)
            hTg, d_hTg = hT2[:, g % 2, :, :], d_hT2[g % 2]
            ksched = {1: [(1, 0)], 4: [(2, 0)], 5: [(1, 1)], 9: [(2, 1)]}
            unit = 0
            for h in range(NH):
                if s == 0 and h in (1, 4, 7) and late_w:
                    late_w.pop(0)()
                pb, d_pb = next_bank()
                for k in range(KC):
                    T.op("pe", lambda e, k=k, h=h, pb=pb, hTg=hTg: e.matmul(
                        pb[:, 0:QW], lhsT=wKV[:, k, h * 128:(h + 1) * 128],
                        rhs=hTg[:, k, :], start=(k == 0), stop=(k == KC - 1)),
                        reads=[*d_wKV, d_hTg], writes=[d_pb], inc=(k == KC - 1))
                evac_copy(KT[:, h, g * QW:(g + 1) * QW], pb[:, 0:QW], [d_pb], [d_KT[h]])
                if g + 1 < NJ:
                    for st_, tl_ in ksched.get(unit, []):
                        kv_norm_stage(g + 1, st_, tl_)
                unit += 1
            for tl in range(2):
                t_in = 2 * g + tl
                for half in range(2):
                    pb, d_pb = next_bank()
                    for k in range(KC):
                        T.op("pe", lambda e, k=k, tl=tl, half=half, pb=pb, hTg=hTg: e.matmul(
                            pb[:, 0:512], lhsT=hTg[:, k, tl * 128:(tl + 1) * 128],
                            rhs=wKV[:, k, D + half * 512: D + (half + 1) * 512], start=(k == 0), stop=(k == KC - 1)),
                            reads=[*d_wKV, d_hTg], writes=[d_pb], inc=(k == KC - 1))
                    evac_copy(Va[:, t_in * NH + 4 * half: t_in * NH + 4 * half + 4, 0:128],
                              pb[:, 0:512].rearrange("p (h e) -> p h e", h=4), [d_pb], [d_V])
                    if g + 1 < NJ:
                        for st_, tl_ in ksched.get(unit, []):
                            kv_norm_stage(g + 1, st_, tl_)
                    unit += 1

        if s == 0:
            while late_w:
                late_w.pop(0)()
        T.barrier()
        if s == 0:
            for b_ in range(2):
                T.op("pool", lambda e, b_=b_: e.memset(QT2[:, b_, :, :], 0.0), writes=[d_QT2[b_]])

        pn_state = {}
        act_q = []
        cur_step = [None]

        def defer_act(fn, delay=2):
            if cur_step[0] is None:
                fn()
            else:
                act_q.append((cur_step[0] + delay, fn))

        def run_deferred(upto=None):
            while act_q and (upto is None or act_q[0][0] <= upto):
                act_q.pop(0)[1]()
            if not act_q:
                y_pending.clear()
        y_flush[0] = run_deferred

        def pro_norm_a1(j, tl):
            def f():
                gtile = tbase + 2 * j + tl
                si = xslots.pop(gtile)
                smi = sm_r.next()
                pn_state[(j, tl)] = (si, smi)
                norm_stage_a1(T, xt_r.ap(si), xt_r.deps[si], junk[:], d_junk, sm_r.ap(smi), sm_r.deps[smi], nhalf, d_nh)
            return f

        def pro_norm_a2(j, tl):
            def f():
                si, smi = pn_state[(j, tl)]
                hi = hbf_r.next()
                pn_state[(j, tl)] = hi
                norm_stage_a2(T, xt_r.ap(si), xt_r.deps[si], sm_r.ap(smi), sm_r.deps[smi], hbf_r.ap(hi), hbf_r.deps[hi])
                if tl == 1 and j + 1 < NJ:
                    fetch_x1(tbase + 2 * (j + 1))
                    fetch_x1(tbase + 2 * (j + 1) + 1)
            return f

        def pro_norm_b(j, tl):
            def f():
                hi = pn_state.pop((j, tl))
                pYb, d_Yb = next_y()
                norm_stage_b(T, hbf_r.ap(hi), hbf_r.deps[hi], identb[:], d_id, pYb[:].bitcast(BF16), d_Yb,
                             hT2[:, j % 2, :, tl * 128:(tl + 1) * 128], d_hT2[j % 2])
            return f

        def pro_q(j, h):
            def f():
                hTj, d_hTj = hT2[:, j % 2, :, :], d_hT2[j % 2]
                pY, d_Y = next_y()
                for k in range(KC):
                    T.op("pe", lambda e, k=k: e.matmul(pY[:, 0:QW], lhsT=wIN[:, k, h * 128:(h + 1) * 128],
                                                       rhs=hTj[:, k, :], start=(k == 0), stop=(k == KC - 1)),
                         reads=[*d_wIN, d_hTj], writes=[d_Y], inc=(k == KC - 1))
                for c in range(2):
                    T.op("dve", lambda e, c=c: e.tensor_scalar(
                        out=QT2[c * 64:(c + 1) * 64, j % 2, h, c * QW:(c + 1) * QW], in0=pY[c * 64:(c + 1) * 64, 0:QW],
                        scalar1=0.125, scalar2=None, op0=ALU.mult), reads=[d_Y], writes=[d_QT2[j % 2]])
            return f

        def pro_z(j, tl, half):
            def f():
                hTj, d_hTj = hT2[:, j % 2, :, :], d_hT2[j % 2]
                pY, d_Y = next_y()
                for k in range(KC):
                    T.op("pe", lambda e, k=k: e.matmul(
                        pY[:, 0:512], lhsT=hTj[:, k, tl * 128:(tl + 1) * 128],
                        rhs=wIN[:, k, D + half * 512: D + (half + 1) * 512], start=(k == 0), stop=(k == KC - 1)),
                        reads=[*d_wIN, d_hTj], writes=[d_Y], inc=(k == KC - 1))

                def post():
                    ti = tz_r.next()
                    gz_ap = gz2[:, j % 2, tl, half * 512:(half + 1) * 512]
                    T.op("act", lambda e: e.activation(out=tz_r.ap(ti), in_=pY[:, 0:512], func=AF.Tanh),
                         reads=[d_Y], writes=[tz_r.deps[ti]])
                    T.op("dve", lambda e: e.scalar_tensor_tensor(
                        out=gz_ap, in0=tz_r.ap(ti), scalar=1.0, in1=pY[:, 0:512], op0=ALU.add, op1=ALU.mult),
                        reads=[tz_r.deps[ti], d_Y], writes=[d_gz2[j % 2]])
                if cur_step[0] is not None:
                    y_pending.add(yb_i[0])
                defer_act(post)
            return f

        def prologue_items(j):
            return ([pro_norm_a1(j, 0), pro_norm_a1(j, 1), pro_norm_a2(j, 0), pro_norm_b(j, 0), pro_norm_a2(j, 1), pro_norm_b(j, 1)]
                    + [pro_q(j, h) for h in range(NH)]
                    + [pro_z(j, tl, half) for tl in range(2) for half in range(2)])

        def interleave(la, lb):
            out = []
            for k_ in range(max(len(la), len(lb))):
                if k_ < len(lb):
                    out.append(lb[k_])
                if k_ < len(la):
                    out.append(la[k_])
            return out

        ep_state = {}

        xr_state = {}

        def ep_reload(j):
            def f():
                for i in range(2):
                    gtile = tbase + 2 * j + i
                    ri = xr_r.next()
                    xr_state[(j, i)] = ri
                    T.dma("sp", lambda e, o=xr_r.ap(ri), i_=x1_d[gtile * 128:(gtile + 1) * 128, :]: e.dma_start(out=o, in_=i_),
                          xr_r.sems[ri], reads=[x1_deps[gtile]], writes=[xr_r.deps[ri]])
            return f

        def ep_tr(j, i):
            def f():
                gtile = tbase + 2 * j + i
                ri = xr_state.pop((j, i))
                gi = gT_r.next()
                g_ap, d_g = gT_r.ap(gi), gT_r.deps[gi]
                pYb, d_pT = next_y()
                pT = pYb[:].bitcast(BF16)
                for k in range(KC):
                    T.op("pe", lambda e, k=k: e.transpose(out=pT[:, k * 128:(k + 1) * 128],
                                                          in_=gated2[:, j % 2, i, k * 128:(k + 1) * 128], identity=identb[:]),
                         reads=[d_gated2[j % 2][i], d_id], writes=[d_pT], inc=(k == KC - 1))
                T.op("dve", lambda e: e.tensor_copy(out=g_ap, in_=pT.rearrange("p (k n) -> p k n", k=KC)),
                     reads=[d_pT], writes=[d_g])
                ep_state[(j, i)] = (ri, g_ap, d_g, yo_r.next(), sm_r.next())
            return f

        def ep_y(j, i, half):
            def f():
                ri, g_ap, d_g, yi, smi = ep_state[(j, i)]
                pY, d_Y = next_y()
                for k in range(KC):
                    T.op("pe", lambda e, k=k: e.matmul(
                        pY[:, 0:512], lhsT=g_ap[:, k, :], rhs=wOB[:, k, half * 512:(half + 1) * 512],
                        start=(k == 0), stop=(k == KC - 1)),
                        reads=[d_g, *d_wOB], writes=[d_Y], inc=(k == KC - 1))

                def post():
                    small = sm_r.ap(smi)
                    T.op("act", lambda e: e.activation(out=yo_r.ap(yi)[:, half * 512:(half + 1) * 512], in_=pY[:, 0:512],
                                                       func=AF.Copy), reads=[d_Y], writes=[yo_r.deps[yi]])
                    T.op("act", lambda e: e.activation(out=junk[:, 0:512], in_=pY[:, 0:512], func=AF.Square,
                                                       accum_out=small[:, 4 + half:5 + half]),
                         reads=[d_Y], writes=[d_junk, sm_r.deps[smi]])
                if cur_step[0] is not None:
                    y_pending.add(yb_i[0])
                defer_act(post)
            return f

        def ep_post_a(j, i):
            def f():
                ri, g_ap, d_g, yi, smi = ep_state[(j, i)]
                run_deferred()
                emit_post_a(T, sm_r.ap(smi), sm_r.deps[smi], nhalf, d_nh)
            return f

        def ep_post_b(j, i):
            def f():
                gtile = tbase + 2 * j + i
                ri, g_ap, d_g, yi, smi = ep_state.pop((j, i))
                emit_post_b(T, yo_r.ap(yi), yo_r.deps[yi], sm_r.ap(smi), sm_r.deps[smi], gpost[:], d_gp,
                            xr_r.ap(ri), xr_r.deps[ri], yo_r.ap(yi), yo_r.deps[yi])
                T.dma("sp", lambda e, o=out_d[gtile * 128:(gtile + 1) * 128, :], i_=yo_r.ap(yi): e.dma_start(out=o, in_=i_),
                      yo_r.sems[yi], reads=[yo_r.deps[yi]], writes=[out_deps[gtile]])
            return f

        def epilogue_items(j):
            out = [ep_reload(j)]
            for i in range(2):
                out += [ep_tr(j, i), ep_y(j, i, 0), ep_y(j, i, 1), ep_post_a(j, i), ep_post_b(j, i)]
            return out

        fetch_x1(tbase + 0)
        fetch_x1(tbase + 1)
        for it in prologue_items(0):
            it()

        for j in range(NJ):
            QT, d_QT = QT2[:, j % 2, :, :], d_QT2[j % 2]
            gz, d_gz = gz2[:, j % 2, :, :], d_gz2[j % 2]
            gated, d_gated = gated2[:, j % 2, :, :], d_gated2[j % 2]
            side = interleave(epilogue_items(j - 1) if j >= 1 else [],
                              prologue_items(j + 1) if j + 1 < NJ else [])
            nkb = 2 * j + 2
            steps = [(h, kb) for h in range(NH) for kb in range(nkb)]
            st_e = {}
            obank = {}
            for h_ in range(NH):
                obank[h_] = (0, 1)
                o_ctr[0] += 1

            def emit_S(n):
                h, kb = steps[n]
                pSb = pS[n % 3]
                diag_i = [i for i in range(2) if kb == 2 * j + i]
                if kb == 2 * j + 1:
                    T.op("pe", lambda e, kb=kb, pSb=pSb, h=h, QT=QT, last=(not diag_i): e.matmul(
                        pSb[:, :, 128:QW], lhsT=KT[:, h, kb * 128:(kb + 1) * 128],
                        rhs=QT[:, h, :].rearrange("p (c n) -> p c n", c=2)[:, :, 128:QW], start=True, stop=last),
                        reads=[d_KT[h], d_QT], writes=[d_S[n % 3]], inc=(not diag_i))
                else:
                    T.op("pe", lambda e, kb=kb, pSb=pSb, h=h, QT=QT, last=(not diag_i): e.matmul(
                        pSb[:].rearrange("p c n -> p (c n)"), lhsT=KT[:, h, kb * 128:(kb + 1) * 128],
                        rhs=QT[:, h, :], start=True, stop=last),
                        reads=[d_KT[h], d_QT], writes=[d_S[n % 3]], inc=(not diag_i))
                for i in diag_i:
                    T.op("pe", lambda e, i=i, pSb=pSb: e.matmul(
                        pSb[:, :, i * 128:(i + 1) * 128], lhsT=identb[:], rhs=mneg[:], start=False, stop=True),
                        reads=[d_id, d_mneg], writes=[d_S[n % 3]], inc=True)

            def emit_exp(n):
                h, kb = steps[n]
                c0 = 128 if kb == 2 * j + 1 else 0
                pSb = pS[n % 3]
                dS0 = dS1 = d_S[n % 3]
                ei = et_r.next()
                e_ap = et_r.ap(ei)
                d_e = et_r.deps[ei]
                st_e[n] = (e_ap, d_e)
                if h == 0:
                    for i in range(2):
                        if 128 * i < c0:
                            continue
                        col = 128 + h * 17 + (kb - 2 * j - i) + 15
                        T.op("act", lambda e, i=i, col=col, pSb=pSb, e_ap=e_ap: e.activation(
                            out=e_ap[:, :, i * 128:(i + 1) * 128], in_=pSb[:, :, i * 128:(i + 1) * 128], func=AF.Exp,
                            bias=cst[:, col:col + 1], scale=1.0), reads=[dS0, dS1, d_cst], writes=[d_e])
                else:
                    col = 128 + h * 17 + (kb - 2 * j) + 15
                    T.op("act", lambda e, col=col, c0=c0, pSb=pSb, e_ap=e_ap: e.activation(
                        out=e_ap[:, :, c0:QW], in_=pSb[:, :, c0:QW], func=AF.Exp,
                        bias=cst[:, col:col + 1], scale=1.0), reads=[dS0, dS1, d_cst], writes=[d_e])

            def emit_PV(n):
                h, kb = steps[n]
                e_ap, d_e = st_e.pop(n)
                pv = [(i, c) for i in range(2) if kb <= 2 * j + i for c in range(2)]
                for n_, (i, c) in enumerate(pv):
                    ob = obank[h][i]
                    T.op("pe", lambda e, i=i, c=c, kb=kb, h=h, e_ap=e_ap, ob=ob, last=(kb == 2 * j + i): e.matmul(
                        pOs[ob][:, c * 130:(c + 1) * 130], lhsT=e_ap[:, c, i * 128:(i + 1) * 128],
                        rhs=Va[:, kb * NH + h, 0:130], start=(kb == 0 and c == 0), stop=last,
                        skip_group_check=True),
                        reads=[d_e, d_V], writes=[d_Os[ob]], inc=(n_ == len(pv) - 1))

            fin = {}

            def emit_final_copy(h, i):
                oi = ot_r.next()
                o_ap = ot_r.ap(oi)
                d_o = ot_r.deps[oi]
                raw = o_ap[:, 0:3, :].rearrange("p a n -> p (a n)")
                qi = osm_r.next()
                q_ap = osm_r.ap(qi)
                d_q = osm_r.deps[qi]
                ob = obank[h][i]
                T.op("dve", lambda e, ob=ob, raw=raw: e.tensor_copy(out=raw[:, 0:260], in_=pOs[ob][:, 0:260]),
                     reads=[d_Os[ob]], writes=[d_o])
                fin[(h, i)] = (o_ap, d_o, raw, q_ap, d_q)

            fin2 = {}

            def emit_final_rest1(h, i):
                o_ap, d_o, raw, q_ap, d_q = fin.pop((h, i))
                oo, sq = o_ap[:, 3, :], o_ap[:, 0, :]
                T.op("dve", lambda e, raw=raw, q_ap=q_ap: e.reciprocal(
                    out=q_ap[:, 0:2], in_=raw[:, 0:260].rearrange("p (c n) -> p c n", c=2)[:, :, 128]),
                    reads=[d_o], writes=[d_q])
                T.op("dve", lambda e, q_ap=q_ap: e.tensor_tensor(out=q_ap[:, 5:6], in0=q_ap[:, 1:2], in1=nlam, op=ALU.mult),
                     reads=[d_q, d_lams], writes=[d_q])
                T.op("dve", lambda e, raw=raw, q_ap=q_ap: e.tensor_scalar(
                    out=raw[:, 0:128], in0=raw[:, 0:128], scalar1=q_ap[:, 0:1], scalar2=None, op0=ALU.mult),
                    reads=[d_o, d_q], writes=[d_o])
                T.op("dve", lambda e, raw=raw, q_ap=q_ap, oo=oo: e.scalar_tensor_tensor(
                    out=oo, in0=raw[:, 130:258], scalar=q_ap[:, 5:6], in1=raw[:, 0:128], op0=ALU.mult, op1=ALU.add),
                    reads=[d_o, d_q], writes=[d_o])
                T.op("dve", lambda e, oo=oo, sq=sq, q_ap=q_ap: e.scalar_tensor_tensor(
                    out=sq, in0=oo, scalar=1.0, in1=oo, op0=ALU.mult, op1=ALU.mult, accum_out=q_ap[:, 2:3]),
                    reads=[d_o], writes=[d_o, d_q])
                emit_rstd(T, q_ap[:, 2:3], q_ap[:, 3:4], q_ap[:, 4:5], nhalf[:, 0:1], d_q, d_q, d_q, d_nh, 1.0 / 128)
                fin2[(h, i)] = (oo, d_o, q_ap, d_q)

            def emit_final_rest2(h, i):
                oo, d_o, q_ap, d_q = fin2.pop((h, i))
                T.op("dve", lambda e, i=i, h=h, oo=oo, q_ap=q_ap, gated=gated, gz=gz: e.scalar_tensor_tensor(
                    out=gated[:, i, h * 128:(h + 1) * 128], in0=oo, scalar=q_ap[:, 4:5],
                    in1=gz[:, i, h * 128:(h + 1) * 128], op0=ALU.mult, op1=ALU.mult),
                    reads=[d_o, d_q, d_gz], writes=[d_gated[i]])

            nsteps = len(steps)
            nside = len(side)
            emit_S(0)
            if nsteps > 1:
                emit_S(1)
            done_side = 0

            def after_PV(n):
                h, kb = steps[n]
                for i in range(2):
                    if kb == 2 * j + i:
                        emit_final_copy(h, i)
                if kb == nkb - 1:
                    emit_final_rest1(h, 0)
                    emit_final_rest1(h, 1)
                    emit_final_rest2(h, 0)
                    emit_final_rest2(h, 1)

            for n in range(nsteps):
                cur_step[0] = n
                emit_exp(n)
                run_deferred(n)
                if n + 2 < nsteps:
                    emit_S(n + 2)
                if n >= 1:
                    emit_PV(n - 1)
                    after_PV(n - 1)
                want = min(nside, ((n + 1) * nside * 10) // (nsteps * 6))
                while done_side < want:
                    side[done_side]()
                    done_side += 1
            emit_PV(nsteps - 1)
            after_PV(nsteps - 1)
            cur_step[0] = None
            run_deferred()

        for it in epilogue_items(NJ - 1):
            it()

def build_program(mode="fused"):
    nc = bass.Bass("TRN2", target_bir_lowering=False)
    dr = {}
    if mode in ("A", "fused"):
        dr["x"] = nc.dram_tensor("x", [NTOK, D], F32, kind="ExternalInput").ap()
    for name, shape in (("a_w_in", [D, 3 * D]), ("a_w_out", [D, D]), ("w_kv", [D, 2 * D]),
                        ("b_w_in", [D, 2 * D]), ("b_w_out", [D, D]), ("colv", [128, NCOLV]),
                        ("rowv", [3, D]), ("lamv", [256]), ("cst", [128, NCST])):
        dr[name] = nc.dram_tensor(name, shape, F32, kind="ExternalInput").ap()
    if mode == "A":
        dr["x1"] = nc.dram_tensor("x1", [NTOK, D], F32, kind="ExternalOutput").ap()
    elif mode == "B":
        dr["x1"] = nc.dram_tensor("x1", [NTOK, D], F32, kind="ExternalInput").ap()
    else:
        dr["x1"] = nc.dram_tensor("x1", [NTOK, D], F32).ap()
    if mode in ("B", "fused"):
        dr["out"] = nc.dram_tensor("out", [NTOK, D], F32, kind="ExternalOutput").ap()
        dr["wkvbf"] = nc.dram_tensor("wkvbf", [128, KC, 2 * D], BF16).ap()
    with ExitStack() as st0:
        T = Tracker(nc, st0)
        x1_deps = [Dep() for _ in range(NTOK // 128)]
        out_deps = [Dep() for _ in range(NTOK // 128)]
        if mode in ("A", "fused"):
            with ExitStack() as st:
                phase_a(nc, T, st, dr, x1_deps)
        if mode == "fused":
            T.barrier()
        if mode in ("B", "fused"):
            with ExitStack() as st:
                phase_b(nc, T, st, dr, x1_deps, out_deps)
        final = [d.w for d in (x1_deps if mode == "A" else out_deps)]
        T.finish(final)
        T.replay()
    return nc


def host_pack(inputs):
    f = lambda a: np.ascontiguousarray(np.asarray(a, dtype=np.float32))
    pc = lambda v: f(v).reshape(KC, 128).T
    colv = np.zeros((128, NCOLV), np.float32)
    colv[:, 0:8] = pc(inputs["a_g_pre"][0])
    colv[:, 8:16] = pc(inputs["kv_g"])
    colv[:, 16:24] = pc(inputs["b_g_pre"][0])
    colv[:, 24:32] = pc(inputs["a_b_dw"][0])
    colv[:, 32:40] = pc(inputs["a_ln_g"][0])
    colv[:, 40:48] = pc(inputs["a_ln_b"][0])
    colv[:, 48:48 + KC * KW] = f(inputs["a_w_dw"][0]).reshape(KW, KC, 128).transpose(2, 1, 0).reshape(128, KC * KW)
    colv[:, 48 + KC * KW] = f(inputs["b_g_sub"][0])
    rowv = np.stack([f(inputs["a_g_post"][0]), f(inputs["b_g_post"][0]), np.tile(f(inputs["b_g_sub"][0]), NH)])
    lamv = f(inputs["b_lambda"][0]).reshape(256)
    cst = np.zeros((128, NCST), np.float32)
    cst[:, 0:128] = np.eye(128, dtype=np.float32)
    p = np.arange(128, dtype=np.float32)
    for h in range(NH):
        m = 2.0 ** (-(h + 1))
        for dl in range(-15, 2):
            cst[:, 128 + h * 17 + dl + 15] = m * (p + 128.0 * dl)
    shared = {"a_w_in": f(inputs["a_w_in"][0]), "a_w_out": f(inputs["a_w_out"][0]), "w_kv": f(inputs["w_kv"]),
              "b_w_in": f(inputs["b_w_in"][0]), "b_w_out": f(inputs["b_w_out"][0]),
              "colv": colv, "rowv": f(rowv), "lamv": lamv, "cst": cst}
    return shared


MODE = "fused"


def kernel(**inputs):
    x = np.asarray(inputs["x"], dtype=np.float32)
    shared = host_pack(inputs)
    xs = [np.ascontiguousarray(x[2 * c:2 * c + 2].reshape(NTOK, D)) for c in range(8)]
    cores = list(range(8))
    if MODE == "unfused":
        ncA = build_program("A")
        resA = run_bass_kernel_spmd(ncA, [dict(shared, x=xs[c]) for c in cores], core_ids=cores)
        ncB = build_program("B")
        resB = run_bass_kernel_spmd(ncB, [dict(shared, x1=resA.results[c]["x1"]) for c in cores], core_ids=cores)
        res = resB
    else:
        nc = build_program("fused")
        res = run_bass_kernel_spmd(nc, [dict(shared, x=xs[c]) for c in cores], core_ids=cores)
    out = np.concatenate([r["out"].reshape(NSEQ, S, D) for r in res.results], axis=0)
    return out.astype(np.float32)
```

```python
import math
import numpy as np
from contextlib import ExitStack
import concourse.bass as bass
import concourse.mybir as mybir
from concourse.bass_utils import run_bass_kernel_spmd

F32 = mybir.dt.float32
BF16 = mybir.dt.bfloat16
AF = mybir.ActivationFunctionType
ALU = mybir.AluOpType

D = 1024
KC = 8
S = 2048
NSEQ = 2
NTOK = NSEQ * S
EPS = 1e-6
NH = 8
KW = 31
LAM_INIT = 0.8 - 0.6 * math.exp(-0.3 * 2)
GA = 256
NCOLV = 48 + 8 * KW + 1
NCST = 128 + NH * 17

ENGS = ("pe", "act", "dve", "pool", "sp")


class Dep:
    __slots__ = ("w", "r", "psum")

    def __init__(self, psum=False):
        self.w = None
        self.r = []
        self.psum = psum


class Tracker:
    def __init__(self, nc, stack):
        self.nc = nc
        self.stack = stack
        self.ops = {e: [] for e in ENGS}
        self.sems = {}
        self.cnt = {}
        self.seen = {e: {} for e in ENGS}
        for e in ENGS:
            self._sem("S_" + e)
        self.n_dma = 0

    def _sem(self, name):
        self.sems[name] = self.stack.enter_context(self.nc.semaphore(name))
        self.cnt[name] = 0
        return name

    def dma_sem(self):
        self.n_dma += 1
        return self._sem("D%d" % self.n_dma)

    def _collect(self, eng, reads, writes):
        own = "S_" + eng
        deps = {}

        def add(tok):
            if tok is not None and deps.get(tok[0], 0) < tok[1]:
                deps[tok[0]] = tok[1]

        for b in reads:
            add(b.w)
            if b.psum:
                for t in b.r:
                    if t[0] != own:
                        add(t)
        for b in writes:
            if b.w is not None and b.w[0] != own:
                add(b.w)
            for t in b.r:
                if t[0] != own:
                    add(t)
        waits = []
        for s, v in deps.items():
            if self.seen[eng].get(s, 0) < v:
                self.seen[eng][s] = v
                waits.append((s, v))
        return waits

    @staticmethod
    def _update(tok, reads, writes):
        for b in reads:
            b.r = [t for t in b.r if t[0] != tok[0]]
            b.r.append(tok)
        for b in writes:
            b.w = tok
            b.r = []

    def op(self, eng, fn, reads=(), writes=(), inc=True):
        waits = self._collect(eng, reads, writes)
        own = "S_" + eng
        if inc:
            self.cnt[own] += 1
            tok = (own, self.cnt[own])
            self.ops[eng].append((waits, fn, own, 1))
        else:
            tok = (own, self.cnt[own] + 1)
            self.ops[eng].append((waits, fn, None, 0))
        self._update(tok, reads, writes)
        return tok

    def dma(self, queue, fn, sem, reads=(), writes=()):
        waits = self._collect(queue, reads, writes)
        self.cnt[sem] += 16
        tok = (sem, self.cnt[sem])
        self.ops[queue].append((waits, fn, sem, 16))
        self._update(tok, reads, writes)
        return tok

    def barrier(self):
        for e in ENGS:
            waits = []
            for s, v in self.cnt.items():
                if v > 0 and s != "S_" + e and self.seen[e].get(s, 0) < v:
                    self.seen[e][s] = v
                    waits.append((s, v))
            if waits:
                self.ops[e].append((waits, None, None, 0))

    def finish(self, toks):
        best = {}
        for s, v in toks:
            best[s] = max(best.get(s, 0), v)
        self.ops["sp"].append((list(best.items()), None, None, 0))

    def replay(self):
        nc, sems, ops = self.nc, self.sems, self.ops
        with nc.Block() as block:
            def run(name):
                def body(eng):
                    for waits, fn, isem, ival in ops[name]:
                        for s, v in waits:
                            eng.wait_ge(sems[s], v)
                        if fn is not None:
                            ins = fn(eng)
                            if isem is not None:
                                ins.then_inc(sems[isem], ival)
                return body

            block.tensor(run("pe"))
            block.scalar(run("act"))
            block.vector(run("dve"))
            block.gpsimd(run("pool"))
            block.sync(run("sp"))


class Ring:
    def __init__(self, T, n, apfn, dma=False, psum=False):
        self.n = n
        self.apfn = apfn
        self.deps = [Dep(psum) for _ in range(n)]
        self.sems = [T.dma_sem() for _ in range(n)] if dma else None
        self.i = -1

    def next(self):
        self.i = (self.i + 1) % self.n
        return self.i

    def ap(self, i):
        return self.apfn(i)


def emit_rstd(T, ss_ap, v_ap, r_ap, nh_ap, d_ss, d_v, d_r, d_nh, inv_n):
    T.op("dve", lambda e: e.tensor_scalar(out=v_ap, in0=ss_ap, scalar1=inv_n, scalar2=EPS,
                                          op0=ALU.mult, op1=ALU.add), reads=[d_ss], writes=[d_v])
    T.op("pool", lambda e: e.tensor_tensor(out=r_ap, in0=v_ap, in1=nh_ap, op=ALU.pow),
         reads=[d_v, d_nh], writes=[d_r])


def weight_chunk_emitters(T, w_dram, ncols, wbf, d_w, slots, scale_fn, colblk=1024, ctr=[0], dve_only=False):
    out = []
    for k in range(KC):
        for cb in range(ncols // colblk):
            def f(k=k, cb=cb):
                n = ctr[0]
                ctr[0] += 1
                st_ap, st_dep, st_sem = slots[n % len(slots)]
                src = w_dram[k * 128:(k + 1) * 128, cb * colblk:(cb + 1) * colblk]
                T.dma("sp", lambda e, o=st_ap, i=src: e.dma_start(out=o, in_=i), st_sem, writes=[st_dep])
                dst = wbf[:, k, cb * colblk:(cb + 1) * colblk]
                sc = scale_fn(k, cb)
                rd = [st_dep]
                if sc is not None:
                    sc_ap, sc_dep = sc
                    rd = rd + [sc_dep]
                if n % 2 == 0 and not dve_only:
                    if sc is None:
                        T.op("act", lambda e, o=dst, i=st_ap: e.activation(out=o, in_=i, func=AF.Copy),
                             reads=rd, writes=[d_w[0]])
                    else:
                        T.op("act", lambda e, o=dst, i=st_ap, s=sc_ap: e.activation(out=o, in_=i, func=AF.Copy, scale=s),
                             reads=rd, writes=[d_w[0]])
                else:
                    if sc is None:
                        T.op("dve", lambda e, o=dst, i=st_ap: e.tensor_copy(out=o, in_=i), reads=rd, writes=[d_w[1]])
                    else:
                        T.op("dve", lambda e, o=dst, i=st_ap, s=sc_ap: e.tensor_scalar(
                            out=o, in0=i, scalar1=s, scalar2=None, op0=ALU.mult), reads=rd, writes=[d_w[1]])
            out.append(f)
    return out


def load_cast_weight(T, w_dram, ncols, wbf, d_w, slots, scale_fn, colblk=1024):
    for f in weight_chunk_emitters(T, w_dram, ncols, wbf, d_w, slots, scale_fn, colblk):
        f()


def ring_slots(*rings):
    out = []
    for r in rings:
        for i in range(r.n):
            out.append((r.ap(i), r.deps[i], r.sems[i]))
    return out


def norm_stage_a1(T, x_ap, d_x, junk, d_junk, small, d_small, nhalf, d_nh):
    ss, v, r = small[:, 0:1], small[:, 1:2], small[:, 2:3]
    T.op("act", lambda e: e.activation(out=junk, in_=x_ap, func=AF.Square, accum_out=ss),
         reads=[d_x], writes=[d_junk, d_small])
    emit_rstd(T, ss, v, r, nhalf[:, 0:1], d_small, d_small, d_small, d_nh, 1.0 / D)


def norm_stage_a2(T, x_ap, d_x, small, d_small, hbf_ap, d_hbf):
    r = small[:, 2:3]
    T.op("act", lambda e: e.activation(out=hbf_ap, in_=x_ap, func=AF.Copy, scale=r),
         reads=[d_x, d_small], writes=[d_hbf])


def norm_stage_a(T, x_ap, d_x, junk, d_junk, small, d_small, nhalf, d_nh, hbf_ap, d_hbf):
    norm_stage_a1(T, x_ap, d_x, junk, d_junk, small, d_small, nhalf, d_nh)
    norm_stage_a2(T, x_ap, d_x, small, d_small, hbf_ap, d_hbf)


def norm_stage_b(T, hbf_ap, d_hbf, identb, d_id, pT, d_pT, hT_out_ap, d_hT):
    for k in range(KC):
        T.op("pe", lambda e, k=k: e.transpose(out=pT[:, k * 128:(k + 1) * 128],
                                              in_=hbf_ap[:, k * 128:(k + 1) * 128], identity=identb),
             reads=[d_hbf, d_id], writes=[d_pT], inc=(k == KC - 1))
    T.op("dve", lambda e: e.tensor_copy(out=hT_out_ap, in_=pT.rearrange("p (k n) -> p k n", k=KC)),
         reads=[d_pT], writes=[d_hT])


def norm_transpose(T, x_ap, d_x, junk, d_junk, small, d_small, nhalf, d_nh, hbf_ap, d_hbf,
                   identb, d_id, pT, d_pT, hT_out_ap, d_hT):
    norm_stage_a(T, x_ap, d_x, junk, d_junk, small, d_small, nhalf, d_nh, hbf_ap, d_hbf)
    norm_stage_b(T, hbf_ap, d_hbf, identb, d_id, pT, d_pT, hT_out_ap, d_hT)


def emit_y_half(T, half, pY, d_pY, w_bf, d_w, lhs_fn, lhs_deps, yo_ap, d_yo, junk, d_junk, small, d_small):
    ssy = small[:, 4:6]
    for k in range(KC):
        T.op("pe", lambda e, k=k: e.matmul(
            pY[:, 0:512], lhsT=lhs_fn(k), rhs=w_bf[:, k, half * 512:(half + 1) * 512],
            start=(k == 0), stop=(k == KC - 1)),
            reads=list(lhs_deps) + list(d_w), writes=[d_pY], inc=(k == KC - 1))
    T.op("act", lambda e: e.activation(out=yo_ap[:, half * 512:(half + 1) * 512], in_=pY[:, 0:512],
                                       func=AF.Copy), reads=[d_pY], writes=[d_yo])
    T.op("act", lambda e: e.activation(out=junk[:, 0:512], in_=pY[:, 0:512], func=AF.Square,
                                       accum_out=ssy[:, half:half + 1]),
         reads=[d_pY], writes=[d_junk, d_small])


def emit_post_a(T, small, d_small, nhalf, d_nh):
    ssy = small[:, 4:6]
    ss, v, r = small[:, 6:7], small[:, 7:8], small[:, 8:9]
    T.op("dve", lambda e: e.tensor_tensor(out=ss, in0=ssy[:, 0:1], in1=ssy[:, 1:2], op=ALU.add),
         reads=[d_small], writes=[d_small])
    emit_rstd(T, ss, v, r, nhalf[:, 0:1], d_small, d_small, d_small, d_nh, 1.0 / D)


def emit_post_b(T, yo_ap, d_yo, small, d_small, gpost, d_gp, xres_ap, d_xres, out_ap, d_out):
    r = small[:, 8:9]
    T.op("dve", lambda e: e.scalar_tensor_tensor(out=yo_ap, in0=yo_ap, scalar=r, in1=gpost,
                                                 op0=ALU.mult, op1=ALU.mult),
         reads=[d_yo, d_small, d_gp], writes=[d_yo])
    T.op("dve", lambda e: e.tensor_tensor(out=out_ap, in0=yo_ap, in1=xres_ap, op=ALU.add),
         reads=[d_yo, d_xres], writes=[d_out])


def emit_post(T, yo_ap, d_yo, small, d_small, nhalf, d_nh, gpost, d_gp, xres_ap, d_xres, out_ap, d_out):
    emit_post_a(T, small, d_small, nhalf, d_nh)
    emit_post_b(T, yo_ap, d_yo, small, d_small, gpost, d_gp, xres_ap, d_xres, out_ap, d_out)


def post_norm_residual(T, pY, d_pY, w_bf, d_w, lhs_fn, lhs_deps, yo_ap, d_yo, junk, d_junk,
                       small, d_small, nhalf, d_nh, gpost, d_gp, xres_ap, d_xres, out_ap, d_out):
    for half in range(2):
        emit_y_half(T, half, pY, d_pY, w_bf, d_w, lhs_fn, lhs_deps, yo_ap, d_yo, junk, d_junk, small, d_small)
    emit_post(T, yo_ap, d_yo, small, d_small, nhalf, d_nh, gpost, d_gp, xres_ap, d_xres, out_ap, d_out)


def phase_a(nc, T, st, dr, x1_deps):
    sb = lambda name, shape, dt: st.enter_context(nc.sbuf_tensor("sb_" + name, shape, dt))
    x_d, x1_d = dr["x"], dr["x1"]

    colv = sb("colv", [128, NCOLV], F32); d_colv = Dep()
    cst = sb("cst", [128, NCST], F32); d_cst = Dep()
    identb = sb("identb", [128, 128], BF16); d_id = Dep()
    onesb = sb("onesb", [128, 128], BF16); d_ones = Dep()
    nhalf = sb("nhalf", [128, GA], F32); d_nh = Dep()
    hv = sb("hv", [128, 24], F32); d_hv = Dep()
    gpost = sb("gpost", [128, D], F32); d_gp = Dep()
    wA = sb("wA", [128, KC, 3 * D], BF16); d_wA = (Dep(), Dep())
    wO = sb("wO", [128, KC, D], BF16); d_wO = (Dep(), Dep())
    diag = sb("diag", [128, KC * KW, 128], BF16); d_diag = [(Dep(), Dep()) for _ in range(KC)]
    xt = sb("xt", [128, 3, D], F32)
    xr = sb("xr", [128, 2, D], F32)
    junk = sb("junk", [128, D], BF16); d_junk = Dep()
    hbf = sb("hbf", [128, 2, D], BF16)
    hT2 = sb("hT", [128, 2, KC, GA], BF16); d_hT2 = [Dep(), Dep()]
    cbuf = sb("cbuf", [128, KC, 30 + GA], BF16); d_c = [Dep() for _ in range(KC)]
    th = sb("th", [128, 2, GA], F32)
    szt = sb("szt", [128, KC, GA], BF16); d_sz = [Dep() for _ in range(KC)]
    cvo = sb("cvo", [128, KC, GA], F32); d_cvo = [Dep() for _ in range(KC)]
    cbq = sb("cbq", [128, 2, 2, GA], BF16)
    stt = sb("stt", [128, 3, GA], F32); d_stt = Dep()
    lnt = sb("lnt", [128, 1, 3, GA], F32)
    gat = sb("gat", [128, KC, GA], BF16); d_gat = Dep()
    yo = sb("yo", [128, 2, D], F32)
    smalls = sb("smalls", [128, 4, 16], F32)

    xt_r = Ring(T, 3, lambda i: xt[:, i, :], dma=True)
    xr_r = Ring(T, 2, lambda i: xr[:, i, :], dma=True)
    hbf_r = Ring(T, 2, lambda i: hbf[:, i, :])
    th_r = Ring(T, 2, lambda i: th[:, i, :])
    cbq_r = Ring(T, 2, lambda i: cbq[:, i, :, :])
    ln_r = Ring(T, 1, lambda i: lnt[:, i, :, :])
    yo_r = Ring(T, 2, lambda i: yo[:, i, :], dma=True)
    stage_slots = ring_slots(yo_r, xr_r, xt_r)
    sm_r = Ring(T, 4, lambda i: smalls[:, i, :])

    PS = [(st.enter_context(nc.psum_tensor("psA%d" % i, [128, 512], F32)), Dep(psum=True)) for i in range(8)]
    (pTt, d_pT), (pAB0, d_AB0), (pAB1, d_AB1), (pZ, d_Z), (pCV0, d_CV0), (pCV1, d_CV1), (pST, d_ST), (pY, d_Y) = PS
    pT = pTt[:].bitcast(BF16)
    AB = [(pAB0, d_AB0), (pAB1, d_AB1)]
    CV = [(pCV0, d_CV0), (pCV1, d_CV1)]

    s_c1, s_c2, s_c3 = T.dma_sem(), T.dma_sem(), T.dma_sem()
    T.dma("sp", lambda e: e.dma_start(out=colv[:], in_=dr["colv"]), s_c1, writes=[d_colv])
    T.dma("sp", lambda e: e.dma_start(out=cst[:], in_=dr["cst"]), s_c2, writes=[d_cst])
    T.dma("sp", lambda e: e.dma_start(out=gpost[:], in_=dr["rowv"][0].partition_broadcast(128)), s_c3, writes=[d_gp])
    T.op("dve", lambda e: e.tensor_copy(out=identb[:], in_=cst[:, 0:128]), reads=[d_cst], writes=[d_id])
    T.op("dve", lambda e: e.memset(onesb[:], 1.0), writes=[d_ones])
    T.op("pool", lambda e: e.memset(nhalf[:], -0.5), writes=[d_nh])
    T.op("dve", lambda e: e.tensor_scalar(out=hv[:, 0:8], in0=colv[:, 0:8], scalar1=0.5, scalar2=None, op0=ALU.mult),
         reads=[d_colv], writes=[d_hv])
    T.op("dve", lambda e: e.tensor_scalar(out=hv[:, 8:24], in0=colv[:, 32:48], scalar1=0.5, scalar2=None, op0=ALU.mult),
         reads=[d_colv], writes=[d_hv])

    def scaleA(k, cb):
        if cb == 1:
            return (colv[:, k:k + 1], d_colv)
        return (hv[:, k:k + 1], d_hv)
    load_cast_weight(T, dr["a_w_in"], 3 * D, wA, d_wA, stage_slots, scaleA)
    dctr = [0]

    def build_diag(ec):
        for k in range(KW):
            col = 48 + ec * KW + k
            dst = diag[:, ec * KW + k, :]
            if dctr[0] % 2 == 0:
                T.op("dve", lambda e, o=dst, c=col: e.tensor_scalar(out=o, in0=cst[:, 0:128], scalar1=colv[:, c:c + 1],
                                                                   scalar2=None, op0=ALU.mult),
                     reads=[d_cst, d_colv], writes=[d_diag[ec][0]])
            else:
                T.op("act", lambda e, o=dst, c=col: e.activation(out=o, in_=cst[:, 0:128], func=AF.Copy,
                                                                 scale=colv[:, c:c + 1]),
                     reads=[d_cst, d_colv], writes=[d_diag[ec][1]])
            dctr[0] += 1
    late_wO = weight_chunk_emitters(T, dr["a_w_out"], D, wO, d_wO, ring_slots(yo_r, xr_r), lambda k, cb: None)

    NG = NTOK // GA
    TPG = GA // 128
    xslots = {}

    def fetch_x(tile):
        si = xt_r.next()
        xslots[tile] = si
        T.dma("sp", lambda e, o=xt_r.ap(si), i=x_d[tile * 128:(tile + 1) * 128, :]: e.dma_start(out=o, in_=i),
              xt_r.sems[si], writes=[xt_r.deps[si]])

    fh_state = {}

    def front_head_stage(g, stage, tl):
        tile = g * TPG + tl
        if stage == 0:
            si = xslots.pop(tile)
            smi = sm_r.next()
            fh_state[(g, tl)] = (si, smi)
            norm_stage_a1(T, xt_r.ap(si), xt_r.deps[si], junk[:], d_junk, sm_r.ap(smi), sm_r.deps[smi], nhalf, d_nh)
        elif stage == 1:
            si, smi = fh_state[(g, tl)]
            hi = hbf_r.next()
            fh_state[(g, tl)] = (si, smi, hi)
            norm_stage_a2(T, xt_r.ap(si), xt_r.deps[si], sm_r.ap(smi), sm_r.deps[smi], hbf_r.ap(hi), hbf_r.deps[hi])
            if tl == TPG - 1 and g + 1 < NG:
                for t2 in range(TPG):
                    fetch_x((g + 1) * TPG + t2)
        else:
            si, smi, hi = fh_state.pop((g, tl))
            norm_stage_b(T, hbf_r.ap(hi), hbf_r.deps[hi], identb[:], d_id, pT, d_pT,
                         hT2[:, g % 2, :, tl * 128:(tl + 1) * 128], d_hT2[g % 2])

    def front_head(g):
        for stage in range(3):
            for tl in range(TPG):
                front_head_stage(g, stage, tl)

    def front_halo(g):
        first_in_seq = (g % (S // GA) == 0)
        for ec in range(KC):
            if first_in_seq:
                T.op("pool", lambda e, ec=ec: e.memset(cbuf[:, ec, 0:30], 0.0), writes=[d_c[ec]])
            else:
                T.op("pool", lambda e, ec=ec: e.tensor_copy(out=cbuf[:, ec, 0:30], in_=cbuf[:, ec, GA:GA + 30]),
                     reads=[d_c[ec]], writes=[d_c[ec]])

    def emit_ab(ec, g):
        hT, d_hT = hT2[:, g % 2, :, :], d_hT2[g % 2]
        pab, d_ab = AB[ec % 2]
        for part, coff in ((0, 0), (1, D)):
            for k in range(KC):
                T.op("pe", lambda e, k=k, part=part, coff=coff, pab=pab: e.matmul(
                    pab[:, part * GA:(part + 1) * GA], lhsT=wA[:, k, coff + ec * 128: coff + (ec + 1) * 128],
                    rhs=hT[:, k, :], start=(k == 0), stop=(k == KC - 1)),
                    reads=[*d_wA, d_hT], writes=[d_ab], inc=(part == 1 and k == KC - 1))
        ti = th_r.next()
        T.op("act", lambda e, pab=pab, ti=ti: e.activation(out=th_r.ap(ti), in_=pab[:, GA:2 * GA], func=AF.Tanh, scale=0.5),
             reads=[d_ab], writes=[th_r.deps[ti]])
        T.op("dve", lambda e, pab=pab, ti=ti: e.scalar_tensor_tensor(
            out=cbuf[:, ec, 30:30 + GA], in0=th_r.ap(ti), scalar=1.0, in1=pab[:, 0:GA], op0=ALU.add, op1=ALU.mult),
            reads=[th_r.deps[ti], d_ab], writes=[d_c[ec]])

    def emit_conv(ec):
        pcv, d_cv = CV[ec % 2]
        for k in range(KW):
            T.op("pe", lambda e, k=k, pcv=pcv: e.matmul(
                pcv[:, 0:GA], lhsT=diag[:, ec * KW + k, :], rhs=cbuf[:, ec, k:k + GA],
                start=(k == 0), stop=(k == KW - 1)),
                reads=[*d_diag[ec], d_c[ec]], writes=[d_cv], inc=(k == KW - 1))
        qi = cbq_r.next()
        q_ap = cbq_r.ap(qi)
        bcol = colv[:, 24 + ec:25 + ec]
        T.op("act", lambda e, pcv=pcv: e.activation(out=cvo[:, ec, :], in_=pcv[:, 0:GA], func=AF.Identity, bias=bcol),
             reads=[d_cv, d_colv], writes=[d_cvo[ec]])
        T.op("act", lambda e, pcv=pcv, q_ap=q_ap: e.activation(out=q_ap[:, 1, :], in_=pcv[:, 0:GA], func=AF.Square, bias=bcol),
             reads=[d_cv, d_colv], writes=[cbq_r.deps[qi]])
        T.op("dve", lambda e, q_ap=q_ap: e.tensor_copy(out=q_ap[:, 0, :], in_=cvo[:, ec, :]),
             reads=[d_cvo[ec]], writes=[cbq_r.deps[qi]])
        return qi

    def emit_stats(ec, qi):
        q_ap = cbq_r.ap(qi)
        T.op("pe", lambda e: e.matmul(pST[:, 0:2 * GA], lhsT=onesb[:], rhs=q_ap.rearrange("p a n -> p (a n)"),
                                      start=(ec == 0), stop=(ec == KC - 1)),
             reads=[d_ones, cbq_r.deps[qi]], writes=[d_ST], inc=True)

    def emit_z(ec, g):
        hT, d_hT = hT2[:, g % 2, :, :], d_hT2[g % 2]
        for k in range(KC):
            T.op("pe", lambda e, k=k, ec=ec: e.matmul(
                pZ[:, 0:GA], lhsT=wA[:, k, 2 * D + ec * 128: 2 * D + (ec + 1) * 128], rhs=hT[:, k, :],
                start=(k == 0), stop=(k == KC - 1)), reads=[*d_wA, d_hT], writes=[d_Z], inc=(k == KC - 1))
        ti = th_r.next()
        T.op("act", lambda e, ti=ti: e.activation(out=th_r.ap(ti), in_=pZ[:, 0:GA], func=AF.Tanh),
             reads=[d_Z], writes=[th_r.deps[ti]])
        T.op("dve", lambda e, ti=ti, ec=ec: e.scalar_tensor_tensor(
            out=szt[:, ec, :], in0=th_r.ap(ti), scalar=1.0, in1=pZ[:, 0:GA], op0=ALU.add, op1=ALU.mult),
            reads=[th_r.deps[ti], d_Z], writes=[d_sz[ec]])

    mu, msq, rs = stt[:, 0, :], stt[:, 1, :], stt[:, 2, :]

    def back_stats():
        T.op("dve", lambda e: e.tensor_scalar(out=mu, in0=pST[:, 0:GA], scalar1=1.0 / D, scalar2=None, op0=ALU.mult),
             reads=[d_ST], writes=[d_stt])
        T.op("dve", lambda e: e.tensor_tensor(out=msq, in0=mu, in1=mu, op=ALU.mult), reads=[d_stt], writes=[d_stt])
        T.op("dve", lambda e: e.scalar_tensor_tensor(out=msq, in0=pST[:, GA:2 * GA], scalar=1.0 / D, in1=msq,
                                                     op0=ALU.mult, op1=ALU.subtract),
             reads=[d_ST, d_stt], writes=[d_stt])
        T.op("dve", lambda e: e.tensor_scalar(out=msq, in0=msq, scalar1=EPS, scalar2=None, op0=ALU.add),
             reads=[d_stt], writes=[d_stt])
        T.op("act", lambda e: e.activation(out=msq, in_=msq, func=AF.Sqrt), reads=[d_stt], writes=[d_stt])
        T.op("dve", lambda e: e.reciprocal(out=rs, in_=msq), reads=[d_stt], writes=[d_stt])

    def back_ln(ec):
        li = ln_r.next()
        l_ap = ln_r.ap(li)
        d_l = ln_r.deps[li]
        d2, tt, up = l_ap[:, 0, :], l_ap[:, 1, :], l_ap[:, 2, :]
        T.op("pool", lambda e, ec=ec, d2=d2: e.tensor_tensor(out=d2, in0=cvo[:, ec, :], in1=mu, op=ALU.subtract),
             reads=[d_cvo[ec], d_stt], writes=[d_l])
        T.op("dve", lambda e, d2=d2: e.tensor_tensor(out=d2, in0=d2, in1=rs, op=ALU.mult),
             reads=[d_l, d_stt], writes=[d_l])
        T.op("act", lambda e, ec=ec, d2=d2, tt=tt: e.activation(out=tt, in_=d2, func=AF.Tanh, bias=hv[:, 16 + ec:17 + ec],
                                                                scale=hv[:, 8 + ec:9 + ec]),
             reads=[d_l, d_hv], writes=[d_l])
        T.op("dve", lambda e, ec=ec, d2=d2, up=up: e.tensor_scalar(out=up, in0=d2, scalar1=hv[:, 8 + ec:9 + ec],
                                                                   scalar2=hv[:, 16 + ec:17 + ec], op0=ALU.mult, op1=ALU.add),
             reads=[d_l, d_hv], writes=[d_l])
        T.op("dve", lambda e, tt=tt, up=up: e.scalar_tensor_tensor(out=up, in0=tt, scalar=1.0, in1=up, op0=ALU.add, op1=ALU.mult),
             reads=[d_l], writes=[d_l])
        T.op("dve", lambda e, ec=ec, up=up: e.tensor_tensor(out=gat[:, ec, :], in0=up, in1=szt[:, ec, :], op=ALU.mult),
             reads=[d_l, d_sz[ec]], writes=[d_gat])

    def back_out(g, xr_slots):
        for tl in range(TPG):
            tile = g * TPG + tl
            ri = xr_slots[tl]
            yi = yo_r.next()
            smi = sm_r.next()
            for half in range(2):
                pb, d_pb = (pY, d_Y) if half == 0 else (pZ, d_Z)
                emit_y_half(T, half, pb, d_pb, wO, d_wO, lambda k, tl=tl: gat[:, k, tl * 128:(tl + 1) * 128], [d_gat],
                            yo_r.ap(yi), yo_r.deps[yi], junk[:], d_junk, sm_r.ap(smi), sm_r.deps[smi])
            emit_post(T, yo_r.ap(yi), yo_r.deps[yi], sm_r.ap(smi), sm_r.deps[smi], nhalf, d_nh, gpost[:], d_gp,
                      xr_r.ap(ri), xr_r.deps[ri], yo_r.ap(yi), yo_r.deps[yi])
            T.dma("sp", lambda e, o=x1_d[tile * 128:(tile + 1) * 128, :], i=yo_r.ap(yi): e.dma_start(out=o, in_=i),
                  yo_r.sems[yi], reads=[yo_r.deps[yi]], writes=[x1_deps[tile]])

    def fetch_xr(g):
        slots = []
        for tl in range(TPG):
            tile = g * TPG + tl
            ri = xr_r.next()
            slots.append(ri)
            T.dma("sp", lambda e, o=xr_r.ap(ri), i=x_d[tile * 128:(tile + 1) * 128, :]: e.dma_start(out=o, in_=i),
                  xr_r.sems[ri], writes=[xr_r.deps[ri]])
        return slots

    for tl in range(TPG):
        fetch_x(tl)
    front_head(0)
    for gi in range(NG + 1):
        fg = gi if gi < NG else None
        bg = gi - 1 if gi >= 1 else None
        if fg == 0:
            front_halo(fg)
            emit_ab(0, fg)
        if bg is not None:
            xr_slots = fetch_xr(bg)
        pend = []
        for ec in range(KC):
            if fg is not None and ec + 1 < KC:
                emit_ab(ec + 1, fg)
            if bg is not None:
                back_ln(ec)
            if fg is not None:
                if gi == 0:
                    build_diag(ec)
                qi = emit_conv(ec)
                emit_z(ec, fg)
                if gi == 0 and late_wO:
                    late_wO.pop(0)()
                if pend:
                    emit_stats(*pend.pop(0))
                pend.append((ec, qi))
                if fg + 1 < NG:
                    sched = {1: [(0, 0), (0, 1)], 3: [(1, 0)], 4: [(1, 1)], 5: [(2, 0)], 6: [(2, 1)]}
                    for stage, tl in sched.get(ec, []):
                        front_head_stage(fg + 1, stage, tl)
        if fg is not None:
            emit_stats(*pend.pop(0))
            back_stats()
            if fg + 1 < NG:
                front_halo(fg + 1)
                emit_ab(0, fg + 1)
        if bg is not None:
            back_out(bg, xr_slots)


def phase_b(nc, T, st, dr, x1_deps, out_deps):
    sb = lambda name, shape, dt: st.enter_context(nc.sbuf_tensor("sc_" + name, shape, dt))
    x1_d, out_d = dr["x1"], dr["out"]
    QW = 256
    NJ = S // QW
    NT = S // 128

    colv = sb("colv", [128, NCOLV], F32); d_colv = Dep()
    cst = sb("cst", [128, NCST], F32); d_cst = Dep()
    identb = sb("identb", [128, 128], BF16); d_id = Dep()
    nhalf = sb("nhalf", [128, 8], F32); d_nh = Dep()
    mneg = sb("mneg", [128, 2, 128], BF16); d_mneg = Dep()
    hv = sb("hv", [128, 8], F32); d_hv = Dep()
    gsc = sb("gsc", [128, 2], F32); d_gsc = Dep()
    gpost = sb("gpost", [128, D], F32); d_gp = Dep()
    lamt = sb("lamt", [128, 256], F32); d_lam = Dep()
    lams = sb("lams", [128, 8], F32); d_lams = Dep()
    KT = sb("KT", [128, NH, S], BF16); d_KT = [Dep() for _ in range(NH)]
    Va = sb("Va", [128, NT * NH, 130], BF16); d_V = Dep()
    wIN = sb("wIN", [128, KC, 2 * D], BF16); d_wIN = (Dep(), Dep())
    wOB = sb("wOB", [128, KC, D], BF16); d_wOB = (Dep(), Dep())
    R = sb("R", [128, 18432], BF16)
    xt = sb("xt", [128, 2, D], F32)
    xr = sb("xr", [128, 2, D], F32)
    junk = sb("junk", [128, D], BF16); d_junk = Dep()
    hbf = sb("hbf", [128, 1, D], BF16)
    hT2 = sb("hT", [128, 2, KC, QW], BF16); d_hT2 = [Dep(), Dep()]
    QT2 = sb("QT", [128, 2, NH, 2 * QW], BF16); d_QT2 = [Dep(), Dep()]
    hT, d_hT = hT2[:, 0, :, :], d_hT2[0]
    tz = sb("tz", [128, 1, 512], F32)
    smalls = sb("smalls", [128, 4, 16], F32)
    osm = sb("osm", [128, 4, 8], F32)

    wKV = R[:, 0:16384].rearrange("p (k n) -> p k n", k=KC); d_wKV = (Dep(), Dep())
    gz2 = R[:, 0:4096].rearrange("p (b t n) -> p b t n", b=2, t=2); d_gz2 = [Dep(), Dep()]
    yo = R[:, 4096:8192].bitcast(F32).rearrange("p (t n) -> p t n", t=2)
    gated2 = R[:, 8192:12288].rearrange("p (b t n) -> p b t n", b=2, t=2); d_gated2 = [[Dep(), Dep()], [Dep(), Dep()]]
    gT = R[:, 12288:14336].rearrange("p (s k n) -> p s k n", s=2, k=KC)
    et = R[:, 14336:16384].rearrange("p (s c n) -> p s c n", s=4, c=2)
    otmp = R[:, 16384:18432].bitcast(F32).rearrange("p (s a n) -> p s a n", s=2, a=4)

    xt_r = Ring(T, 2, lambda i: xt[:, i, :], dma=True)
    xr_r = Ring(T, 2, lambda i: xr[:, i, :], dma=True)
    hbf_r = Ring(T, 1, lambda i: hbf[:, i, :])
    tz_r = Ring(T, 1, lambda i: tz[:, i, :])
    sm_r = Ring(T, 4, lambda i: smalls[:, i, :])
    osm_r = Ring(T, 4, lambda i: osm[:, i, :])
    yo_r = Ring(T, 2, lambda i: yo[:, i, :], dma=True)
    gT_r = Ring(T, 2, lambda i: gT[:, i, :, :])
    et_r = Ring(T, 4, lambda i: et[:, i, :, :])
    ot_r = Ring(T, 2, lambda i: otmp[:, i, :, :])

    pS = [st.enter_context(nc.psum_tensor("psS%d" % i, [128, 2, QW], F32)) for i in range(3)]
    d_S = [Dep(psum=True), Dep(psum=True), Dep(psum=True)]
    pOs = [st.enter_context(nc.psum_tensor("psO%d" % i, [128, 512], F32)) for i in range(2)]
    d_Os = [Dep(psum=True) for _ in range(2)]
    pYs = [st.enter_context(nc.psum_tensor("psY%d" % i, [128, 512], F32)) for i in range(3)]
    d_Ys = [Dep(psum=True) for _ in range(3)]
    pT, d_pT = pYs[0][:].bitcast(BF16), d_Ys[0]
    yb_i = [0]
    o_ctr = [0]

    y_flush = [None]

    def next_y():
        yb_i[0] = (yb_i[0] + 1) % 3
        if y_flush[0] is not None and yb_i[0] in y_pending:
            y_flush[0]()
        return pYs[yb_i[0]], d_Ys[yb_i[0]]
    y_pending = set()
    pY, d_Y = pYs[0], d_Ys[0]
    GB = [(pS[0][:].rearrange("p c n -> p (c n)"), d_S[0]), (pS[1][:].rearrange("p c n -> p (c n)"), d_S[1]),
          (pS[2][:].rearrange("p c n -> p (c n)"), d_S[2]),
          (pOs[0][:], d_Os[0]), (pOs[1][:], d_Os[1]), (pYs[1][:], d_Ys[1]), (pYs[2][:], d_Ys[2])]
    gb_i = [0]

    def next_bank():
        gb_i[0] = (gb_i[0] + 1) % len(GB)
        return GB[gb_i[0]]

    sems = [T.dma_sem() for _ in range(5)]
    T.dma("sp", lambda e: e.dma_start(out=colv[:], in_=dr["colv"]), sems[0], writes=[d_colv])
    T.dma("sp", lambda e: e.dma_start(out=cst[:], in_=dr["cst"]), sems[1], writes=[d_cst])
    T.dma("sp", lambda e: e.dma_start(out=gpost[:], in_=dr["rowv"][1].partition_broadcast(128)), sems[2], writes=[d_gp])
    T.dma("sp", lambda e: e.dma_start(out=lamt[:], in_=dr["lamv"].partition_broadcast(128)), sems[4], writes=[d_lam])
    T.op("dve", lambda e: e.tensor_copy(out=identb[:], in_=cst[:, 0:128]), reads=[d_cst], writes=[d_id])
    T.op("pool", lambda e: e.memset(nhalf[:], -0.5), writes=[d_nh])
    T.op("pool", lambda e: e.memset(Va[:, :, 128:130], 1.0), writes=[d_V])
    T.op("pool", lambda e: e.memset(mneg[:], 0.0), writes=[d_mneg])
    T.op("pool", lambda e: e.affine_select(out=mneg[:], in_=mneg[:], pattern=[[0, 2], [1, 128]], compare_op=ALU.is_ge,
                                           fill=-30000.0, base=0, channel_multiplier=-1), reads=[d_mneg], writes=[d_mneg])
    T.op("dve", lambda e: e.tensor_scalar(out=hv[:, 0:8], in0=colv[:, 16:24], scalar1=0.5, scalar2=None, op0=ALU.mult),
         reads=[d_colv], writes=[d_hv])
    T.op("dve", lambda e: e.tensor_scalar(out=gsc[:, 0:1], in0=colv[:, NCOLV - 1:NCOLV], scalar1=1.0 - LAM_INIT, scalar2=None,
                                          op0=ALU.mult), reads=[d_colv], writes=[d_gsc])
    T.op("dve", lambda e: e.scalar_tensor_tensor(out=junk[:, 0:64], in0=lamt[:, 0:64], scalar=1.0, in1=lamt[:, 64:128],
                                                 op0=ALU.mult, op1=ALU.mult, accum_out=lams[:, 0:1]),
         reads=[d_lam], writes=[d_junk, d_lams])
    T.op("dve", lambda e: e.scalar_tensor_tensor(out=junk[:, 0:64], in0=lamt[:, 128:192], scalar=1.0, in1=lamt[:, 192:256],
                                                 op0=ALU.mult, op1=ALU.mult, accum_out=lams[:, 1:2]),
         reads=[d_lam], writes=[d_junk, d_lams])
    T.op("act", lambda e: e.activation(out=lams[:, 2:4], in_=lams[:, 0:2], func=AF.Exp), reads=[d_lams], writes=[d_lams])
    T.op("dve", lambda e: e.scalar_tensor_tensor(out=lams[:, 4:5], in0=lams[:, 3:4], scalar=-LAM_INIT, in1=lams[:, 2:3],
                                                 op0=ALU.add, op1=ALU.subtract), reads=[d_lams], writes=[d_lams])
    nlam = lams[:, 4:5]

    def scaleIN(k, cb):
        if cb == 0:
            return (colv[:, 16 + k:17 + k], d_colv)
        return (hv[:, k:k + 1], d_hv)
    stage_slots = ring_slots(xt_r, xr_r)
    qst = QT2[:].rearrange("p b h n -> p (b h n)").bitcast(F32).rearrange("p (s n) -> p s n", s=4)
    qst_slots = [(qst[:, i_, :], Dep(), T.dma_sem()) for i_ in range(4)]
    late_w = (weight_chunk_emitters(T, dr["b_w_in"], 2 * D, wIN, d_wIN, qst_slots, scaleIN, dve_only=True)
              + weight_chunk_emitters(T, dr["b_w_out"], D, wOB, d_wOB, qst_slots, lambda k, cb: (gsc[:, 0:1], d_gsc),
                                      dve_only=True))

    wkv_sems = [T.dma_sem() for _ in range(KC)]
    d_wkvbf = [Dep() for _ in range(KC)]
    xslots = {}

    def fetch_x1(gtile):
        si = xt_r.next()
        xslots[gtile] = si
        T.dma("sp", lambda e, o=xt_r.ap(si), i=x1_d[gtile * 128:(gtile + 1) * 128, :]: e.dma_start(out=o, in_=i),
              xt_r.sems[si], reads=[x1_deps[gtile]], writes=[xt_r.deps[si]])

    n_evac = [0]

    def evac_copy(out_ap, in_ap, rd, wr, scale=None):
        n_evac[0] += 1
        if False:
            if scale is None:
                T.op("act", lambda e: e.activation(out=out_ap, in_=in_ap, func=AF.Copy), reads=rd, writes=wr)
            else:
                T.op("act", lambda e: e.activation(out=out_ap, in_=in_ap, func=AF.Copy, scale=scale), reads=rd, writes=wr)
        else:
            if scale is None:
                T.op("dve", lambda e: e.tensor_copy(out=out_ap, in_=in_ap), reads=rd, writes=wr)
            else:
                T.op("dve", lambda e: e.tensor_scalar(out=out_ap, in0=in_ap, scalar1=scale, scalar2=None, op0=ALU.mult),
                     reads=rd, writes=wr)

    for s in range(NSEQ):
        tbase = s * NT
        T.barrier()
        if s == 0:
            load_cast_weight(T, dr["w_kv"], 2 * D, wKV, d_wKV, stage_slots, lambda k, cb: (colv[:, 8 + k:9 + k], d_colv))
            for k in range(KC):
                T.dma("sp", lambda e, k=k: e.dma_start(out=dr["wkvbf"][:, k, :], in_=wKV[:, k, :]), wkv_sems[k],
                      reads=[*d_wKV], writes=[d_wkvbf[k]])
        else:
            for k in range(KC):
                T.dma("sp", lambda e, k=k: e.dma_start(out=wKV[:, k, :], in_=dr["wkvbf"][:, k, :]), wkv_sems[k],
                      reads=[d_wkvbf[k]], writes=[d_wKV[k % 2]])
        kv_slots = ring_slots(xt_r, xr_r)
        kv_map = {}
        kv_ctr = [0]

        def kv_fetch(gtile):
            ap_, dep_, sem_ = kv_slots[kv_ctr[0] % len(kv_slots)]
            kv_ctr[0] += 1
            kv_map[gtile] = (ap_, dep_)
            T.dma("sp", lambda e, o=ap_, i=x1_d[gtile * 128:(gtile + 1) * 128, :]: e.dma_start(out=o, in_=i),
                  sem_, reads=[x1_deps[gtile]], writes=[dep_])

        kvn = {}

        def kv_norm_stage(g, stage, tl):
            gtile = tbase + 2 * g + tl
            if stage == 0:
                ap_, dep_ = kv_map.pop(gtile)
                smi = sm_r.next()
                kvn[(g, tl)] = (ap_, dep_, smi)
                norm_stage_a1(T, ap_, dep_, junk[:], d_junk, sm_r.ap(smi), sm_r.deps[smi], nhalf, d_nh)
            elif stage == 1:
                ap_, dep_, smi = kvn.pop((g, tl))
                hi = hbf_r.next()
                kvn[(g, tl)] = hi
                norm_stage_a2(T, ap_, dep_, sm_r.ap(smi), sm_r.deps[smi], hbf_r.ap(hi), hbf_r.deps[hi])
                if g + 2 < NJ:
                    kv_fetch(tbase + 2 * (g + 2) + tl)
            else:
                hi = kvn.pop((g, tl))
                norm_stage_b(T, hbf_r.ap(hi), hbf_r.deps[hi], identb[:], d_id, pT, d_pT,
                             hT2[:, g % 2, :, tl * 128:(tl + 1) * 128], d_hT2[g % 2])

        def kv_norm(g):
            for tl in range(2):
                kv_norm_stage(g, 0, tl)
            for tl in range(2):
                kv_norm_stage(g, 1, tl)
                kv_norm_stage(g, 2, tl)

        for t_ in range(4):
            kv_fetch(tbase + t_)
        kv_norm(0)
        for g in range(NJ):
            if g + 1 < NJ:
                kv_norm_stage(g + 1, 0, 0)
                kv_norm_stage(g + 1, 0, 1)
            hTg, d_hTg = hT2[:, g % 2, :, :], d_hT2[g % 2]
            ksched = {1: [(1, 0)], 4: [(2, 0)], 5: [(1, 1)], 9: [(2, 1)]}
            unit = 0
            for h in range(NH):
                if s == 0 and h in (1, 4, 7) and late_w:
                    late_w.pop(0)()
                pb, d_pb = next_bank()
                for k in range(KC):
                    T.op("pe", lambda e, k=k, h=h, pb=pb, hTg=hTg: e.matmul(
                        pb[:, 0:QW], lhsT=wKV[:, k, h * 128:(h + 1) * 128],
                        rhs=hTg[:, k, :], start=(k == 0), stop=(k == KC - 1)),
                        reads=[*d_wKV, d_hTg], writes=[d_pb], inc=(k == KC - 1))
                evac_copy(KT[:, h, g * QW:(g + 1) * QW], pb[:, 0:QW], [d_pb], [d_KT[h]])
                if g + 1 < NJ:
                    for st_, tl_ in ksched.get(unit, []):
                        kv_norm_stage(g + 1, st_, tl_)
                unit += 1
            for tl in range(2):
                t_in = 2 * g + tl
                for half in range(2):
                    pb, d_pb = next_bank()
                    for k in range(KC):
                        T.op("pe", lambda e, k=k, tl=tl, half=half, pb=pb, hTg=hTg: e.matmul(
                            pb[:, 0:512], lhsT=hTg[:, k, tl * 128:(tl + 1) * 128],
                            rhs=wKV[:, k, D + half * 512: D + (half + 1) * 512], start=(k == 0), stop=(k == KC - 1)),
                            reads=[*d_wKV, d_hTg], writes=[d_pb], inc=(k == KC - 1))
                    evac_copy(Va[:, t_in * NH + 4 * half: t_in * NH + 4 * half + 4, 0:128],
                              pb[:, 0:512].rearrange("p (h e) -> p h e", h=4), [d_pb], [d_V])
                    if g + 1 < NJ:
                        for st_, tl_ in ksched.get(unit, []):
                            kv_norm_stage(g + 1, st_, tl_)
                    unit += 1

        if s == 0:
            while late_w:
                late_w.pop(0)()
        T.barrier()
        if s == 0:
            for b_ in range(2):
                T.op("pool", lambda e, b_=b_: e.memset(QT2[:, b_, :, :], 0.0), writes=[d_QT2[b_]])

        pn_state = {}
        act_q = []
        cur_step = [None]

        def defer_act(fn, delay=2):
            if cur_step[0] is None:
                fn()
            else:
                act_q.append((cur_step[0] + delay, fn))

        def run_deferred(upto=None):
            while act_q and (upto is None or act_q[0][0] <= upto):
                act_q.pop(0)[1]()
            if not act_q:
                y_pending.clear()
        y_flush[0] = run_deferred

        def pro_norm_a1(j, tl):
            def f():
                gtile = tbase + 2 * j + tl
                si = xslots.pop(gtile)
                smi = sm_r.next()
                pn_state[(j, tl)] = (si, smi)
                norm_stage_a1(T, xt_r.ap(si), xt_r.deps[si], junk[:], d_junk, sm_r.ap(smi), sm_r.deps[smi], nhalf, d_nh)
            return f

        def pro_norm_a2(j, tl):
            def f():
                si, smi = pn_state[(j, tl)]
                hi = hbf_r.next()
                pn_state[(j, tl)] = hi
                norm_stage_a2(T, xt_r.ap(si), xt_r.deps[si], sm_r.ap(smi), sm_r.deps[smi], hbf_r.ap(hi), hbf_r.deps[hi])
                if tl == 1 and j + 1 < NJ:
                    fetch_x1(tbase + 2 * (j + 1))
                    fetch_x1(tbase + 2 * (j + 1) + 1)
            return f

        def pro_norm_b(j, tl):
            def f():
                hi = pn_state.pop((j, tl))
                pYb, d_Yb = next_y()
                norm_stage_b(T, hbf_r.ap(hi), hbf_r.deps[hi], identb[:], d_id, pYb[:].bitcast(BF16), d_Yb,
                             hT2[:, j % 2, :, tl * 128:(tl + 1) * 128], d_hT2[j % 2])
            return f

        def pro_q(j, h):
            def f():
                hTj, d_hTj = hT2[:, j % 2, :, :], d_hT2[j % 2]
                pY, d_Y = next_y()
                for k in range(KC):
                    T.op("pe", lambda e, k=k: e.matmul(pY[:, 0:QW], lhsT=wIN[:, k, h * 128:(h + 1) * 128],
                                                       rhs=hTj[:, k, :], start=(k == 0), stop=(k == KC - 1)),
                         reads=[*d_wIN, d_hTj], writes=[d_Y], inc=(k == KC - 1))
                for c in range(2):
                    T.op("dve", lambda e, c=c: e.tensor_scalar(
                        out=QT2[c * 64:(c + 1) * 64, j % 2, h, c * QW:(c + 1) * QW], in0=pY[c * 64:(c + 1) * 64, 0:QW],
                        scalar1=0.125, scalar2=None, op0=ALU.mult), reads=[d_Y], writes=[d_QT2[j % 2]])
            return f

        def pro_z(j, tl, half):
            def f():
                hTj, d_hTj = hT2[:, j % 2, :, :], d_hT2[j % 2]
                pY, d_Y = next_y()
                for k in range(KC):
                    T.op("pe", lambda e, k=k: e.matmul(
                        pY[:, 0:512], lhsT=hTj[:, k, tl * 128:(tl + 1) * 128],
                        rhs=wIN[:, k, D + half * 512: D + (half + 1) * 512], start=(k == 0), stop=(k == KC - 1)),
                        reads=[*d_wIN, d_hTj], writes=[d_Y], inc=(k == KC - 1))

                def post():
                    ti = tz_r.next()
                    gz_ap = gz2[:, j % 2, tl, half * 512:(half + 1) * 512]
                    T.op("act", lambda e: e.activation(out=tz_r.ap(ti), in_=pY[:, 0:512], func=AF.Tanh),
                         reads=[d_Y], writes=[tz_r.deps[ti]])
                    T.op("dve", lambda e: e.scalar_tensor_tensor(
                        out=gz_ap, in0=tz_r.ap(ti), scalar=1.0, in1=pY[:, 0:512], op0=ALU.add, op1=ALU.mult),
                        reads=[tz_r.deps[ti], d_Y], writes=[d_gz2[j % 2]])
                if cur_step[0] is not None:
                    y_pending.add(yb_i[0])
                defer_act(post)
            return f

        def prologue_items(j):
            return ([pro_norm_a1(j, 0), pro_norm_a1(j, 1), pro_norm_a2(j, 0), pro_norm_b(j, 0), pro_norm_a2(j, 1), pro_norm_b(j, 1)]
                    + [pro_q(j, h) for h in range(NH)]
                    + [pro_z(j, tl, half) for tl in range(2) for half in range(2)])

        def interleave(la, lb):
            out = []
            for k_ in range(max(len(la), len(lb))):
                if k_ < len(lb):
                    out.append(lb[k_])
                if k_ < len(la):
                    out.append(la[k_])
            return out

        ep_state = {}

        xr_state = {}

        def ep_reload(j):
            def f():
                for i in range(2):
                    gtile = tbase + 2 * j + i
                    ri = xr_r.next()
                    xr_state[(j, i)] = ri
                    T.dma("sp", lambda e, o=xr_r.ap(ri), i_=x1_d[gtile * 128:(gtile + 1) * 128, :]: e.dma_start(out=o, in_=i_),
                          xr_r.sems[ri], reads=[x1_deps[gtile]], writes=[xr_r.deps[ri]])
            return f

        def ep_tr(j, i):
            def f():
                gtile = tbase + 2 * j + i
                ri = xr_state.pop((j, i))
                gi = gT_r.next()
                g_ap, d_g = gT_r.ap(gi), gT_r.deps[gi]
                pYb, d_pT = next_y()
                pT = pYb[:].bitcast(BF16)
                for k in range(KC):
                    T.op("pe", lambda e, k=k: e.transpose(out=pT[:, k * 128:(k + 1) * 128],
                                                          in_=gated2[:, j % 2, i, k * 128:(k + 1) * 128], identity=identb[:]),
                         reads=[d_gated2[j % 2][i], d_id], writes=[d_pT], inc=(k == KC - 1))
                T.op("dve", lambda e: e.tensor_copy(out=g_ap, in_=pT.rearrange("p (k n) -> p k n", k=KC)),
                     reads=[d_pT], writes=[d_g])
                ep_state[(j, i)] = (ri, g_ap, d_g, yo_r.next(), sm_r.next())
            return f

        def ep_y(j, i, half):
            def f():
                ri, g_ap, d_g, yi, smi = ep_state[(j, i)]
                pY, d_Y = next_y()
                for k in range(KC):
                    T.op("pe", lambda e, k=k: e.matmul(
                        pY[:, 0:512], lhsT=g_ap[:, k, :], rhs=wOB[:, k, half * 512:(half + 1) * 512],
                        start=(k == 0), stop=(k == KC - 1)),
                        reads=[d_g, *d_wOB], writes=[d_Y], inc=(k == KC - 1))

                def post():
                    small = sm_r.ap(smi)
                    T.op("act", lambda e: e.activation(out=yo_r.ap(yi)[:, half * 512:(half + 1) * 512], in_=pY[:, 0:512],
                                                       func=AF.Copy), reads=[d_Y], writes=[yo_r.deps[yi]])
                    T.op("act", lambda e: e.activation(out=junk[:, 0:512], in_=pY[:, 0:512], func=AF.Square,
                                                       accum_out=small[:, 4 + half:5 + half]),
                         reads=[d_Y], writes=[d_junk, sm_r.deps[smi]])
                if cur_step[0] is not None:
                    y_pending.add(yb_i[0])
                defer_act(post)
            return f

        def ep_post_a(j, i):
            def f():
                ri, g_ap, d_g, yi, smi = ep_state[(j, i)]
                run_deferred()
                emit_post_a(T, sm_r.ap(smi), sm_r.deps[smi], nhalf, d_nh)
            return f

        def ep_post_b(j, i):
            def f():
                gtile = tbase + 2 * j + i
                ri, g_ap, d_g, yi, smi = ep_state.pop((j, i))
                emit_post_b(T, yo_r.ap(yi), yo_r.deps[yi], sm_r.ap(smi), sm_r.deps[smi], gpost[:], d_gp,
                            xr_r.ap(ri), xr_r.deps[ri], yo_r.ap(yi), yo_r.deps[yi])
                T.dma("sp", lambda e, o=out_d[gtile * 128:(gtile + 1) * 128, :], i_=yo_r.ap(yi): e.dma_start(out=o, in_=i_),
                      yo_r.sems[yi], reads=[yo_r.deps[yi]], writes=[out_deps[gtile]])
            return f

        def epilogue_items(j):
            out = [ep_reload(j)]
            for i in range(2):
                out += [ep_tr(j, i), ep_y(j, i, 0), ep_y(j, i, 1), ep_post_a(j, i), ep_post_b(j, i)]
            return out

        fetch_x1(tbase + 0)
        fetch_x1(tbase + 1)
        for it in prologue_items(0):
            it()

        for j in range(NJ):
            QT, d_QT = QT2[:, j % 2, :, :], d_QT2[j % 2]
            gz, d_gz = gz2[:, j % 2, :, :], d_gz2[j % 2]
            gated, d_gated = gated2[:, j % 2, :, :], d_gated2[j % 2]
            side = interleave(epilogue_items(j - 1) if j >= 1 else [],
                              prologue_items(j + 1) if j + 1 < NJ else [])
            nkb = 2 * j + 2
            steps = [(h, kb) for h in range(NH) for kb in range(nkb)]
            st_e = {}
            obank = {}
            for h_ in range(NH):
                obank[h_] = (0, 1)
                o_ctr[0] += 1

            def emit_S(n):
                h, kb = steps[n]
                pSb = pS[n % 3]
                diag_i = [i for i in range(2) if kb == 2 * j + i]
                if kb == 2 * j + 1:
                    T.op("pe", lambda e, kb=kb, pSb=pSb, h=h, QT=QT, last=(not diag_i): e.matmul(
                        pSb[:, :, 128:QW], lhsT=KT[:, h, kb * 128:(kb + 1) * 128],
                        rhs=QT[:, h, :].rearrange("p (c n) -> p c n", c=2)[:, :, 128:QW], start=True, stop=last),
                        reads=[d_KT[h], d_QT], writes=[d_S[n % 3]], inc=(not diag_i))
                else:
                    T.op("pe", lambda e, kb=kb, pSb=pSb, h=h, QT=QT, last=(not diag_i): e.matmul(
                        pSb[:].rearrange("p c n -> p (c n)"), lhsT=KT[:, h, kb * 128:(kb + 1) * 128],
                        rhs=QT[:, h, :], start=True, stop=last),
                        reads=[d_KT[h], d_QT], writes=[d_S[n % 3]], inc=(not diag_i))
                for i in diag_i:
                    T.op("pe", lambda e, i=i, pSb=pSb: e.matmul(
                        pSb[:, :, i * 128:(i + 1) * 128], lhsT=identb[:], rhs=mneg[:], start=False, stop=True),
                        reads=[d_id, d_mneg], writes=[d_S[n % 3]], inc=True)

            def emit_exp(n):
                h, kb = steps[n]
                c0 = 128 if kb == 2 * j + 1 else 0
                pSb = pS[n % 3]
                dS0 = dS1 = d_S[n % 3]
                ei = et_r.next()
                e_ap = et_r.ap(ei)
                d_e = et_r.deps[ei]
                st_e[n] = (e_ap, d_e)
                if h == 0:
                    for i in range(2):
                        if 128 * i < c0:
                            continue
                        col = 128 + h * 17 + (kb - 2 * j - i) + 15
                        T.op("act", lambda e, i=i, col=col, pSb=pSb, e_ap=e_ap: e.activation(
                            out=e_ap[:, :, i * 128:(i + 1) * 128], in_=pSb[:, :, i * 128:(i + 1) * 128], func=AF.Exp,
                            bias=cst[:, col:col + 1], scale=1.0), reads=[dS0, dS1, d_cst], writes=[d_e])
                else:
                    col = 128 + h * 17 + (kb - 2 * j) + 15
                    T.op("act", lambda e, col=col, c0=c0, pSb=pSb, e_ap=e_ap: e.activation(
                        out=e_ap[:, :, c0:QW], in_=pSb[:, :, c0:QW], func=AF.Exp,
                        bias=cst[:, col:col + 1], scale=1.0), reads=[dS0, dS1, d_cst], writes=[d_e])

            def emit_PV(n):
                h, kb = steps[n]
                e_ap, d_e = st_e.pop(n)
                pv = [(i, c) for i in range(2) if kb <= 2 * j + i for c in range(2)]
                for n_, (i, c) in enumerate(pv):
                    ob = obank[h][i]
                    T.op("pe", lambda e, i=i, c=c, kb=kb, h=h, e_ap=e_ap, ob=ob, last=(kb == 2 * j + i): e.matmul(
                        pOs[ob][:, c * 130:(c + 1) * 130], lhsT=e_ap[:, c, i * 128:(i + 1) * 128],
                        rhs=Va[:, kb * NH + h, 0:130], start=(kb == 0 and c == 0), stop=last,
                        skip_group_check=True),
                        reads=[d_e, d_V], writes=[d_Os[ob]], inc=(n_ == len(pv) - 1))

            fin = {}

            def emit_final_copy(h, i):
                oi = ot_r.next()
                o_ap = ot_r.ap(oi)
                d_o = ot_r.deps[oi]
                raw = o_ap[:, 0:3, :].rearrange("p a n -> p (a n)")
                qi = osm_r.next()
                q_ap = osm_r.ap(qi)
                d_q = osm_r.deps[qi]
                ob = obank[h][i]
                T.op("dve", lambda e, ob=ob, raw=raw: e.tensor_copy(out=raw[:, 0:260], in_=pOs[ob][:, 0:260]),
                     reads=[d_Os[ob]], writes=[d_o])
                fin[(h, i)] = (o_ap, d_o, raw, q_ap, d_q)

            fin2 = {}

            def emit_final_rest1(h, i):
                o_ap, d_o, raw, q_ap, d_q = fin.pop((h, i))
                oo, sq = o_ap[:, 3, :], o_ap[:, 0, :]
                T.op("dve", lambda e, raw=raw, q_ap=q_ap: e.reciprocal(
                    out=q_ap[:, 0:2], in_=raw[:, 0:260].rearrange("p (c n) -> p c n", c=2)[:, :, 128]),
                    reads=[d_o], writes=[d_q])
                T.op("dve", lambda e, q_ap=q_ap: e.tensor_tensor(out=q_ap[:, 5:6], in0=q_ap[:, 1:2], in1=nlam, op=ALU.mult),
                     reads=[d_q, d_lams], writes=[d_q])
                T.op("dve", lambda e, raw=raw, q_ap=q_ap: e.tensor_scalar(
                    out=raw[:, 0:128], in0=raw[:, 0:128], scalar1=q_ap[:, 0:1], scalar2=None, op0=ALU.mult),
                    reads=[d_o, d_q], writes=[d_o])
                T.op("dve", lambda e, raw=raw, q_ap=q_ap, oo=oo: e.scalar_tensor_tensor(
                    out=oo, in0=raw[:, 130:258], scalar=q_ap[:, 5:6], in1=raw[:, 0:128], op0=ALU.mult, op1=ALU.add),
                    reads=[d_o, d_q], writes=[d_o])
                T.op("dve", lambda e, oo=oo, sq=sq, q_ap=q_ap: e.scalar_tensor_tensor(
                    out=sq, in0=oo, scalar=1.0, in1=oo, op0=ALU.mult, op1=ALU.mult, accum_out=q_ap[:, 2:3]),
                    reads=[d_o], writes=[d_o, d_q])
                emit_rstd(T, q_ap[:, 2:3], q_ap[:, 3:4], q_ap[:, 4:5], nhalf[:, 0:1], d_q, d_q, d_q, d_nh, 1.0 / 128)
                fin2[(h, i)] = (oo, d_o, q_ap, d_q)

            def emit_final_rest2(h, i):
                oo, d_o, q_ap, d_q = fin2.pop((h, i))
                T.op("dve", lambda e, i=i, h=h, oo=oo, q_ap=q_ap, gated=gated, gz=gz: e.scalar_tensor_tensor(
                    out=gated[:, i, h * 128:(h + 1) * 128], in0=oo, scalar=q_ap[:, 4:5],
                    in1=gz[:, i, h * 128:(h + 1) * 128], op0=ALU.mult, op1=ALU.mult),
                    reads=[d_o, d_q, d_gz], writes=[d_gated[i]])

            nsteps = len(steps)
            nside = len(side)
            emit_S(0)
            if nsteps > 1:
                emit_S(1)
            done_side = 0

            def after_PV(n):
                h, kb = steps[n]
                for i in range(2):
                    if kb == 2 * j + i:
                        emit_final_copy(h, i)
                if kb == nkb - 1:
                    emit_final_rest1(h, 0)
                    emit_final_rest1(h, 1)
                    emit_final_rest2(h, 0)
                    emit_final_rest2(h, 1)

            for n in range(nsteps):
                cur_step[0] = n
                emit_exp(n)
                run_deferred(n)
                if n + 2 < nsteps:
                    emit_S(n + 2)
                if n >= 1:
                    emit_PV(n - 1)
                    after_PV(n - 1)
                want = min(nside, ((n + 1) * nside * 20) // (nsteps * 13))
                while done_side < want:
                    side[done_side]()
                    done_side += 1
            emit_PV(nsteps - 1)
            after_PV(nsteps - 1)
            cur_step[0] = None
            run_deferred()

        for it in epilogue_items(NJ - 1):
            it()

def build_program(mode="fused"):
    nc = bass.Bass("TRN2", target_bir_lowering=False)
    dr = {}
    if mode in ("A", "fused"):
        dr["x"] = nc.dram_tensor("x", [NTOK, D], F32, kind="ExternalInput").ap()
    for name, shape in (("a_w_in", [D, 3 * D]), ("a_w_out", [D, D]), ("w_kv", [D, 2 * D]),
                        ("b_w_in", [D, 2 * D]), ("b_w_out", [D, D]), ("colv", [128, NCOLV]),
                        ("rowv", [3, D]), ("lamv", [256]), ("cst", [128, NCST])):
        dr[name] = nc.dram_tensor(name, shape, F32, kind="ExternalInput").ap()
    if mode == "A":
        dr["x1"] = nc.dram_tensor("x1", [NTOK, D], F32, kind="ExternalOutput").ap()
    elif mode == "B":
        dr["x1"] = nc.dram_tensor("x1", [NTOK, D], F32, kind="ExternalInput").ap()
    else:
        dr["x1"] = nc.dram_tensor("x1", [NTOK, D], F32).ap()
    if mode in ("B", "fused"):
        dr["out"] = nc.dram_tensor("out", [NTOK, D], F32, kind="ExternalOutput").ap()
        dr["wkvbf"] = nc.dram_tensor("wkvbf", [128, KC, 2 * D], BF16).ap()
    with ExitStack() as st0:
        T = Tracker(nc, st0)
        x1_deps = [Dep() for _ in range(NTOK // 128)]
        out_deps = [Dep() for _ in range(NTOK // 128)]
        if mode in ("A", "fused"):
            with ExitStack() as st:
                phase_a(nc, T, st, dr, x1_deps)
        if mode == "fused":
            T.barrier()
        if mode in ("B", "fused"):
            with ExitStack() as st:
                phase_b(nc, T, st, dr, x1_deps, out_deps)
        final = [d.w for d in (x1_deps if mode == "A" else out_deps)]
        T.finish(final)
        T.replay()
    return nc


def host_pack(inputs):
    f = lambda a: np.ascontiguousarray(np.asarray(a, dtype=np.float32))
    pc = lambda v: f(v).reshape(KC, 128).T
    colv = np.zeros((128, NCOLV), np.float32)
    colv[:, 0:8] = pc(inputs["a_g_pre"][0])
    colv[:, 8:16] = pc(inputs["kv_g"])
    colv[:, 16:24] = pc(inputs["b_g_pre"][0])
    colv[:, 24:32] = pc(inputs["a_b_dw"][0])
    colv[:, 32:40] = pc(inputs["a_ln_g"][0])
    colv[:, 40:48] = pc(inputs["a_ln_b"][0])
    colv[:, 48:48 + KC * KW] = f(inputs["a_w_dw"][0]).reshape(KW, KC, 128).transpose(2, 1, 0).reshape(128, KC * KW)
    colv[:, 48 + KC * KW] = f(inputs["b_g_sub"][0])
    rowv = np.stack([f(inputs["a_g_post"][0]), f(inputs["b_g_post"][0]), np.tile(f(inputs["b_g_sub"][0]), NH)])
    lamv = f(inputs["b_lambda"][0]).reshape(256)
    cst = np.zeros((128, NCST), np.float32)
    cst[:, 0:128] = np.eye(128, dtype=np.float32)
    p = np.arange(128, dtype=np.float32)
    for h in range(NH):
        m = 2.0 ** (-(h + 1))
        for dl in range(-15, 2):
            cst[:, 128 + h * 17 + dl + 15] = m * (p + 128.0 * dl)
    shared = {"a_w_in": f(inputs["a_w_in"][0]), "a_w_out": f(inputs["a_w_out"][0]), "w_kv": f(inputs["w_kv"]),
              "b_w_in": f(inputs["b_w_in"][0]), "b_w_out": f(inputs["b_w_out"][0]),
              "colv": colv, "rowv": f(rowv), "lamv": lamv, "cst": cst}
    return shared


MODE = "fused"


def kernel(**inputs):
    x = np.asarray(inputs["x"], dtype=np.float32)
    shared = host_pack(inputs)
    xs = [np.ascontiguousarray(x[2 * c:2 * c + 2].reshape(NTOK, D)) for c in range(8)]
    cores = list(range(8))
    if MODE == "unfused":
        ncA = build_program("A")
        resA = run_bass_kernel_spmd(ncA, [dict(shared, x=xs[c]) for c in cores], core_ids=cores)
        ncB = build_program("B")
        resB = run_bass_kernel_spmd(ncB, [dict(shared, x1=resA.results[c]["x1"]) for c in cores], core_ids=cores)
        res = resB
    else:
        nc = build_program("fused")
        res = run_bass_kernel_spmd(nc, [dict(shared, x=xs[c]) for c in cores], core_ids=cores)
    out = np.concatenate([r["out"].reshape(NSEQ, S, D) for r in res.results], axis=0)
    return out.astype(np.float32)
```

```python
import math
import numpy as np
from contextlib import ExitStack
import concourse.bass as bass
import concourse.mybir as mybir
from concourse.bass_utils import run_bass_kernel_spmd

F32 = mybir.dt.float32
BF16 = mybir.dt.bfloat16
AF = mybir.ActivationFunctionType
ALU = mybir.AluOpType

D = 1024
KC = 8
S = 2048
NSEQ = 2
NTOK = NSEQ * S
EPS = 1e-6
NH = 8
KW = 31
LAM_INIT = 0.8 - 0.6 * math.exp(-0.3 * 2)
GA = 256
NCOLV = 48 + 8 * KW + 1
NCST = 128 + NH * 17

ENGS = ("pe", "act", "dve", "pool", "sp")


class Dep:
    __slots__ = ("w", "r", "psum")

    def __init__(self, psum=False):
        self.w = None
        self.r = []
        self.psum = psum


class Tracker:
    def __init__(self, nc, stack):
        self.nc = nc
        self.stack = stack
        self.ops = {e: [] for e in ENGS}
        self.sems = {}
        self.cnt = {}
        self.seen = {e: {} for e in ENGS}
        for e in ENGS:
            self._sem("S_" + e)
        self.n_dma = 0

    def _sem(self, name):
        self.sems[name] = self.stack.enter_context(self.nc.semaphore(name))
        self.cnt[name] = 0
        return name

    def dma_sem(self):
        self.n_dma += 1
        return self._sem("D%d" % self.n_dma)

    def _collect(self, eng, reads, writes):
        own = "S_" + eng
        deps = {}

        def add(tok):
            if tok is not None and deps.get(tok[0], 0) < tok[1]:
                deps[tok[0]] = tok[1]

        for b in reads:
            add(b.w)
            if b.psum:
                for t in b.r:
                    if t[0] != own:
                        add(t)
        for b in writes:
            if b.w is not None and b.w[0] != own:
                add(b.w)
            for t in b.r:
                if t[0] != own:
                    add(t)
        waits = []
        for s, v in deps.items():
            if self.seen[eng].get(s, 0) < v:
                self.seen[eng][s] = v
                waits.append((s, v))
        return waits

    @staticmethod
    def _update(tok, reads, writes):
        for b in reads:
            b.r = [t for t in b.r if t[0] != tok[0]]
            b.r.append(tok)
        for b in writes:
            b.w = tok
            b.r = []

    def op(self, eng, fn, reads=(), writes=(), inc=True):
        waits = self._collect(eng, reads, writes)
        own = "S_" + eng
        if inc:
            self.cnt[own] += 1
            tok = (own, self.cnt[own])
            self.ops[eng].append((waits, fn, own, 1))
        else:
            tok = (own, self.cnt[own] + 1)
            self.ops[eng].append((waits, fn, None, 0))
        self._update(tok, reads, writes)
        return tok

    def dma(self, queue, fn, sem, reads=(), writes=()):
        waits = self._collect(queue, reads, writes)
        self.cnt[sem] += 16
        tok = (sem, self.cnt[sem])
        self.ops[queue].append((waits, fn, sem, 16))
        self._update(tok, reads, writes)
        return tok

    def barrier(self):
        for e in ENGS:
            waits = []
            for s, v in self.cnt.items():
                if v > 0 and s != "S_" + e and self.seen[e].get(s, 0) < v:
                    self.seen[e][s] = v
                    waits.append((s, v))
            if waits:
                self.ops[e].append((waits, None, None, 0))

    def finish(self, toks):
        best = {}
        for s, v in toks:
            best[s] = max(best.get(s, 0), v)
        self.ops["sp"].append((list(best.items()), None, None, 0))

    def replay(self):
        nc, sems, ops = self.nc, self.sems, self.ops
        with nc.Block() as block:
            def run(name):
                def body(eng):
                    for waits, fn, isem, ival in ops[name]:
                        for s, v in waits:
                            eng.wait_ge(sems[s], v)
                        if fn is not None:
                            ins = fn(eng)
                            if isem is not None:
                                ins.then_inc(sems[isem], ival)
                return body

            block.tensor(run("pe"))
            block.scalar(run("act"))
            block.vector(run("dve"))
            block.gpsimd(run("pool"))
            block.sync(run("sp"))


class Ring:
    def __init__(self, T, n, apfn, dma=False, psum=False):
        self.n = n
        self.apfn = apfn
        self.deps = [Dep(psum) for _ in range(n)]
        self.sems = [T.dma_sem() for _ in range(n)] if dma else None
        self.i = -1

    def next(self):
        self.i = (self.i + 1) % self.n
        return self.i

    def ap(self, i):
        return self.apfn(i)


def emit_rstd(T, ss_ap, v_ap, r_ap, nh_ap, d_ss, d_v, d_r, d_nh, inv_n):
    T.op("dve", lambda e: e.tensor_scalar(out=v_ap, in0=ss_ap, scalar1=inv_n, scalar2=EPS,
                                          op0=ALU.mult, op1=ALU.add), reads=[d_ss], writes=[d_v])
    T.op("pool", lambda e: e.tensor_tensor(out=r_ap, in0=v_ap, in1=nh_ap, op=ALU.pow),
         reads=[d_v, d_nh], writes=[d_r])


def weight_chunk_emitters(T, w_dram, ncols, wbf, d_w, slots, scale_fn, colblk=1024, ctr=[0], dve_only=False):
    out = []
    for k in range(KC):
        for cb in range(ncols // colblk):
            def f(k=k, cb=cb):
                n = ctr[0]
                ctr[0] += 1
                st_ap, st_dep, st_sem = slots[n % len(slots)]
                src = w_dram[k * 128:(k + 1) * 128, cb * colblk:(cb + 1) * colblk]
                T.dma("sp", lambda e, o=st_ap, i=src: e.dma_start(out=o, in_=i), st_sem, writes=[st_dep])
                dst = wbf[:, k, cb * colblk:(cb + 1) * colblk]
                sc = scale_fn(k, cb)
                rd = [st_dep]
                if sc is not None:
                    sc_ap, sc_dep = sc
                    rd = rd + [sc_dep]
                if n % 2 == 0 and not dve_only:
                    if sc is None:
                        T.op("act", lambda e, o=dst, i=st_ap: e.activation(out=o, in_=i, func=AF.Copy),
                             reads=rd, writes=[d_w[0]])
                    else:
                        T.op("act", lambda e, o=dst, i=st_ap, s=sc_ap: e.activation(out=o, in_=i, func=AF.Copy, scale=s),
                             reads=rd, writes=[d_w[0]])
                else:
                    if sc is None:
                        T.op("dve", lambda e, o=dst, i=st_ap: e.tensor_copy(out=o, in_=i), reads=rd, writes=[d_w[1]])
                    else:
                        T.op("dve", lambda e, o=dst, i=st_ap, s=sc_ap: e.tensor_scalar(
                            out=o, in0=i, scalar1=s, scalar2=None, op0=ALU.mult), reads=rd, writes=[d_w[1]])
            out.append(f)
    return out


def load_cast_weight(T, w_dram, ncols, wbf, d_w, slots, scale_fn, colblk=1024):
    for f in weight_chunk_emitters(T, w_dram, ncols, wbf, d_w, slots, scale_fn, colblk):
        f()


def ring_slots(*rings):
    out = []
    for r in rings:
        for i in range(r.n):
            out.append((r.ap(i), r.deps[i], r.sems[i]))
    return out


def norm_stage_a1(T, x_ap, d_x, junk, d_junk, small, d_small, nhalf, d_nh):
    ss, v, r = small[:, 0:1], small[:, 1:2], small[:, 2:3]
    T.op("act", lambda e: e.activation(out=junk, in_=x_ap, func=AF.Square, accum_out=ss),
         reads=[d_x], writes=[d_junk, d_small])
    emit_rstd(T, ss, v, r, nhalf[:, 0:1], d_small, d_small, d_small, d_nh, 1.0 / D)


def norm_stage_a2(T, x_ap, d_x, small, d_small, hbf_ap, d_hbf):
    r = small[:, 2:3]
    T.op("act", lambda e: e.activation(out=hbf_ap, in_=x_ap, func=AF.Copy, scale=r),
         reads=[d_x, d_small], writes=[d_hbf])


def norm_stage_a(T, x_ap, d_x, junk, d_junk, small, d_small, nhalf, d_nh, hbf_ap, d_hbf):
    norm_stage_a1(T, x_ap, d_x, junk, d_junk, small, d_small, nhalf, d_nh)
    norm_stage_a2(T, x_ap, d_x, small, d_small, hbf_ap, d_hbf)


def norm_stage_b(T, hbf_ap, d_hbf, identb, d_id, pT, d_pT, hT_out_ap, d_hT):
    for k in range(KC):
        T.op("pe", lambda e, k=k: e.transpose(out=pT[:, k * 128:(k + 1) * 128],
                                              in_=hbf_ap[:, k * 128:(k + 1) * 128], identity=identb),
             reads=[d_hbf, d_id], writes=[d_pT], inc=(k == KC - 1))
    T.op("dve", lambda e: e.tensor_copy(out=hT_out_ap, in_=pT.rearrange("p (k n) -> p k n", k=KC)),
         reads=[d_pT], writes=[d_hT])


def norm_transpose(T, x_ap, d_x, junk, d_junk, small, d_small, nhalf, d_nh, hbf_ap, d_hbf,
                   identb, d_id, pT, d_pT, hT_out_ap, d_hT):
    norm_stage_a(T, x_ap, d_x, junk, d_junk, small, d_small, nhalf, d_nh, hbf_ap, d_hbf)
    norm_stage_b(T, hbf_ap, d_hbf, identb, d_id, pT, d_pT, hT_out_ap, d_hT)


def emit_y_half(T, half, pY, d_pY, w_bf, d_w, lhs_fn, lhs_deps, yo_ap, d_yo, junk, d_junk, small, d_small):
    ssy = small[:, 4:6]
    for k in range(KC):
        T.op("pe", lambda e, k=k: e.matmul(
            pY[:, 0:512], lhsT=lhs_fn(k), rhs=w_bf[:, k, half * 512:(half + 1) * 512],
            start=(k == 0), stop=(k == KC - 1)),
            reads=list(lhs_deps) + list(d_w), writes=[d_pY], inc=(k == KC - 1))
    T.op("act", lambda e: e.activation(out=yo_ap[:, half * 512:(half + 1) * 512], in_=pY[:, 0:512],
                                       func=AF.Copy), reads=[d_pY], writes=[d_yo])
    T.op("act", lambda e: e.activation(out=junk[:, 0:512], in_=pY[:, 0:512], func=AF.Square,
                                       accum_out=ssy[:, half:half + 1]),
         reads=[d_pY], writes=[d_junk, d_small])


def emit_post_a(T, small, d_small, nhalf, d_nh):
    ssy = small[:, 4:6]
    ss, v, r = small[:, 6:7], small[:, 7:8], small[:, 8:9]
    T.op("dve", lambda e: e.tensor_tensor(out=ss, in0=ssy[:, 0:1], in1=ssy[:, 1:2], op=ALU.add),
         reads=[d_small], writes=[d_small])
    emit_rstd(T, ss, v, r, nhalf[:, 0:1], d_small, d_small, d_small, d_nh, 1.0 / D)


def emit_post_b(T, yo_ap, d_yo, small, d_small, gpost, d_gp, xres_ap, d_xres, out_ap, d_out):
    r = small[:, 8:9]
    T.op("dve", lambda e: e.scalar_tensor_tensor(out=yo_ap, in0=yo_ap, scalar=r, in1=gpost,
                                                 op0=ALU.mult, op1=ALU.mult),
         reads=[d_yo, d_small, d_gp], writes=[d_yo])
    T.op("dve", lambda e: e.tensor_tensor(out=out_ap, in0=yo_ap, in1=xres_ap, op=ALU.add),
         reads=[d_yo, d_xres], writes=[d_out])


def emit_post(T, yo_ap, d_yo, small, d_small, nhalf, d_nh, gpost, d_gp, xres_ap, d_xres, out_ap, d_out):
    emit_post_a(T, small, d_small, nhalf, d_nh)
    emit_post_b(T, yo_ap, d_yo, small, d_small, gpost, d_gp, xres_ap, d_xres, out_ap, d_out)


def post_norm_residual(T, pY, d_pY, w_bf, d_w, lhs_fn, lhs_deps, yo_ap, d_yo, junk, d_junk,
                       small, d_small, nhalf, d_nh, gpost, d_gp, xres_ap, d_xres, out_ap, d_out):
    for half in range(2):
        emit_y_half(T, half, pY, d_pY, w_bf, d_w, lhs_fn, lhs_deps, yo_ap, d_yo, junk, d_junk, small, d_small)
    emit_post(T, yo_ap, d_yo, small, d_small, nhalf, d_nh, gpost, d_gp, xres_ap, d_xres, out_ap, d_out)


def phase_a(nc, T, st, dr, x1_deps):
    sb = lambda name, shape, dt: st.enter_context(nc.sbuf_tensor("sb_" + name, shape, dt))
    x_d, x1_d = dr["x"], dr["x1"]

    colv = sb("colv", [128, NCOLV], F32); d_colv = Dep()
    cst = sb("cst", [128, NCST], F32); d_cst = Dep()
    identb = sb("identb", [128, 128], BF16); d_id = Dep()
    onesb = sb("onesb", [128, 128], BF16); d_ones = Dep()
    nhalf = sb("nhalf", [128, GA], F32); d_nh = Dep()
    hv = sb("hv", [128, 24], F32); d_hv = Dep()
    gpost = sb("gpost", [128, D], F32); d_gp = Dep()
    wA = sb("wA", [128, KC, 3 * D], BF16); d_wA = (Dep(), Dep())
    wO = sb("wO", [128, KC, D], BF16); d_wO = (Dep(), Dep())
    diag = sb("diag", [128, KC * KW, 128], BF16); d_diag = [(Dep(), Dep()) for _ in range(KC)]
    xt = sb("xt", [128, 3, D], F32)
    xr = sb("xr", [128, 2, D], F32)
    junk = sb("junk", [128, D], BF16); d_junk = Dep()
    hbf = sb("hbf", [128, 2, D], BF16)
    hT2 = sb("hT", [128, 2, KC, GA], BF16); d_hT2 = [Dep(), Dep()]
    cbuf = sb("cbuf", [128, KC, 30 + GA], BF16); d_c = [Dep() for _ in range(KC)]
    th = sb("th", [128, 2, GA], F32)
    szt = sb("szt", [128, KC, GA], BF16); d_sz = [Dep() for _ in range(KC)]
    cvo = sb("cvo", [128, KC, GA], F32); d_cvo = [Dep() for _ in range(KC)]
    cbq = sb("cbq", [128, 2, 2, GA], BF16)
    stt = sb("stt", [128, 3, GA], F32); d_stt = Dep()
    lnt = sb("lnt", [128, 1, 3, GA], F32)
    gat = sb("gat", [128, KC, GA], BF16); d_gat = Dep()
    yo = sb("yo", [128, 2, D], F32)
    smalls = sb("smalls", [128, 4, 16], F32)

    xt_r = Ring(T, 3, lambda i: xt[:, i, :], dma=True)
    xr_r = Ring(T, 2, lambda i: xr[:, i, :], dma=True)
    hbf_r = Ring(T, 2, lambda i: hbf[:, i, :])
    th_r = Ring(T, 2, lambda i: th[:, i, :])
    cbq_r = Ring(T, 2, lambda i: cbq[:, i, :, :])
    ln_r = Ring(T, 1, lambda i: lnt[:, i, :, :])
    yo_r = Ring(T, 2, lambda i: yo[:, i, :], dma=True)
    stage_slots = ring_slots(yo_r, xr_r, xt_r)
    sm_r = Ring(T, 4, lambda i: smalls[:, i, :])

    PS = [(st.enter_context(nc.psum_tensor("psA%d" % i, [128, 512], F32)), Dep(psum=True)) for i in range(8)]
    (pTt, d_pT), (pAB0, d_AB0), (pAB1, d_AB1), (pZ, d_Z), (pCV0, d_CV0), (pCV1, d_CV1), (pST, d_ST), (pY, d_Y) = PS
    pT = pTt[:].bitcast(BF16)
    AB = [(pAB0, d_AB0), (pAB1, d_AB1)]
    CV = [(pCV0, d_CV0), (pCV1, d_CV1)]

    s_c1, s_c2, s_c3 = T.dma_sem(), T.dma_sem(), T.dma_sem()
    T.dma("sp", lambda e: e.dma_start(out=colv[:], in_=dr["colv"]), s_c1, writes=[d_colv])
    T.dma("sp", lambda e: e.dma_start(out=cst[:], in_=dr["cst"]), s_c2, writes=[d_cst])
    T.dma("sp", lambda e: e.dma_start(out=gpost[:], in_=dr["rowv"][0].partition_broadcast(128)), s_c3, writes=[d_gp])
    T.op("dve", lambda e: e.tensor_copy(out=identb[:], in_=cst[:, 0:128]), reads=[d_cst], writes=[d_id])
    T.op("dve", lambda e: e.memset(onesb[:], 1.0), writes=[d_ones])
    T.op("pool", lambda e: e.memset(nhalf[:], -0.5), writes=[d_nh])
    T.op("dve", lambda e: e.tensor_scalar(out=hv[:, 0:8], in0=colv[:, 0:8], scalar1=0.5, scalar2=None, op0=ALU.mult),
         reads=[d_colv], writes=[d_hv])
    T.op("dve", lambda e: e.tensor_scalar(out=hv[:, 8:24], in0=colv[:, 32:48], scalar1=0.5, scalar2=None, op0=ALU.mult),
         reads=[d_colv], writes=[d_hv])

    def scaleA(k, cb):
        if cb == 1:
            return (colv[:, k:k + 1], d_colv)
        return (hv[:, k:k + 1], d_hv)
    load_cast_weight(T, dr["a_w_in"], 3 * D, wA, d_wA, stage_slots, scaleA)
    dctr = [0]

    def build_diag(ec):
        for k in range(KW):
            col = 48 + ec * KW + k
            dst = diag[:, ec * KW + k, :]
            if dctr[0] % 2 == 0:
                T.op("dve", lambda e, o=dst, c=col: e.tensor_scalar(out=o, in0=cst[:, 0:128], scalar1=colv[:, c:c + 1],
                                                                   scalar2=None, op0=ALU.mult),
                     reads=[d_cst, d_colv], writes=[d_diag[ec][0]])
            else:
                T.op("act", lambda e, o=dst, c=col: e.activation(out=o, in_=cst[:, 0:128], func=AF.Copy,
                                                                 scale=colv[:, c:c + 1]),
                     reads=[d_cst, d_colv], writes=[d_diag[ec][1]])
            dctr[0] += 1
    late_wO = weight_chunk_emitters(T, dr["a_w_out"], D, wO, d_wO, ring_slots(yo_r, xr_r), lambda k, cb: None)

    NG = NTOK // GA
    TPG = GA // 128
    xslots = {}

    def fetch_x(tile):
        si = xt_r.next()
        xslots[tile] = si
        T.dma("sp", lambda e, o=xt_r.ap(si), i=x_d[tile * 128:(tile + 1) * 128, :]: e.dma_start(out=o, in_=i),
              xt_r.sems[si], writes=[xt_r.deps[si]])

    fh_state = {}

    def front_head_stage(g, stage, tl):
        tile = g * TPG + tl
        if stage == 0:
            si = xslots.pop(tile)
            smi = sm_r.next()
            fh_state[(g, tl)] = (si, smi)
            norm_stage_a1(T, xt_r.ap(si), xt_r.deps[si], junk[:], d_junk, sm_r.ap(smi), sm_r.deps[smi], nhalf, d_nh)
        elif stage == 1:
            si, smi = fh_state[(g, tl)]
            hi = hbf_r.next()
            fh_state[(g, tl)] = (si, smi, hi)
            norm_stage_a2(T, xt_r.ap(si), xt_r.deps[si], sm_r.ap(smi), sm_r.deps[smi], hbf_r.ap(hi), hbf_r.deps[hi])
            if tl == TPG - 1 and g + 1 < NG:
                for t2 in range(TPG):
                    fetch_x((g + 1) * TPG + t2)
        else:
            si, smi, hi = fh_state.pop((g, tl))
            norm_stage_b(T, hbf_r.ap(hi), hbf_r.deps[hi], identb[:], d_id, pT, d_pT,
                         hT2[:, g % 2, :, tl * 128:(tl + 1) * 128], d_hT2[g % 2])

    def front_head(g):
        for stage in range(3):
            for tl in range(TPG):
                front_head_stage(g, stage, tl)

    def front_halo(g):
        first_in_seq = (g % (S // GA) == 0)
        for ec in range(KC):
            if first_in_seq:
                T.op("pool", lambda e, ec=ec: e.memset(cbuf[:, ec, 0:30], 0.0), writes=[d_c[ec]])
            else:
                T.op("pool", lambda e, ec=ec: e.tensor_copy(out=cbuf[:, ec, 0:30], in_=cbuf[:, ec, GA:GA + 30]),
                     reads=[d_c[ec]], writes=[d_c[ec]])

    def emit_ab(ec, g):
        hT, d_hT = hT2[:, g % 2, :, :], d_hT2[g % 2]
        pab, d_ab = AB[ec % 2]
        for part, coff in ((0, 0), (1, D)):
            for k in range(KC):
                T.op("pe", lambda e, k=k, part=part, coff=coff, pab=pab: e.matmul(
                    pab[:, part * GA:(part + 1) * GA], lhsT=wA[:, k, coff + ec * 128: coff + (ec + 1) * 128],
                    rhs=hT[:, k, :], start=(k == 0), stop=(k == KC - 1)),
                    reads=[*d_wA, d_hT], writes=[d_ab], inc=(part == 1 and k == KC - 1))
        ti = th_r.next()
        T.op("act", lambda e, pab=pab, ti=ti: e.activation(out=th_r.ap(ti), in_=pab[:, GA:2 * GA], func=AF.Tanh, scale=0.5),
             reads=[d_ab], writes=[th_r.deps[ti]])
        T.op("dve", lambda e, pab=pab, ti=ti: e.scalar_tensor_tensor(
            out=cbuf[:, ec, 30:30 + GA], in0=th_r.ap(ti), scalar=1.0, in1=pab[:, 0:GA], op0=ALU.add, op1=ALU.mult),
            reads=[th_r.deps[ti], d_ab], writes=[d_c[ec]])

    def emit_conv(ec):
        pcv, d_cv = CV[ec % 2]
        for k in range(KW):
            T.op("pe", lambda e, k=k, pcv=pcv: e.matmul(
                pcv[:, 0:GA], lhsT=diag[:, ec * KW + k, :], rhs=cbuf[:, ec, k:k + GA],
                start=(k == 0), stop=(k == KW - 1)),
                reads=[*d_diag[ec], d_c[ec]], writes=[d_cv], inc=(k == KW - 1))
        qi = cbq_r.next()
        q_ap = cbq_r.ap(qi)
        bcol = colv[:, 24 + ec:25 + ec]
        T.op("act", lambda e, pcv=pcv: e.activation(out=cvo[:, ec, :], in_=pcv[:, 0:GA], func=AF.Identity, bias=bcol),
             reads=[d_cv, d_colv], writes=[d_cvo[ec]])
        T.op("act", lambda e, pcv=pcv, q_ap=q_ap: e.activation(out=q_ap[:, 1, :], in_=pcv[:, 0:GA], func=AF.Square, bias=bcol),
             reads=[d_cv, d_colv], writes=[cbq_r.deps[qi]])
        T.op("dve", lambda e, q_ap=q_ap: e.tensor_copy(out=q_ap[:, 0, :], in_=cvo[:, ec, :]),
             reads=[d_cvo[ec]], writes=[cbq_r.deps[qi]])
        return qi

    def emit_stats(ec, qi):
        q_ap = cbq_r.ap(qi)
        T.op("pe", lambda e: e.matmul(pST[:, 0:2 * GA], lhsT=onesb[:], rhs=q_ap.rearrange("p a n -> p (a n)"),
                                      start=(ec == 0), stop=(ec == KC - 1)),
             reads=[d_ones, cbq_r.deps[qi]], writes=[d_ST], inc=True)

    def emit_z(ec, g):
        hT, d_hT = hT2[:, g % 2, :, :], d_hT2[g % 2]
        for k in range(KC):
            T.op("pe", lambda e, k=k, ec=ec: e.matmul(
                pZ[:, 0:GA], lhsT=wA[:, k, 2 * D + ec * 128: 2 * D + (ec + 1) * 128], rhs=hT[:, k, :],
                start=(k == 0), stop=(k == KC - 1)), reads=[*d_wA, d_hT], writes=[d_Z], inc=(k == KC - 1))
        ti = th_r.next()
        T.op("act", lambda e, ti=ti: e.activation(out=th_r.ap(ti), in_=pZ[:, 0:GA], func=AF.Tanh),
             reads=[d_Z], writes=[th_r.deps[ti]])
        T.op("dve", lambda e, ti=ti, ec=ec: e.scalar_tensor_tensor(
            out=szt[:, ec, :], in0=th_r.ap(ti), scalar=1.0, in1=pZ[:, 0:GA], op0=ALU.add, op1=ALU.mult),
            reads=[th_r.deps[ti], d_Z], writes=[d_sz[ec]])

    mu, msq, rs = stt[:, 0, :], stt[:, 1, :], stt[:, 2, :]

    def back_stats():
        T.op("dve", lambda e: e.tensor_scalar(out=mu, in0=pST[:, 0:GA], scalar1=1.0 / D, scalar2=None, op0=ALU.mult),
             reads=[d_ST], writes=[d_stt])
        T.op("dve", lambda e: e.tensor_tensor(out=msq, in0=mu, in1=mu, op=ALU.mult), reads=[d_stt], writes=[d_stt])
        T.op("dve", lambda e: e.scalar_tensor_tensor(out=msq, in0=pST[:, GA:2 * GA], scalar=1.0 / D, in1=msq,
                                                     op0=ALU.mult, op1=ALU.subtract),
             reads=[d_ST, d_stt], writes=[d_stt])
        T.op("dve", lambda e: e.tensor_scalar(out=msq, in0=msq, scalar1=EPS, scalar2=None, op0=ALU.add),
             reads=[d_stt], writes=[d_stt])
        T.op("act", lambda e: e.activation(out=msq, in_=msq, func=AF.Sqrt), reads=[d_stt], writes=[d_stt])
        T.op("dve", lambda e: e.reciprocal(out=rs, in_=msq), reads=[d_stt], writes=[d_stt])

    def back_ln(ec):
        li = ln_r.next()
        l_ap = ln_r.ap(li)
        d_l = ln_r.deps[li]
        d2, tt, up = l_ap[:, 0, :], l_ap[:, 1, :], l_ap[:, 2, :]
        T.op("pool", lambda e, ec=ec, d2=d2: e.tensor_tensor(out=d2, in0=cvo[:, ec, :], in1=mu, op=ALU.subtract),
             reads=[d_cvo[ec], d_stt], writes=[d_l])
        T.op("dve", lambda e, d2=d2: e.tensor_tensor(out=d2, in0=d2, in1=rs, op=ALU.mult),
             reads=[d_l, d_stt], writes=[d_l])
        T.op("act", lambda e, ec=ec, d2=d2, tt=tt: e.activation(out=tt, in_=d2, func=AF.Tanh, bias=hv[:, 16 + ec:17 + ec],
                                                                scale=hv[:, 8 + ec:9 + ec]),
             reads=[d_l, d_hv], writes=[d_l])
        T.op("dve", lambda e, ec=ec, d2=d2, up=up: e.tensor_scalar(out=up, in0=d2, scalar1=hv[:, 8 + ec:9 + ec],
                                                                   scalar2=hv[:, 16 + ec:17 + ec], op0=ALU.mult, op1=ALU.add),
             reads=[d_l, d_hv], writes=[d_l])
        T.op("dve", lambda e, tt=tt, up=up: e.scalar_tensor_tensor(out=up, in0=tt, scalar=1.0, in1=up, op0=ALU.add, op1=ALU.mult),
             reads=[d_l], writes=[d_l])
        T.op("dve", lambda e, ec=ec, up=up: e.tensor_tensor(out=gat[:, ec, :], in0=up, in1=szt[:, ec, :], op=ALU.mult),
             reads=[d_l, d_sz[ec]], writes=[d_gat])

    def back_out(g, xr_slots):
        for tl in range(TPG):
            tile = g * TPG + tl
            ri = xr_slots[tl]
            yi = yo_r.next()
            smi = sm_r.next()
            for half in range(2):
                pb, d_pb = (pY, d_Y) if half == 0 else (pZ, d_Z)
                emit_y_half(T, half, pb, d_pb, wO, d_wO, lambda k, tl=tl: gat[:, k, tl * 128:(tl + 1) * 128], [d_gat],
                            yo_r.ap(yi), yo_r.deps[yi], junk[:], d_junk, sm_r.ap(smi), sm_r.deps[smi])
            emit_post(T, yo_r.ap(yi), yo_r.deps[yi], sm_r.ap(smi), sm_r.deps[smi], nhalf, d_nh, gpost[:], d_gp,
                      xr_r.ap(ri), xr_r.deps[ri], yo_r.ap(yi), yo_r.deps[yi])
            T.dma("sp", lambda e, o=x1_d[tile * 128:(tile + 1) * 128, :], i=yo_r.ap(yi): e.dma_start(out=o, in_=i),
                  yo_r.sems[yi], reads=[yo_r.deps[yi]], writes=[x1_deps[tile]])

    def fetch_xr(g):
        slots = []
        for tl in range(TPG):
            tile = g * TPG + tl
            ri = xr_r.next()
            slots.append(ri)
            T.dma("sp", lambda e, o=xr_r.ap(ri), i=x_d[tile * 128:(tile + 1) * 128, :]: e.dma_start(out=o, in_=i),
                  xr_r.sems[ri], writes=[xr_r.deps[ri]])
        return slots

    for tl in range(TPG):
        fetch_x(tl)
    front_head(0)
    for gi in range(NG + 1):
        fg = gi if gi < NG else None
        bg = gi - 1 if gi >= 1 else None
        if fg == 0:
            front_halo(fg)
            emit_ab(0, fg)
        if bg is not None:
            xr_slots = fetch_xr(bg)
        pend = []
        for ec in range(KC):
            if fg is not None and ec + 1 < KC:
                emit_ab(ec + 1, fg)
            if bg is not None:
                back_ln(ec)
            if fg is not None:
                if gi == 0:
                    build_diag(ec)
                qi = emit_conv(ec)
                emit_z(ec, fg)
                if gi == 0 and late_wO:
                    late_wO.pop(0)()
                if pend:
                    emit_stats(*pend.pop(0))
                pend.append((ec, qi))
                if fg + 1 < NG:
                    sched = {1: [(0, 0), (0, 1)], 3: [(1, 0)], 4: [(1, 1)], 5: [(2, 0)], 6: [(2, 1)]}
                    for stage, tl in sched.get(ec, []):
                        front_head_stage(fg + 1, stage, tl)
        if fg is not None:
            emit_stats(*pend.pop(0))
            back_stats()
            if fg + 1 < NG:
                front_halo(fg + 1)
                emit_ab(0, fg + 1)
        if bg is not None:
            back_out(bg, xr_slots)


def phase_b(nc, T, st, dr, x1_deps, out_deps):
    sb = lambda name, shape, dt: st.enter_context(nc.sbuf_tensor("sc_" + name, shape, dt))
    x1_d, out_d = dr["x1"], dr["out"]
    QW = 256
    NJ = S // QW
    NT = S // 128

    colv = sb("colv", [128, NCOLV], F32); d_colv = Dep()
    cst = sb("cst", [128, NCST], F32); d_cst = Dep()
    identb = sb("identb", [128, 128], BF16); d_id = Dep()
    nhalf = sb("nhalf", [128, 8], F32); d_nh = Dep()
    mneg = sb("mneg", [128, 2, 128], BF16); d_mneg = Dep()
    hv = sb("hv", [128, 8], F32); d_hv = Dep()
    gsc = sb("gsc", [128, 2], F32); d_gsc = Dep()
    gpost = sb("gpost", [128, D], F32); d_gp = Dep()
    lamt = sb("lamt", [128, 256], F32); d_lam = Dep()
    lams = sb("lams", [128, 8], F32); d_lams = Dep()
    KT = sb("KT", [128, NH, S], BF16); d_KT = [Dep() for _ in range(NH)]
    Va = sb("Va", [128, NT * NH, 130], BF16); d_V = Dep()
    wIN = sb("wIN", [128, KC, 2 * D], BF16); d_wIN = (Dep(), Dep())
    wOB = sb("wOB", [128, KC, D], BF16); d_wOB = (Dep(), Dep())
    R = sb("R", [128, 18432], BF16)
    xt = sb("xt", [128, 2, D], F32)
    xr = sb("xr", [128, 2, D], F32)
    junk = sb("junk", [128, D], BF16); d_junk = Dep()
    hbf = sb("hbf", [128, 1, D], BF16)
    hT2 = sb("hT", [128, 2, KC, QW], BF16); d_hT2 = [Dep(), Dep()]
    QT2 = sb("QT", [128, 2, NH, 2 * QW], BF16); d_QT2 = [Dep(), Dep()]
    hT, d_hT = hT2[:, 0, :, :], d_hT2[0]
    tz = sb("tz", [128, 1, 512], F32)
    smalls = sb("smalls", [128, 4, 16], F32)
    osm = sb("osm", [128, 4, 8], F32)

    wKV = R[:, 0:16384].rearrange("p (k n) -> p k n", k=KC); d_wKV = (Dep(), Dep())
    gz2 = R[:, 0:4096].rearrange("p (b t n) -> p b t n", b=2, t=2); d_gz2 = [Dep(), Dep()]
    yo = R[:, 4096:8192].bitcast(F32).rearrange("p (t n) -> p t n", t=2)
    gated2 = R[:, 8192:12288].rearrange("p (b t n) -> p b t n", b=2, t=2); d_gated2 = [[Dep(), Dep()], [Dep(), Dep()]]
    gT = R[:, 12288:14336].rearrange("p (s k n) -> p s k n", s=2, k=KC)
    et = R[:, 14336:16384].rearrange("p (s c n) -> p s c n", s=4, c=2)
    otmp = R[:, 16384:18432].bitcast(F32).rearrange("p (s a n) -> p s a n", s=2, a=4)

    xt_r = Ring(T, 2, lambda i: xt[:, i, :], dma=True)
    xr_r = Ring(T, 2, lambda i: xr[:, i, :], dma=True)
    hbf_r = Ring(T, 1, lambda i: hbf[:, i, :])
    tz_r = Ring(T, 1, lambda i: tz[:, i, :])
    sm_r = Ring(T, 4, lambda i: smalls[:, i, :])
    osm_r = Ring(T, 4, lambda i: osm[:, i, :])
    yo_r = Ring(T, 2, lambda i: yo[:, i, :], dma=True)
    gT_r = Ring(T, 2, lambda i: gT[:, i, :, :])
    et_r = Ring(T, 4, lambda i: et[:, i, :, :])
    ot_r = Ring(T, 2, lambda i: otmp[:, i, :, :])

    pS = [st.enter_context(nc.psum_tensor("psS%d" % i, [128, 2, QW], F32)) for i in range(3)]
    d_S = [Dep(psum=True), Dep(psum=True), Dep(psum=True)]
    pOs = [st.enter_context(nc.psum_tensor("psO%d" % i, [128, 512], F32)) for i in range(2)]
    d_Os = [Dep(psum=True) for _ in range(2)]
    pYs = [st.enter_context(nc.psum_tensor("psY%d" % i, [128, 512], F32)) for i in range(3)]
    d_Ys = [Dep(psum=True) for _ in range(3)]
    pT, d_pT = pYs[0][:].bitcast(BF16), d_Ys[0]
    yb_i = [0]
    o_ctr = [0]

    y_flush = [None]

    def next_y():
        yb_i[0] = (yb_i[0] + 1) % 3
        if y_flush[0] is not None and yb_i[0] in y_pending:
            y_flush[0]()
        return pYs[yb_i[0]], d_Ys[yb_i[0]]
    y_pending = set()
    pY, d_Y = pYs[0], d_Ys[0]
    GB = [(pS[0][:].rearrange("p c n -> p (c n)"), d_S[0]), (pS[1][:].rearrange("p c n -> p (c n)"), d_S[1]),
          (pS[2][:].rearrange("p c n -> p (c n)"), d_S[2]),
          (pOs[0][:], d_Os[0]), (pOs[1][:], d_Os[1]), (pYs[1][:], d_Ys[1]), (pYs[2][:], d_Ys[2])]
    gb_i = [0]

    def next_bank():
        gb_i[0] = (gb_i[0] + 1) % len(GB)
        return GB[gb_i[0]]

    sems = [T.dma_sem() for _ in range(5)]
    T.dma("sp", lambda e: e.dma_start(out=colv[:], in_=dr["colv"]), sems[0], writes=[d_colv])
    T.dma("sp", lambda e: e.dma_start(out=cst[:], in_=dr["cst"]), sems[1], writes=[d_cst])
    T.dma("sp", lambda e: e.dma_start(out=gpost[:], in_=dr["rowv"][1].partition_broadcast(128)), sems[2], writes=[d_gp])
    T.dma("sp", lambda e: e.dma_start(out=lamt[:], in_=dr["lamv"].partition_broadcast(128)), sems[4], writes=[d_lam])
    T.op("dve", lambda e: e.tensor_copy(out=identb[:], in_=cst[:, 0:128]), reads=[d_cst], writes=[d_id])
    T.op("pool", lambda e: e.memset(nhalf[:], -0.5), writes=[d_nh])
    T.op("pool", lambda e: e.memset(Va[:, :, 128:130], 1.0), writes=[d_V])
    T.op("pool", lambda e: e.memset(mneg[:], 0.0), writes=[d_mneg])
    T.op("pool", lambda e: e.affine_select(out=mneg[:], in_=mneg[:], pattern=[[0, 2], [1, 128]], compare_op=ALU.is_ge,
                                           fill=-30000.0, base=0, channel_multiplier=-1), reads=[d_mneg], writes=[d_mneg])
    T.op("dve", lambda e: e.tensor_scalar(out=hv[:, 0:8], in0=colv[:, 16:24], scalar1=0.5, scalar2=None, op0=ALU.mult),
         reads=[d_colv], writes=[d_hv])
    T.op("dve", lambda e: e.tensor_scalar(out=gsc[:, 0:1], in0=colv[:, NCOLV - 1:NCOLV], scalar1=1.0 - LAM_INIT, scalar2=None,
                                          op0=ALU.mult), reads=[d_colv], writes=[d_gsc])
    T.op("dve", lambda e: e.scalar_tensor_tensor(out=junk[:, 0:64], in0=lamt[:, 0:64], scalar=1.0, in1=lamt[:, 64:128],
                                                 op0=ALU.mult, op1=ALU.mult, accum_out=lams[:, 0:1]),
         reads=[d_lam], writes=[d_junk, d_lams])
    T.op("dve", lambda e: e.scalar_tensor_tensor(out=junk[:, 0:64], in0=lamt[:, 128:192], scalar=1.0, in1=lamt[:, 192:256],
                                                 op0=ALU.mult, op1=ALU.mult, accum_out=lams[:, 1:2]),
         reads=[d_lam], writes=[d_junk, d_lams])
    T.op("act", lambda e: e.activation(out=lams[:, 2:4], in_=lams[:, 0:2], func=AF.Exp), reads=[d_lams], writes=[d_lams])
    T.op("dve", lambda e: e.scalar_tensor_tensor(out=lams[:, 4:5], in0=lams[:, 3:4], scalar=-LAM_INIT, in1=lams[:, 2:3],
                                                 op0=ALU.add, op1=ALU.subtract), reads=[d_lams], writes=[d_lams])
    nlam = lams[:, 4:5]

    def scaleIN(k, cb):
        if cb == 0:
            return (colv[:, 16 + k:17 + k], d_colv)
        return (hv[:, k:k + 1], d_hv)
    stage_slots = ring_slots(xt_r, xr_r)
    qst = QT2[:].rearrange("p b h n -> p (b h n)").bitcast(F32).rearrange("p (s n) -> p s n", s=4)
    qst_slots = [(qst[:, i_, :], Dep(), T.dma_sem()) for i_ in range(4)]
    late_w = (weight_chunk_emitters(T, dr["b_w_in"], 2 * D, wIN, d_wIN, qst_slots, scaleIN, dve_only=True)
              + weight_chunk_emitters(T, dr["b_w_out"], D, wOB, d_wOB, qst_slots, lambda k, cb: (gsc[:, 0:1], d_gsc),
                                      dve_only=True))

    wkv_sems = [T.dma_sem() for _ in range(KC)]
    d_wkvbf = [Dep() for _ in range(KC)]
    xslots = {}

    def fetch_x1(gtile):
        si = xt_r.next()
        xslots[gtile] = si
        T.dma("sp", lambda e, o=xt_r.ap(si), i=x1_d[gtile * 128:(gtile + 1) * 128, :]: e.dma_start(out=o, in_=i),
              xt_r.sems[si], reads=[x1_deps[gtile]], writes=[xt_r.deps[si]])

    n_evac = [0]

    def evac_copy(out_ap, in_ap, rd, wr, scale=None):
        n_evac[0] += 1
        if False:
            if scale is None:
                T.op("act", lambda e: e.activation(out=out_ap, in_=in_ap, func=AF.Copy), reads=rd, writes=wr)
            else:
                T.op("act", lambda e: e.activation(out=out_ap, in_=in_ap, func=AF.Copy, scale=scale), reads=rd, writes=wr)
        else:
            if scale is None:
                T.op("dve", lambda e: e.tensor_copy(out=out_ap, in_=in_ap), reads=rd, writes=wr)
            else:
                T.op("dve", lambda e: e.tensor_scalar(out=out_ap, in0=in_ap, scalar1=scale, scalar2=None, op0=ALU.mult),
                     reads=rd, writes=wr)

    for s in range(NSEQ):
        tbase = s * NT
        T.barrier()
        if s == 0:
            load_cast_weight(T, dr["w_kv"], 2 * D, wKV, d_wKV, stage_slots, lambda k, cb: (colv[:, 8 + k:9 + k], d_colv))
            for k in range(KC):
                T.dma("sp", lambda e, k=k: e.dma_start(out=dr["wkvbf"][:, k, :], in_=wKV[:, k, :]), wkv_sems[k],
                      reads=[*d_wKV], writes=[d_wkvbf[k]])
        else:
            for k in range(KC):
                T.dma("sp", lambda e, k=k: e.dma_start(out=wKV[:, k, :], in_=dr["wkvbf"][:, k, :]), wkv_sems[k],
                      reads=[d_wkvbf[k]], writes=[d_wKV[k % 2]])
        kv_slots = ring_slots(xt_r, xr_r)
        kv_map = {}
        kv_ctr = [0]

        def kv_fetch(gtile):
            ap_, dep_, sem_ = kv_slots[kv_ctr[0] % len(kv_slots)]
            kv_ctr[0] += 1
            kv_map[gtile] = (ap_, dep_)
            T.dma("sp", lambda e, o=ap_, i=x1_d[gtile * 128:(gtile + 1) * 128, :]: e.dma_start(out=o, in_=i),
                  sem_, reads=[x1_deps[gtile]], writes=[dep_])

        kvn = {}

        def kv_norm_stage(g, stage, tl):
            gtile = tbase + 2 * g + tl
            if stage == 0:
                ap_, dep_ = kv_map.pop(gtile)
                smi = sm_r.next()
                kvn[(g, tl)] = (ap_, dep_, smi)
                norm_stage_a1(T, ap_, dep_, junk[:], d_junk, sm_r.ap(smi), sm_r.deps[smi], nhalf, d_nh)
            elif stage == 1:
                ap_, dep_, smi = kvn.pop((g, tl))
                hi = hbf_r.next()
                kvn[(g, tl)] = hi
                norm_stage_a2(T, ap_, dep_, sm_r.ap(smi), sm_r.deps[smi], hbf_r.ap(hi), hbf_r.deps[hi])
                if g + 2 < NJ:
                    kv_fetch(tbase + 2 * (g + 2) + tl)
            else:
                hi = kvn.pop((g, tl))
                norm_stage_b(T, hbf_r.ap(hi), hbf_r.deps[hi], identb[:], d_id, pT, d_pT,
                             hT2[:, g % 2, :, tl * 128:(tl + 1) * 128], d_hT2[g % 2])

        def kv_norm(g):
            for tl in range(2):
                kv_norm_stage(g, 0, tl)
            for tl in range(2):
                kv_norm_stage(g, 1, tl)
                kv_norm_stage(g, 2, tl)

        for t_ in range(4):
            kv_fetch(tbase + t_)
        kv_norm(0)
        for g in range(NJ):
            if g + 1 < NJ:
                kv_norm_stage(g + 1, 0, 0)
                kv_norm_stage(g + 1, 0, 1)
            hTg, d_hTg = hT2[:, g % 2, :, :], d_hT2[g % 2]
            ksched = {1: [(1, 0)], 4: [(2, 0)], 5: [(1, 1)], 9: [(2, 1)]}
            unit = 0
            for h in range(NH):
                if s == 0 and h in (1, 4, 7) and late_w:
                    late_w.pop(0)()
                pb, d_pb = next_bank()
                for k in range(KC):
                    T.op("pe", lambda e, k=k, h=h, pb=pb, hTg=hTg: e.matmul(
                        pb[:, 0:QW], lhsT=wKV[:, k, h * 128:(h + 1) * 128],
                        rhs=hTg[:, k, :], start=(k == 0), stop=(k == KC - 1)),
                        reads=[*d_wKV, d_hTg], writes=[d_pb], inc=(k == KC - 1))
                evac_copy(KT[:, h, g * QW:(g + 1) * QW], pb[:, 0:QW], [d_pb], [d_KT[h]])
                if g + 1 < NJ:
                    for st_, tl_ in ksched.get(unit, []):
                        kv_norm_stage(g + 1, st_, tl_)
                unit += 1
            for tl in range(2):
                t_in = 2 * g + tl
                for half in range(2):
                    pb, d_pb = next_bank()
                    for k in range(KC):
                        T.op("pe", lambda e, k=k, tl=tl, half=half, pb=pb, hTg=hTg: e.matmul(
                            pb[:, 0:512], lhsT=hTg[:, k, tl * 128:(tl + 1) * 128],
                            rhs=wKV[:, k, D + half * 512: D + (half + 1) * 512], start=(k == 0), stop=(k == KC - 1)),
                            reads=[*d_wKV, d_hTg], writes=[d_pb], inc=(k == KC - 1))
                    evac_copy(Va[:, t_in * NH + 4 * half: t_in * NH + 4 * half + 4, 0:128],
                              pb[:, 0:512].rearrange("p (h e) -> p h e", h=4), [d_pb], [d_V])
                    if g + 1 < NJ:
                        for st_, tl_ in ksched.get(unit, []):
                            kv_norm_stage(g + 1, st_, tl_)
                    unit += 1

        if s == 0:
            while late_w:
                late_w.pop(0)()
        T.barrier()
        if s == 0:
            for b_ in range(2):
                T.op("pool", lambda e, b_=b_: e.memset(QT2[:, b_, :, :], 0.0), writes=[d_QT2[b_]])

        pn_state = {}
        act_q = []
        cur_step = [None]

        def defer_act(fn, delay=2):
            if cur_step[0] is None:
                fn()
            else:
                act_q.append((cur_step[0] + delay, fn))

        def run_deferred(upto=None):
            while act_q and (upto is None or act_q[0][0] <= upto):
                act_q.pop(0)[1]()
            if not act_q:
                y_pending.clear()
        y_flush[0] = run_deferred

        def pro_norm_a1(j, tl):
            def f():
                gtile = tbase + 2 * j + tl
                si = xslots.pop(gtile)
                smi = sm_r.next()
                pn_state[(j, tl)] = (si, smi)
                norm_stage_a1(T, xt_r.ap(si), xt_r.deps[si], junk[:], d_junk, sm_r.ap(smi), sm_r.deps[smi], nhalf, d_nh)
            return f

        def pro_norm_a2(j, tl):
            def f():
                si, smi = pn_state[(j, tl)]
                hi = hbf_r.next()
                pn_state[(j, tl)] = hi
                norm_stage_a2(T, xt_r.ap(si), xt_r.deps[si], sm_r.ap(smi), sm_r.deps[smi], hbf_r.ap(hi), hbf_r.deps[hi])
                if tl == 1 and j + 1 < NJ:
                    fetch_x1(tbase + 2 * (j + 1))
                    fetch_x1(tbase + 2 * (j + 1) + 1)
            return f

        def pro_norm_b(j, tl):
            def f():
                hi = pn_state.pop((j, tl))
                pYb, d_Yb = next_y()
                norm_stage_b(T, hbf_r.ap(hi), hbf_r.deps[hi], identb[:], d_id, pYb[:].bitcast(BF16), d_Yb,
                             hT2[:, j % 2, :, tl * 128:(tl + 1) * 128], d_hT2[j % 2])
            return f

        def pro_q(j, h):
            def f():
                hTj, d_hTj = hT2[:, j % 2, :, :], d_hT2[j % 2]
                pY, d_Y = next_y()
                for k in range(KC):
                    T.op("pe", lambda e, k=k: e.matmul(pY[:, 0:QW], lhsT=wIN[:, k, h * 128:(h + 1) * 128],
                                                       rhs=hTj[:, k, :], start=(k == 0), stop=(k == KC - 1)),
                         reads=[*d_wIN, d_hTj], writes=[d_Y], inc=(k == KC - 1))
                for c in range(2):
                    T.op("dve", lambda e, c=c: e.tensor_scalar(
                        out=QT2[c * 64:(c + 1) * 64, j % 2, h, c * QW:(c + 1) * QW], in0=pY[c * 64:(c + 1) * 64, 0:QW],
                        scalar1=0.125, scalar2=None, op0=ALU.mult), reads=[d_Y], writes=[d_QT2[j % 2]])
            return f

        def pro_z(j, tl, half):
            def f():
                hTj, d_hTj = hT2[:, j % 2, :, :], d_hT2[j % 2]
                pY, d_Y = next_y()
                for k in range(KC):
                    T.op("pe", lambda e, k=k: e.matmul(
                        pY[:, 0:512], lhsT=hTj[:, k, tl * 128:(tl + 1) * 128],
                        rhs=wIN[:, k, D + half * 512: D + (half + 1) * 512], start=(k == 0), stop=(k == KC - 1)),
                        reads=[*d_wIN, d_hTj], writes=[d_Y], inc=(k == KC - 1))

                def post():
                    ti = tz_r.next()
                    gz_ap = gz2[:, j % 2, tl, half * 512:(half + 1) * 512]
                    T.op("act", lambda e: e.activation(out=tz_r.ap(ti), in_=pY[:, 0:512], func=AF.Tanh),
                         reads=[d_Y], writes=[tz_r.deps[ti]])
                    T.op("dve", lambda e: e.scalar_tensor_tensor(
                        out=gz_ap, in0=tz_r.ap(ti), scalar=1.0, in1=pY[:, 0:512], op0=ALU.add, op1=ALU.mult),
                        reads=[tz_r.deps[ti], d_Y], writes=[d_gz2[j % 2]])
                if cur_step[0] is not None:
                    y_pending.add(yb_i[0])
                defer_act(post)
            return f

        def prologue_items(j):
            return ([pro_norm_a1(j, 0), pro_norm_a1(j, 1), pro_norm_a2(j, 0), pro_norm_b(j, 0), pro_norm_a2(j, 1), pro_norm_b(j, 1)]
                    + [pro_q(j, h) for h in range(NH)]
                    + [pro_z(j, tl, half) for tl in range(2) for half in range(2)])

        def interleave(la, lb):
            out = []
            for k_ in range(max(len(la), len(lb))):
                if k_ < len(lb):
                    out.append(lb[k_])
                if k_ < len(la):
                    out.append(la[k_])
            return out

        ep_state = {}

        xr_state = {}

        def ep_reload(j):
            def f():
                for i in range(2):
                    gtile = tbase + 2 * j + i
                    ri = xr_r.next()
                    xr_state[(j, i)] = ri
                    T.dma("sp", lambda e, o=xr_r.ap(ri), i_=x1_d[gtile * 128:(gtile + 1) * 128, :]: e.dma_start(out=o, in_=i_),
                          xr_r.sems[ri], reads=[x1_deps[gtile]], writes=[xr_r.deps[ri]])
            return f

        def ep_tr(j, i):
            def f():
                gtile = tbase + 2 * j + i
                ri = xr_state.pop((j, i))
                gi = gT_r.next()
                g_ap, d_g = gT_r.ap(gi), gT_r.deps[gi]
                pYb, d_pT = next_y()
                pT = pYb[:].bitcast(BF16)
                for k in range(KC):
                    T.op("pe", lambda e, k=k: e.transpose(out=pT[:, k * 128:(k + 1) * 128],
                                                          in_=gated2[:, j % 2, i, k * 128:(k + 1) * 128], identity=identb[:]),
                         reads=[d_gated2[j % 2][i], d_id], writes=[d_pT], inc=(k == KC - 1))
                T.op("dve", lambda e: e.tensor_copy(out=g_ap, in_=pT.rearrange("p (k n) -> p k n", k=KC)),
                     reads=[d_pT], writes=[d_g])
                ep_state[(j, i)] = (ri, g_ap, d_g, yo_r.next(), sm_r.next())
            return f

        def ep_y(j, i, half):
            def f():
                ri, g_ap, d_g, yi, smi = ep_state[(j, i)]
                pY, d_Y = next_y()
                for k in range(KC):
                    T.op("pe", lambda e, k=k: e.matmul(
                        pY[:, 0:512], lhsT=g_ap[:, k, :], rhs=wOB[:, k, half * 512:(half + 1) * 512],
                        start=(k == 0), stop=(k == KC - 1)),
                        reads=[d_g, *d_wOB], writes=[d_Y], inc=(k == KC - 1))

                def post():
                    small = sm_r.ap(smi)
                    T.op("act", lambda e: e.activation(out=yo_r.ap(yi)[:, half * 512:(half + 1) * 512], in_=pY[:, 0:512],
                                                       func=AF.Copy), reads=[d_Y], writes=[yo_r.deps[yi]])
                    T.op("act", lambda e: e.activation(out=junk[:, 0:512], in_=pY[:, 0:512], func=AF.Square,
                                                       accum_out=small[:, 4 + half:5 + half]),
                         reads=[d_Y], writes=[d_junk, sm_r.deps[smi]])
                if cur_step[0] is not None:
                    y_pending.add(yb_i[0])
                defer_act(post)
            return f

        def ep_post_a(j, i):
            def f():
                ri, g_ap, d_g, yi, smi = ep_state[(j, i)]
                run_deferred()
                emit_post_a(T, sm_r.ap(smi), sm_r.deps[smi], nhalf, d_nh)
            return f

        def ep_post_b(j, i):
            def f():
                gtile = tbase + 2 * j + i
                ri, g_ap, d_g, yi, smi = ep_state.pop((j, i))
                emit_post_b(T, yo_r.ap(yi), yo_r.deps[yi], sm_r.ap(smi), sm_r.deps[smi], gpost[:], d_gp,
                            xr_r.ap(ri), xr_r.deps[ri], yo_r.ap(yi), yo_r.deps[yi])
                T.dma("sp", lambda e, o=out_d[gtile * 128:(gtile + 1) * 128, :], i_=yo_r.ap(yi): e.dma_start(out=o, in_=i_),
                      yo_r.sems[yi], reads=[yo_r.deps[yi]], writes=[out_deps[gtile]])
            return f

        def epilogue_items(j):
            out = [ep_reload(j)]
            for i in range(2):
                out += [ep_tr(j, i), ep_y(j, i, 0), ep_y(j, i, 1), ep_post_a(j, i), ep_post_b(j, i)]
            return out

        fetch_x1(tbase + 0)
        fetch_x1(tbase + 1)
        for it in prologue_items(0):
            it()

        for j in range(NJ):
            QT, d_QT = QT2[:, j % 2, :, :], d_QT2[j % 2]
            gz, d_gz = gz2[:, j % 2, :, :], d_gz2[j % 2]
            gated, d_gated = gated2[:, j % 2, :, :], d_gated2[j % 2]
            side = interleave(epilogue_items(j - 1) if j >= 1 else [],
                              prologue_items(j + 1) if j + 1 < NJ else [])
            nkb = 2 * j + 2
            steps = [(h, kb) for h in range(NH) for kb in range(nkb)]
            st_e = {}
            obank = {}
            for h_ in range(NH):
                obank[h_] = (0, 1)
                o_ctr[0] += 1

            def emit_S(n):
                h, kb = steps[n]
                pSb = pS[n % 3]
                diag_i = [i for i in range(2) if kb == 2 * j + i]
                if kb == 2 * j + 1:
                    T.op("pe", lambda e, kb=kb, pSb=pSb, h=h, QT=QT, last=(not diag_i): e.matmul(
                        pSb[:, :, 128:QW], lhsT=KT[:, h, kb * 128:(kb + 1) * 128],
                        rhs=QT[:, h, :].rearrange("p (c n) -> p c n", c=2)[:, :, 128:QW], start=True, stop=last),
                        reads=[d_KT[h], d_QT], writes=[d_S[n % 3]], inc=(not diag_i))
                else:
                    T.op("pe", lambda e, kb=kb, pSb=pSb, h=h, QT=QT, last=(not diag_i): e.matmul(
                        pSb[:].rearrange("p c n -> p (c n)"), lhsT=KT[:, h, kb * 128:(kb + 1) * 128],
                        rhs=QT[:, h, :], start=True, stop=last),
                        reads=[d_KT[h], d_QT], writes=[d_S[n % 3]], inc=(not diag_i))
                for i in diag_i:
                    T.op("pe", lambda e, i=i, pSb=pSb: e.matmul(
                        pSb[:, :, i * 128:(i + 1) * 128], lhsT=identb[:], rhs=mneg[:], start=False, stop=True),
                        reads=[d_id, d_mneg], writes=[d_S[n % 3]], inc=True)

            def emit_exp(n):
                h, kb = steps[n]
                c0 = 128 if kb == 2 * j + 1 else 0
                pSb = pS[n % 3]
                dS0 = dS1 = d_S[n % 3]
                ei = et_r.next()
                e_ap = et_r.ap(ei)
                d_e = et_r.deps[ei]
                st_e[n] = (e_ap, d_e)
                if h == 0:
                    for i in range(2):
                        if 128 * i < c0:
                            continue
                        col = 128 + h * 17 + (kb - 2 * j - i) + 15
                        T.op("act", lambda e, i=i, col=col, pSb=pSb, e_ap=e_ap: e.activation(
                            out=e_ap[:, :, i * 128:(i + 1) * 128], in_=pSb[:, :, i * 128:(i + 1) * 128], func=AF.Exp,
                            bias=cst[:, col:col + 1], scale=1.0), reads=[dS0, dS1, d_cst], writes=[d_e])
                else:
                    col = 128 + h * 17 + (kb - 2 * j) + 15
                    T.op("act", lambda e, col=col, c0=c0, pSb=pSb, e_ap=e_ap: e.activation(
                        out=e_ap[:, :, c0:QW], in_=pSb[:, :, c0:QW], func=AF.Exp,
                        bias=cst[:, col:col + 1], scale=1.0), reads=[dS0, dS1, d_cst], writes=[d_e])

            def emit_PV(n):
                h, kb = steps[n]
                e_ap, d_e = st_e.pop(n)
                pv = [(i, c) for i in range(2) if kb <= 2 * j + i for c in range(2)]
                for n_, (i, c) in enumerate(pv):
                    ob = obank[h][i]
                    T.op("pe", lambda e, i=i, c=c, kb=kb, h=h, e_ap=e_ap, ob=ob, last=(kb == 2 * j + i): e.matmul(
                        pOs[ob][:, c * 130:(c + 1) * 130], lhsT=e_ap[:, c, i * 128:(i + 1) * 128],
                        rhs=Va[:, kb * NH + h, 0:130], start=(kb == 0 and c == 0), stop=last,
                        skip_group_check=True),
                        reads=[d_e, d_V], writes=[d_Os[ob]], inc=(n_ == len(pv) - 1))

            fin = {}

            def emit_final_copy(h, i):
                oi = ot_r.next()
                o_ap = ot_r.ap(oi)
                d_o = ot_r.deps[oi]
                raw = o_ap[:, 0:3, :].rearrange("p a n -> p (a n)")
                qi = osm_r.next()
                q_ap = osm_r.ap(qi)
                d_q = osm_r.deps[qi]
                ob = obank[h][i]
                T.op("dve", lambda e, ob=ob, raw=raw: e.tensor_copy(out=raw[:, 0:260], in_=pOs[ob][:, 0:260]),
                     reads=[d_Os[ob]], writes=[d_o])
                fin[(h, i)] = (o_ap, d_o, raw, q_ap, d_q)

            fin2 = {}

            def emit_final_rest1(h, i):
                o_ap, d_o, raw, q_ap, d_q = fin.pop((h, i))
                oo, sq = o_ap[:, 3, :], o_ap[:, 0, :]
                T.op("dve", lambda e, raw=raw, q_ap=q_ap: e.reciprocal(
                    out=q_ap[:, 0:2], in_=raw[:, 0:260].rearrange("p (c n) -> p c n", c=2)[:, :, 128]),
                    reads=[d_o], writes=[d_q])
                T.op("dve", lambda e, q_ap=q_ap: e.tensor_tensor(out=q_ap[:, 5:6], in0=q_ap[:, 1:2], in1=nlam, op=ALU.mult),
                     reads=[d_q, d_lams], writes=[d_q])
                T.op("dve", lambda e, raw=raw, q_ap=q_ap: e.tensor_scalar(
                    out=raw[:, 0:128], in0=raw[:, 0:128], scalar1=q_ap[:, 0:1], scalar2=None, op0=ALU.mult),
                    reads=[d_o, d_q], writes=[d_o])
                T.op("dve", lambda e, raw=raw, q_ap=q_ap, oo=oo: e.scalar_tensor_tensor(
                    out=oo, in0=raw[:, 130:258], scalar=q_ap[:, 5:6], in1=raw[:, 0:128], op0=ALU.mult, op1=ALU.add),
                    reads=[d_o, d_q], writes=[d_o])
                T.op("dve", lambda e, oo=oo, sq=sq, q_ap=q_ap: e.scalar_tensor_tensor(
                    out=sq, in0=oo, scalar=1.0, in1=oo, op0=ALU.mult, op1=ALU.mult, accum_out=q_ap[:, 2:3]),
                    reads=[d_o], writes=[d_o, d_q])
                emit_rstd(T, q_ap[:, 2:3], q_ap[:, 3:4], q_ap[:, 4:5], nhalf[:, 0:1], d_q, d_q, d_q, d_nh, 1.0 / 128)
                fin2[(h, i)] = (oo, d_o, q_ap, d_q)

            def emit_final_rest2(h, i):
                oo, d_o, q_ap, d_q = fin2.pop((h, i))
                T.op("dve", lambda e, i=i, h=h, oo=oo, q_ap=q_ap, gated=gated, gz=gz: e.scalar_tensor_tensor(
                    out=gated[:, i, h * 128:(h + 1) * 128], in0=oo, scalar=q_ap[:, 4:5],
                    in1=gz[:, i, h * 128:(h + 1) * 128], op0=ALU.mult, op1=ALU.mult),
                    reads=[d_o, d_q, d_gz], writes=[d_gated[i]])

            nsteps = len(steps)
            nside = len(side)
            emit_S(0)
            if nsteps > 1:
                emit_S(1)
            done_side = 0

            def after_PV(n):
                h, kb = steps[n]
                for i in range(2):
                    if kb == 2 * j + i:
                        emit_final_copy(h, i)
                if kb == nkb - 1:
                    emit_final_rest1(h, 0)
                    emit_final_rest1(h, 1)
                    emit_final_rest2(h, 0)
                    emit_final_rest2(h, 1)

            for n in range(nsteps):
                cur_step[0] = n
                emit_exp(n)
                run_deferred(n)
                if n + 2 < nsteps:
                    emit_S(n + 2)
                if n >= 1:
                    emit_PV(n - 1)
                    after_PV(n - 1)
                want = min(nside, ((n + 1) * nside * 4) // (nsteps * 3))
                while done_side < want:
                    side[done_side]()
                    done_side += 1
            emit_PV(nsteps - 1)
            after_PV(nsteps - 1)
            cur_step[0] = None
            run_deferred()

        for it in epilogue_items(NJ - 1):
            it()

def build_program(mode="fused"):
    nc = bass.Bass("TRN2", target_bir_lowering=False)
    dr = {}
    if mode in ("A", "fused"):
        dr["x"] = nc.dram_tensor("x", [NTOK, D], F32, kind="ExternalInput").ap()
    for name, shape in (("a_w_in", [D, 3 * D]), ("a_w_out", [D, D]), ("w_kv", [D, 2 * D]),
                        ("b_w_in", [D, 2 * D]), ("b_w_out", [D, D]), ("colv", [128, NCOLV]),
                        ("rowv", [3, D]), ("lamv", [256]), ("cst", [128, NCST])):
        dr[name] = nc.dram_tensor(name, shape, F32, kind="ExternalInput").ap()
    if mode == "A":
        dr["x1"] = nc.dram_tensor("x1", [NTOK, D], F32, kind="ExternalOutput").ap()
    elif mode == "B":
        dr["x1"] = nc.dram_tensor("x1", [NTOK, D], F32, kind="ExternalInput").ap()
    else:
        dr["x1"] = nc.dram_tensor("x1", [NTOK, D], F32).ap()
    if mode in ("B", "fused"):
        dr["out"] = nc.dram_tensor("out", [NTOK, D], F32, kind="ExternalOutput").ap()
        dr["wkvbf"] = nc.dram_tensor("wkvbf", [128, KC, 2 * D], BF16).ap()
    with ExitStack() as st0:
        T = Tracker(nc, st0)
        x1_deps = [Dep() for _ in range(NTOK // 128)]
        out_deps = [Dep() for _ in range(NTOK // 128)]
        if mode in ("A", "fused"):
            with ExitStack() as st:
                phase_a(nc, T, st, dr, x1_deps)
        if mode == "fused":
            T.barrier()
        if mode in ("B", "fused"):
            with ExitStack() as st:
                phase_b(nc, T, st, dr, x1_deps, out_deps)
        final = [d.w for d in (x1_deps if mode == "A" else out_deps)]
        T.finish(final)
        T.replay()
    return nc


def host_pack(inputs):
    f = lambda a: np.ascontiguousarray(np.asarray(a, dtype=np.float32))
    pc = lambda v: f(v).reshape(KC, 128).T
    colv = np.zeros((128, NCOLV), np.float32)
    colv[:, 0:8] = pc(inputs["a_g_pre"][0])
    colv[:, 8:16] = pc(inputs["kv_g"])
    colv[:, 16:24] = pc(inputs["b_g_pre"][0])
    colv[:, 24:32] = pc(inputs["a_b_dw"][0])
    colv[:, 32:40] = pc(inputs["a_ln_g"][0])
    colv[:, 40:48] = pc(inputs["a_ln_b"][0])
    colv[:, 48:48 + KC * KW] = f(inputs["a_w_dw"][0]).reshape(KW, KC, 128).transpose(2, 1, 0).reshape(128, KC * KW)
    colv[:, 48 + KC * KW] = f(inputs["b_g_sub"][0])
    rowv = np.stack([f(inputs["a_g_post"][0]), f(inputs["b_g_post"][0]), np.tile(f(inputs["b_g_sub"][0]), NH)])
    lamv = f(inputs["b_lambda"][0]).reshape(256)
    cst = np.zeros((128, NCST), np.float32)
    cst[:, 0:128] = np.eye(128, dtype=np.float32)
    p = np.arange(128, dtype=np.float32)
    for h in range(NH):
        m = 2.0 ** (-(h + 1))
        for dl in range(-15, 2):
            cst[:, 128 + h * 17 + dl + 15] = m * (p + 128.0 * dl)
    shared = {"a_w_in": f(inputs["a_w_in"][0]), "a_w_out": f(inputs["a_w_out"][0]), "w_kv": f(inputs["w_kv"]),
              "b_w_in": f(inputs["b_w_in"][0]), "b_w_out": f(inputs["b_w_out"][0]),
              "colv": colv, "rowv": f(rowv), "lamv": lamv, "cst": cst}
    return shared


MODE = "fused"


def kernel(**inputs):
    x = np.asarray(inputs["x"], dtype=np.float32)
    shared = host_pack(inputs)
    xs = [np.ascontiguousarray(x[2 * c:2 * c + 2].reshape(NTOK, D)) for c in range(8)]
    cores = list(range(8))
    if MODE == "unfused":
        ncA = build_program("A")
        resA = run_bass_kernel_spmd(ncA, [dict(shared, x=xs[c]) for c in cores], core_ids=cores)
        ncB = build_program("B")
        resB = run_bass_kernel_spmd(ncB, [dict(shared, x1=resA.results[c]["x1"]) for c in cores], core_ids=cores)
        res = resB
    else:
        nc = build_program("fused")
        res = run_bass_kernel_spmd(nc, [dict(shared, x=xs[c]) for c in cores], core_ids=cores)
    out = np.concatenate([r["out"].reshape(NSEQ, S, D) for r in res.results], axis=0)
    return out.astype(np.float32)
```
